# Optimizing a Trainium2 kernel written in Bass

```python
import jax, jax.numpy as jnp
from jax import lax
import numpy as np

D_MODEL = 1024
BATCH = 8
SEQ = 2048
DEPTH = 4
DEC_BATCH = 128
DEC_SEQ = 8
PAST_LEN = 16384
PAGE_SIZE = 128

N_MIXERS = 2
D_FF = 2816
EXPAND = 2
D_INNER = EXPAND * D_MODEL
SSD_HEAD_DIM = 64
SSD_HEADS = D_INNER // SSD_HEAD_DIM
SSD_GROUPS = 4
HEADS_PER_GROUP = SSD_HEADS // SSD_GROUPS
D_STATE = 128
CONV_W = 4
CONV_DIM = D_INNER + 2 * SSD_GROUPS * D_STATE
IN_DIM = D_INNER + CONV_DIM + SSD_HEADS
SSD_CHUNK = 128
POOL_WINDOWS = (2, 4, 8, 16)
POOL_GROUPS = len(POOL_WINDOWS)
POOL_GW = D_MODEL // POOL_GROUPS
POOL_BUF = max(POOL_WINDOWS) - 1
N_MEM = 256
MEM_HEADS = 4
MEM_HEAD_DIM = D_MODEL // MEM_HEADS
N_SSD_LAYERS = (DEPTH + 1) // 2
N_POOL_LAYERS = DEPTH // 2
EPS = 1e-5

kernel_name = "hybrid_ssd_pool_macaron_memxattn_step"


def _rms(x, g):
    xf = x.astype(jnp.float32)
    r = lax.rsqrt(jnp.mean(xf * xf, axis=-1, keepdims=True) + EPS)
    return (xf * r * g.astype(jnp.float32)).astype(x.dtype)


def _swiglu(u, wg, wu, wd):
    return (jax.nn.silu(u @ wg) * (u @ wu)) @ wd


def _ssd_scan(x, dt, a, bm, cm, h0):
    Bsz, L = x.shape[0], x.shape[1]
    q = SSD_CHUNK if L % SSD_CHUNK == 0 else L
    nc = L // q
    x = x.reshape(Bsz, nc, q, SSD_GROUPS, HEADS_PER_GROUP, SSD_HEAD_DIM)
    dt = dt.reshape(Bsz, nc, q, SSD_GROUPS, HEADS_PER_GROUP)
    bm = bm.reshape(Bsz, nc, q, SSD_GROUPS, D_STATE)
    cm = cm.reshape(Bsz, nc, q, SSD_GROUPS, D_STATE)
    acs = jnp.cumsum(dt * a.reshape(SSD_GROUPS, HEADS_PER_GROUP), axis=2)
    xdt = x * dt[..., None]
    causal = jnp.tril(jnp.ones((q, q), dtype=bool))[:, :, None, None]
    seg = acs[:, :, :, None] - acs[:, :, None, :]
    decay = jnp.exp(jnp.where(causal, seg, -jnp.inf))
    cb = jnp.einsum('bcqgn,bckgn->bcqkg', cm, bm)
    y_diag = jnp.einsum('bcqkgh,bckghp->bcqghp', cb[..., None] * decay, xdt)
    decay_to_end = jnp.exp(acs[:, :, -1:] - acs)
    states = jnp.einsum('bcqgn,bcqgh,bcqghp->bcghpn', bm, decay_to_end, xdt)
    chunk_decay = jnp.exp(acs[:, :, -1])

    def step(h, inp):
        st, dec = inp
        return h * dec[..., None, None] + st, h

    h_init = h0.reshape(Bsz, SSD_GROUPS, HEADS_PER_GROUP, SSD_HEAD_DIM, D_STATE)
    h_final, h_prev = lax.scan(step, h_init, (jnp.moveaxis(states, 1, 0), jnp.moveaxis(chunk_decay, 1, 0)))
    h_prev = jnp.moveaxis(h_prev, 0, 1)
    y_off = jnp.einsum('bcqgn,bcghpn,bcqgh->bcqghp', cm, h_prev, jnp.exp(acs))
    y = (y_diag + y_off).reshape(Bsz, L, SSD_HEADS, SSD_HEAD_DIM)
    return y, h_final.reshape(Bsz, SSD_HEADS, SSD_HEAD_DIM, D_STATE)


def _ssd_mixer(u, conv_buf, ssm_state, in_w, conv_w, conv_b, dt_bias, a_log, d_skip, norm_w, out_w):
    Bsz, L, _ = u.shape
    zxbcdt = u @ in_w
    z = zxbcdt[..., :D_INNER]
    xbc = zxbcdt[..., D_INNER:D_INNER + CONV_DIM]
    dt_raw = zxbcdt[..., D_INNER + CONV_DIM:]
    xx = jnp.concatenate([conv_buf.astype(xbc.dtype), xbc], axis=1)
    conv = conv_b + sum(xx[:, k:k + L] * conv_w[k] for k in range(CONV_W))
    xbc = jax.nn.silu(conv)
    new_conv = xx[:, L:].astype(conv_buf.dtype)
    gn = SSD_GROUPS * D_STATE
    xs = xbc[..., :D_INNER].reshape(Bsz, L, SSD_HEADS, SSD_HEAD_DIM).astype(jnp.float32)
    bm = xbc[..., D_INNER:D_INNER + gn].reshape(Bsz, L, SSD_GROUPS, D_STATE).astype(jnp.float32)
    cm = xbc[..., D_INNER + gn:].reshape(Bsz, L, SSD_GROUPS, D_STATE).astype(jnp.float32)
    dt = jax.nn.softplus(dt_raw.astype(jnp.float32) + dt_bias.astype(jnp.float32))
    a = -jnp.exp(a_log.astype(jnp.float32))
    y, new_ssm = _ssd_scan(xs, dt, a, bm, cm, ssm_state.astype(jnp.float32))
    y = y + xs * d_skip.astype(jnp.float32)[:, None]
    y = y.reshape(Bsz, L, D_INNER) * jax.nn.silu(z.astype(jnp.float32))
    yg = y.reshape(Bsz, L, SSD_GROUPS, D_INNER // SSD_GROUPS)
    yg = yg * lax.rsqrt(jnp.mean(yg * yg, axis=-1, keepdims=True) + EPS)
    y = (yg.reshape(Bsz, L, D_INNER) * norm_w.astype(jnp.float32)).astype(u.dtype)
    return y @ out_w, new_conv, new_ssm.astype(ssm_state.dtype)


def _pool_mixer(u, buf, pos0, pool_w, pool_scale):
    Bsz, L, _ = u.shape
    xx = jnp.concatenate([buf.astype(jnp.float32), u.astype(jnp.float32)], axis=1)
    cs0 = jnp.concatenate([jnp.zeros((Bsz, 1, D_MODEL), jnp.float32), jnp.cumsum(xx, axis=1)], axis=1)
    end = cs0[:, POOL_BUF + 1:]
    pos = (pos0 + jnp.arange(L)).astype(jnp.float32)
    uf = u.astype(jnp.float32)
    outs = []
    for g, w in enumerate(POOL_WINDOWS):
        lo, hi = g * POOL_GW, (g + 1) * POOL_GW
        start = cs0[:, POOL_BUF + 1 - w:POOL_BUF + 1 - w + L, lo:hi]
        cnt = jnp.minimum(pos + 1.0, float(w))[None, :, None]
        mix = (end[..., lo:hi] - start) / cnt - uf[..., lo:hi]
        outs.append(jnp.einsum('bld,de->ble', mix, pool_w[g].astype(jnp.float32)))
    out = (jnp.concatenate(outs, axis=-1) * pool_scale.astype(jnp.float32)).astype(u.dtype)
    return out, xx[:, L:].astype(buf.dtype)


def _mem_kv(mem, g, wk, wv):
    Bsz = mem.shape[0]
    m = _rms(mem, g)
    k = (m @ wk).reshape(Bsz, N_MEM, MEM_HEADS, MEM_HEAD_DIM)
    v = (m @ wv).reshape(Bsz, N_MEM, MEM_HEADS, MEM_HEAD_DIM)
    return k, v


def _cross_attn(u, mk, mv, wq, wo):
    Bsz, L, _ = u.shape
    q = (u @ wq).reshape(Bsz, L, MEM_HEADS, MEM_HEAD_DIM)
    s = jnp.einsum('blhd,bmhd->bhlm', q.astype(jnp.float32), mk.astype(jnp.float32)) * (MEM_HEAD_DIM ** -0.5)
    p = jax.nn.softmax(s, axis=-1).astype(mv.dtype)
    o = jnp.einsum('bhlm,bmhd->blhd', p, mv).reshape(Bsz, L, D_MODEL)
    return o @ wo


def setup_inputs(seed: int = 0) -> dict:
    key = jax.random.key(seed)
    ks = iter(jax.random.split(key, 64))
    f32 = jnp.float32

    def nrm(shape, scale):
        return jax.random.normal(next(ks), shape, f32) * scale

    def gain(shape):
        return 1.0 + 0.05 * jax.random.normal(next(ks), shape, f32)

    dt0 = jnp.exp(jax.random.uniform(next(ks), (N_SSD_LAYERS, SSD_HEADS), f32)
                  * (np.log(0.1) - np.log(0.001)).astype(np.float32) + np.float32(np.log(0.001)))
    dt_bias = dt0 + jnp.log(-jnp.expm1(-dt0))
    a_log = jnp.log(jax.random.uniform(next(ks), (N_SSD_LAYERS, SSD_HEADS), f32, minval=1.0, maxval=16.0))
    return {
        "x_prompt": nrm((BATCH, SEQ, D_MODEL), 1.0),
        "x_sample": nrm((DEC_BATCH, DEC_SEQ, D_MODEL), 1.0),
        "mem_prompt": nrm((BATCH, N_MEM, D_MODEL), 1.0),
        "cache_mem_k": nrm((DEPTH, DEC_BATCH, N_MEM, MEM_HEADS, MEM_HEAD_DIM), 1.0),
        "cache_mem_v": nrm((DEPTH, DEC_BATCH, N_MEM, MEM_HEADS, MEM_HEAD_DIM), 1.0),
        "state_ssm": nrm((N_SSD_LAYERS, DEC_BATCH, SSD_HEADS, SSD_HEAD_DIM, D_STATE), 0.3),
        "state_conv": nrm((N_SSD_LAYERS, DEC_BATCH, CONV_W - 1, CONV_DIM), 1.0),
        "state_pool": nrm((N_POOL_LAYERS, DEC_BATCH, POOL_BUF, D_MODEL), 1.0),
        "norm_ffn1": gain((DEPTH, D_MODEL)),
        "ffn1_w_gate": nrm((DEPTH, D_MODEL, D_FF), D_MODEL ** -0.5),
        "ffn1_w_up": nrm((DEPTH, D_MODEL, D_FF), D_MODEL ** -0.5),
        "ffn1_w_down": nrm((DEPTH, D_FF, D_MODEL), D_FF ** -0.5),
        "norm_mix": gain((DEPTH, D_MODEL)),
        "ssd_in_w": nrm((N_SSD_LAYERS, D_MODEL, IN_DIM), D_MODEL ** -0.5),
        "ssd_conv_w": nrm((N_SSD_LAYERS, CONV_W, CONV_DIM), CONV_W ** -0.5),
        "ssd_conv_b": nrm((N_SSD_LAYERS, CONV_DIM), 0.02),
        "ssd_dt_bias": dt_bias,
        "ssd_a_log": a_log,
        "ssd_d": gain((N_SSD_LAYERS, SSD_HEADS)),
        "ssd_norm_w": gain((N_SSD_LAYERS, D_INNER)),
        "ssd_out_w": nrm((N_SSD_LAYERS, D_INNER, D_MODEL), D_INNER ** -0.5),
        "pool_w": nrm((N_POOL_LAYERS, POOL_GROUPS, POOL_GW, POOL_GW), POOL_GW ** -0.5),
        "pool_scale": gain((N_POOL_LAYERS, D_MODEL)),
        "norm_cross": gain((DEPTH, D_MODEL)),
        "norm_mem": gain((DEPTH, D_MODEL)),
        "xa_wq": nrm((DEPTH, D_MODEL, D_MODEL), D_MODEL ** -0.5),
        "xa_wk": nrm((DEPTH, D_MODEL, D_MODEL), D_MODEL ** -0.5),
        "xa_wv": nrm((DEPTH, D_MODEL, D_MODEL), D_MODEL ** -0.5),
        "xa_wo": nrm((DEPTH, D_MODEL, D_MODEL), D_MODEL ** -0.5),
        "norm_ffn2": gain((DEPTH, D_MODEL)),
        "ffn2_w_gate": nrm((DEPTH, D_MODEL, D_FF), D_MODEL ** -0.5),
        "ffn2_w_up": nrm((DEPTH, D_MODEL, D_FF), D_MODEL ** -0.5),
        "ffn2_w_down": nrm((DEPTH, D_FF, D_MODEL), D_FF ** -0.5),
        "final_norm": gain((D_MODEL,)),
    }


def reference(x_prompt, x_sample, mem_prompt, cache_mem_k, cache_mem_v, state_ssm, state_conv, state_pool,
              norm_ffn1, ffn1_w_gate, ffn1_w_up, ffn1_w_down, norm_mix,
              ssd_in_w, ssd_conv_w, ssd_conv_b, ssd_dt_bias, ssd_a_log, ssd_d, ssd_norm_w, ssd_out_w,
              pool_w, pool_scale, norm_cross, norm_mem, xa_wq, xa_wk, xa_wv, xa_wo,
              norm_ffn2, ffn2_w_gate, ffn2_w_up, ffn2_w_down, final_norm):

    def run_group(x, pos0, conv_in, ssm_in, pool_in, mem_k, mem_v):
        new_conv, new_ssm, new_pool = [], [], []
        for i in range(DEPTH):
            j = i // N_MIXERS
            x = x + 0.5 * _swiglu(_rms(x, norm_ffn1[i]), ffn1_w_gate[i], ffn1_w_up[i], ffn1_w_down[i])
            u = _rms(x, norm_mix[i])
            if i % N_MIXERS == 0:
                out, cs, ss = _ssd_mixer(u, conv_in[j], ssm_in[j], ssd_in_w[j], ssd_conv_w[j], ssd_conv_b[j],
                                         ssd_dt_bias[j], ssd_a_log[j], ssd_d[j], ssd_norm_w[j], ssd_out_w[j])
                new_conv.append(cs)
                new_ssm.append(ss)
            else:
                out, ps = _pool_mixer(u, pool_in[j], pos0, pool_w[j], pool_scale[j])
                new_pool.append(ps)
            x = x + out
            x = x + _cross_attn(_rms(x, norm_cross[i]), mem_k[i], mem_v[i], xa_wq[i], xa_wo[i])
            x = x + 0.5 * _swiglu(_rms(x, norm_ffn2[i]), ffn2_w_gate[i], ffn2_w_up[i], ffn2_w_down[i])
        return _rms(x, final_norm), jnp.stack(new_ssm), jnp.stack(new_conv), jnp.stack(new_pool)

    dt_p = x_prompt.dtype
    conv0 = [jnp.zeros((BATCH, CONV_W - 1, CONV_DIM), dt_p) for _ in range(N_SSD_LAYERS)]
    ssm0 = [jnp.zeros((BATCH, SSD_HEADS, SSD_HEAD_DIM, D_STATE), dt_p) for _ in range(N_SSD_LAYERS)]
    pool0 = [jnp.zeros((BATCH, POOL_BUF, D_MODEL), dt_p) for _ in range(N_POOL_LAYERS)]
    mk_p, mv_p = [], []
    for i in range(DEPTH):
        k_i, v_i = _mem_kv(mem_prompt, norm_mem[i], xa_wk[i], xa_wv[i])
        mk_p.append(k_i)
        mv_p.append(v_i)
    y_prompt, ssm_p, conv_p, pool_p = run_group(x_prompt, 0, conv0, ssm0, pool0, mk_p, mv_p)
    new_mem_k_prompt = jnp.stack(mk_p)
    new_mem_v_prompt = jnp.stack(mv_p)

    y_sample, ssm_s, conv_s, pool_s = run_group(x_sample, PAST_LEN, state_conv, state_ssm, state_pool,
                                                cache_mem_k, cache_mem_v)
    return (y_prompt, y_sample, ssm_p, conv_p, pool_p, new_mem_k_prompt, new_mem_v_prompt, ssm_s, conv_s, pool_s)
```

```python
import numpy as np
from contextlib import ExitStack
import concourse.bass as bass
import concourse.mybir as mybir
from concourse.bass_utils import run_bass_kernel_spmd

F32 = mybir.dt.float32
BF16 = mybir.dt.bfloat16
AF = mybir.ActivationFunctionType
ALU = mybir.AluOpType

EPOCH = 30000
NSLOT = 8
EPS = 1e-5

D = 1024
DFF = 2816
T = 2176
TILES = [(0, 512), (512, 512), (1024, 512), (1536, 512), (2048, 128)]
PASSES = [[0, 1], [2, 3, 4]]
PASS_COL0 = [0, 1024]
NWB = 4
WBE = 2816


class Sched:
    def __init__(self):
        self.engs = ['pe', 'act', 'dve', 'pool', 'sp']
        self.stream = {e: [] for e in self.engs}
        self.cnt = {e: 0 for e in self.engs}
        self.res = {}
        self.waited = {e: {} for e in self.engs}
        self.dmacnt = {e: 0 for e in self.engs}
        self.semkeys = {}
        self.fence = {}

    def _semkey(self, k):
        self.semkeys.setdefault(k, None)
        return k

    def _need(self, eng, tok, waits):
        if tok[0] == 'e':
            _, e, idx = tok
            if e == eng and eng == 'pe':
                return
            key = ('e', e)
            if self.waited[eng].get(key, -1) >= idx:
                return
            self.waited[eng][key] = idx
            waits[key] = (self._semkey(('e', e, idx // EPOCH)), idx % EPOCH + 1)
        else:
            _, q, k = tok
            key = ('d', q, k % NSLOT)
            if self.waited[eng].get(key, -1) >= k:
                return
            self.waited[eng][key] = k
            waits[key] = (self._semkey(('d', q, k % NSLOT)), 16 * (k // NSLOT + 1))

    def _deps(self, eng, r, w):
        waits = {}
        for key in list(r) + list(w):
            if isinstance(key, tuple) and key[0] in self.fence and key not in self.res:
                for t in self.fence[key[0]].values():
                    self._need(eng, t, waits)
        for key in r:
            st = self.res.get(key)
            if st and st['w'] is not None:
                self._need(eng, st['w'], waits)
        for key in w:
            st = self.res.get(key)
            if st:
                if st['w'] is not None:
                    self._need(eng, st['w'], waits)
                for t in st['r'].values():
                    self._need(eng, t, waits)
        return waits

    def _mark(self, tok, r, w):
        for key in r:
            st = self.res.setdefault(key, {'w': None, 'r': {}})
            if tok[0] == 'e':
                st['r'][('e', tok[1])] = tok
            else:
                st['r'][tok] = tok
        for key in w:
            self.res[key] = {'w': tok, 'r': {}}

    def op(self, eng, fn, r=(), w=()):
        waits = self._deps(eng, r, w)
        for (sk, val) in waits.values():
            self.stream[eng].append(('wait', sk, val))
        idx = self.cnt[eng]
        self.cnt[eng] += 1
        self.stream[eng].append(('op', fn, self._semkey(('e', eng, idx // EPOCH))))
        self._mark(('e', eng, idx), r, w)

    def dma(self, q, fn, r=(), w=()):
        waits = self._deps(q, r, w)
        k = self.dmacnt[q]
        self.dmacnt[q] += 1
        if k >= NSLOT:
            self._need(q, ('d', q, k - NSLOT), waits)
        for (sk, val) in waits.values():
            self.stream[q].append(('wait', sk, val))
        self.stream[q].append(('dma', fn, self._semkey(('d', q, k % NSLOT))))
        self._mark(('d', q, k), r, w)

    def retire(self, region):
        toks = dict(self.fence.get(region, {}))
        for key in list(self.res):
            if isinstance(key, tuple) and key[0] == region:
                st = self.res.pop(key)
                for t in ([st['w']] if st['w'] is not None else []) + list(st['r'].values()):
                    if t[0] == 'e':
                        k = ('e', t[1])
                        if k not in toks or toks[k][2] < t[2]:
                            toks[k] = t
                    else:
                        toks[t] = t
        self.fence[region] = toks

    def finish(self):
        for q in self.engs:
            n = self.dmacnt[q]
            for k in range(max(0, n - NSLOT), n):
                waits = {}
                self._need(q, ('d', q, k), waits)
                for (sk, val) in waits.values():
                    self.stream[q].append(('wait', sk, val))

    def emit(self, nc):
        with ExitStack() as es:
            sems = {}
            for i, k in enumerate(self.semkeys):
                sems[k] = es.enter_context(nc.semaphore("s%d" % i))
            block = es.enter_context(nc.Block())

            def run(e, eng):
                for it in self.stream[e]:
                    if it[0] == 'wait':
                        eng.wait_ge(sems[it[1]], it[2])
                    elif it[0] == 'op':
                        it[1](eng).then_inc(sems[it[2]], 1)
                    else:
                        it[1](eng).then_inc(sems[it[2]], 16)

            @block.tensor
            def _(eng):
                run('pe', eng)

            @block.scalar
            def _(eng):
                run('act', eng)

            @block.vector
            def _(eng):
                run('dve', eng)

            @block.gpsimd
            def _(eng):
                run('pool', eng)

            @block.sync
            def _(eng):
                run('sp', eng)


def vec_layout():
    lay = {}
    c = 0
    for i in range(4):
        for nm in ('nf1', 'nmix', 'ncr', 'nmem', 'nf2'):
            lay[(nm, i)] = c
            c += 8
    lay['fin'] = c
    c += 8
    for j in range(2):
        lay[('psc', j)] = c
        c += 8
        lay[('snw', j)] = c
        c += 16
        lay[('cw', j)] = c
        c += 96
        lay[('cb', j)] = c
        c += 24
        lay[('dsk', j)] = c
        c += 16
    return lay, c


VL, NV = vec_layout()


def build(cfg=None):
    cfg = cfg or {}
    NL = cfg.get('nlayers', 4)
    SUBS = cfg.get('subs', ('ffn1', 'mix', 'attn', 'ffn2'))
    nc = bass.Bass("TRN2", target_bir_lowering=False)
    S = Sched()

    def din(name, shape):
        return nc.dram_tensor(name, shape, F32, kind="ExternalInput").ap()

    def dout(name, shape):
        return nc.dram_tensor(name, shape, F32, kind="ExternalOutput").ap()

    xp_d = din("xp", [2048, 1024])
    xs_d = din("xs", [128, 1024])
    mem_d = din("mem", [256, 1024])
    ck_d = din("ck", [4, 16, 256, 1024])
    cv_d = din("cv", [4, 16, 256, 1024])
    sst_d = din("sst", [2, 16, 2048, 128])
    scv_d = din("scv", [2, 48, 3072])
    spl_d = din("spl", [2, 16, 15, 1024])
    vecs_d = din("vecs", [128, NV])
    hv_d = din("hv", [96, 4])
    wg1_d = din("wg1", [4, 1024, 2816])
    wu1_d = din("wu1", [4, 1024, 2816])
    wd1_d = din("wd1", [4, 2816, 1024])
    wg2_d = din("wg2", [4, 1024, 2816])
    wu2_d = din("wu2", [4, 1024, 2816])
    wd2_d = din("wd2", [4, 2816, 1024])
    inw_d = din("inw", [2, 1024, 5152])
    outw_d = din("outw", [2, 2048, 1024])
    pw_d = din("pw", [2, 4, 256, 256])
    wq_d = din("wq", [4, 1024, 1024])
    wk_d = din("wk", [4, 1024, 1024])
    wv_d = din("wv", [4, 1024, 1024])
    wo_d = din("wo", [4, 1024, 1024])

    yp_o = dout("y_p", [2048, 1024])
    ys_o = dout("y_s", [128, 1024])
    ssmp_o = dout("ssm_p", [2, 2048, 128])
    convp_o = dout("conv_p", [2, 3, 3072])
    poolp_o = dout("pool_p", [2, 15, 1024])
    mkp_o = dout("mk_p", [4, 256, 1024])
    mvp_o = dout("mv_p", [4, 256, 1024])
    ssms_o = dout("ssm_s", [2, 16, 2048, 128])
    convs_o = dout("conv_s", [2, 48, 3072])
    pools_o = dout("pool_s", [2, 16, 15, 1024])

    es = ExitStack()

    def sb(name, shape, dt):
        return es.enter_context(nc.sbuf_tensor("s_" + name, shape, dt))

    xT = sb("xT", [128, 8, T], F32)
    ub = sb("ub", [128, 8, 1152], BF16)
    Sreg = sb("Sreg", [128, 25344], BF16)
    Rreg = sb("Rreg", [128, 5888], F32)
    Mreg = sb("Mreg", [128, 2560], F32)
    wbuf = [sb("wb%d" % i, [128, WBE], BF16) for i in range(NWB)]
    vecs = sb("vecs", [128, NV], F32)
    hv = sb("hv", [96, 4], F32)
    avec = sb("avec", [96, 2], F32)
    ident32 = sb("ident32", [128, 128], F32)
    identb = sb("identb", [128, 128], BF16)
    onesb = sb("onesb", [128, 128], BF16)
    ones1 = sb("ones1", [128, 1], F32)
    m01c = sb("m01c", [128, 128], BF16)
    m01bd = sb("m01bd", [128, 128], BF16)
    mngc = sb("mngc", [128, 128], BF16)
    mngbd = sb("mngbd", [128, 128], BF16)
    i3b = sb("i3b", [96, 32], BF16)
    resetm = sb("resetm", [96, 128], F32)
    bmask = sb("bmask", [128, 16], F32)
    sq2 = [sb("sq%d" % i, [128, 512], BF16) for i in range(2)]
    rsA = sb("rsA", [128, 512], F32)
    rsB = sb("rsB", [128, 512], F32)
    sctm = sb("sctm", [128, 64], F32)
    STt = sb("STt", [64, 128], F32)
    ytmp = [sb("ytmp%d" % i, [128, 128], F32) for i in range(2)]
    cdE = sb("cdE", [128, 32, 16], F32)
    pTs = sb("pTs", [128, 64], BF16)
    rds = sb("rds", [128, 32], F32)
    psum = [es.enter_context(nc.psum_tensor("ps%d" % i, [128, 512], F32)) for i in range(8)]

    st = {'ps': 0, 'wb': 0, 'sq': 0, 'ev': 0, 'resv': set()}

    def psn():
        while True:
            i = st['ps'] % 8
            st['ps'] += 1
            if i not in st['resv']:
                return i

    def carve(reg, esz, off, dt, shape):
        n = 1
        for s_ in shape[1:]:
            n *= s_
        dsz = 4 if dt == F32 else 2
        nbytes = n * dsz
        a = reg[:, off // esz:(off + nbytes) // esz]
        if dsz != esz:
            a = a.bitcast(dt)
        if len(shape) == 3:
            a = a.rearrange("p (a b) -> p a b", a=shape[1])
        elif len(shape) == 4:
            a = a.rearrange("p (a b c) -> p a b c", a=shape[1], b=shape[2])
        return a

    def SV(off, dt, shape):
        return carve(Sreg, 2, off, dt, shape)

    def RV(off, dt, shape):
        return carve(Rreg, 4, off, dt, shape)

    def MV(off, dt, shape):
        return carve(Mreg, 4, off, dt, shape)

    def mm(out, lhsT, rhs, start, stop, r, w):
        S.op('pe', lambda e: e.matmul(out, lhsT=lhsT, rhs=rhs, start=start, stop=stop), r=r, w=w)

    def trp(out, in_, ident, r, w):
        S.op('pe', lambda e: e.transpose(out, in_, ident), r=r, w=w)

    def act(out, in_, func, r, w, bias=None, scale=None):
        kw = {}
        if bias is not None:
            kw['bias'] = bias
        if scale is not None:
            kw['scale'] = scale
        S.op('act', lambda e: e.activation(out=out, in_=in_, func=func, **kw), r=r, w=w)

    def cp(eng, out, in_, r, w):
        if eng == 'act':
            S.op('act', lambda e: e.activation(out=out, in_=in_, func=AF.Copy), r=r, w=w)
        else:
            S.op(eng, lambda e: e.tensor_copy(out=out, in_=in_), r=r, w=w)

    def evq():
        st['ev'] += 1
        return 'act' if st['ev'] % 2 else 'dve'

    def tt(out, in0, in1, op, r, w, eng='dve'):
        S.op(eng, lambda e: e.tensor_tensor(out=out, in0=in0, in1=in1, op=op), r=r, w=w)

    def ts(out, in0, s1, s2, op0, op1, r, w):
        if op1 is None:
            S.op('dve', lambda e: e.tensor_scalar(out=out, in0=in0, scalar1=s1, scalar2=None, op0=op0), r=r, w=w)
        else:
            S.op('dve', lambda e: e.tensor_scalar(out=out, in0=in0, scalar1=s1, scalar2=s2, op0=op0, op1=op1), r=r, w=w)

    def stt(out, in0, scalar, in1, op0, op1, r, w):
        S.op('dve', lambda e: e.scalar_tensor_tensor(out=out, in0=in0, scalar=scalar, in1=in1, op0=op0, op1=op1), r=r, w=w)

    def recip(out, in_, r, w):
        S.op('dve', lambda e: e.reciprocal(out=out, in_=in_), r=r, w=w)

    def memset(eng, ap, val, w, r=()):
        S.op(eng, lambda e: e.memset(ap, val), r=r, w=w)

    def dma(q, out, in_, r, w, nonc=False):
        if nonc:
            S.dma(q, lambda e: e.dma_start(out=out, in_=in_, allow_slow_non_contiguous=True), r=r, w=w)
        else:
            S.dma(q, lambda e: e.dma_start(out=out, in_=in_), r=r, w=w)

    def load_w(W2, K, c0, ncol):
        KC = K // 128
        assert KC * ncol <= WBE
        bi = st['wb'] % NWB
        st['wb'] += 1
        view = wbuf[bi][:, 0:KC * ncol].rearrange("p (k n) -> p k n", k=KC)
        src = W2.rearrange("(k p) n -> p k n", p=128)[:, :, c0:c0 + ncol]
        S.dma('pool', lambda e: e.dma_start(out=view, in_=src), w=[('wb', bi)])
        return view, ('wb', bi)

    C = 'consts'

    dma('sp', vecs[:], vecs_d, r=[], w=[C])
    dma('sp', hv[:], hv_d, r=[], w=[C])
    memset('pool', ones1[:], 1.0, w=[C])
    memset('pool', rsA[:], 1.0, w=['rsA'])
    memset('pool', rsB[:], 0.0, w=['rsB'])
    memset('pool', onesb[:], 1.0, w=[C])
    S.op('pool', lambda e: e.affine_select(out=ident32[:], in_=rsA[:, 0:128], pattern=[[-1, 128]], compare_op=ALU.is_equal,
                                           fill=0.0, base=0, channel_multiplier=1), r=['rsA'], w=[C])
    cp('pool', identb[:], ident32[:], r=[C], w=[C])
    S.op('pool', lambda e: e.affine_select(out=m01c[:], in_=rsA[:, 0:128], pattern=[[1, 128]], compare_op=ALU.is_ge,
                                           fill=0.0, base=0, channel_multiplier=-1), r=['rsA'], w=[C])
    S.op('pool', lambda e: e.affine_select(out=mngc[:], in_=rsB[:, 0:128], pattern=[[1, 128]], compare_op=ALU.is_ge,
                                           fill=-30000.0, base=0, channel_multiplier=-1), r=['rsB'], w=[C])
    S.op('pool', lambda e: e.affine_select(out=m01bd[:].rearrange("p (a b) -> p a b", a=16), in_=m01c[:].rearrange("p (a b) -> p a b", a=16),
                                           pattern=[[-8, 16], [0, 8]], compare_op=ALU.is_ge,
                                           fill=0.0, base=0, channel_multiplier=1), r=[C], w=[C])
    S.op('pool', lambda e: e.affine_select(out=mngbd[:].rearrange("p (a b) -> p a b", a=16), in_=mngc[:].rearrange("p (a b) -> p a b", a=16),
                                           pattern=[[-8, 16], [0, 8]], compare_op=ALU.is_ge,
                                           fill=-30000.0, base=0, channel_multiplier=1), r=[C], w=[C])
    memset('pool', sctm[:], 0.0, w=['sctm'])
    for j3 in range(3):
        S.op('pool', lambda e, j3=j3: e.affine_select(out=sctm[32 * j3:32 * j3 + 32, 0:32], in_=rsA[32 * j3:32 * j3 + 32, 0:32],
                                                      pattern=[[-1, 32]], compare_op=ALU.is_equal, fill=0.0, base=0,
                                                      channel_multiplier=1), r=['rsA', 'sctm'], w=['sctm'])
    cp('pool', i3b[:], sctm[0:96, 0:32], r=['sctm'], w=[C])
    memset('pool', resetm[:], 1.0, w=[C])
    memset('pool', resetm[:].rearrange("p (a b) -> p a b", a=16)[:, :, 0:1], 0.0, w=[C], r=[C])
    S.op('pool', lambda e: e.affine_select(out=bmask[:], in_=rsA[:, 0:16], pattern=[[-8, 16]], compare_op=ALU.is_ge,
                                           fill=0.0, base=0, channel_multiplier=1), r=['rsA'], w=['bm0'])
    S.op('pool', lambda e: e.affine_select(out=bmask[:], in_=bmask[:], pattern=[[8, 16]], compare_op=ALU.is_ge,
                                           fill=0.0, base=7, channel_multiplier=-1), r=['bm0'], w=[C])
    for j in range(2):
        act(avec[:, j:j + 1], hv[:, 2 * j + 1:2 * j + 2], AF.Exp, r=[C], w=[('avec', j)])
        ts(avec[:, j:j + 1], avec[:, j:j + 1], -1.0, None, ALU.mult, None, r=[('avec', j)], w=[('avec', j)])

    for blk in range(17):
        stg = SV((blk % 2) * 4096, F32, [128, 1024])
        src = xp_d[blk * 128:(blk + 1) * 128, :] if blk < 16 else xs_d
        dma('sp', stg, src, r=[], w=[('S', 'stg', blk % 2)])
        t = min(blk // 4, 4)
        for half in range(2):
            pi = psn()
            for kk in range(4):
                trp(psum[pi][:, kk * 128:(kk + 1) * 128], stg[:, (half * 4 + kk) * 128:(half * 4 + kk + 1) * 128], ident32[:],
                    r=[('S', 'stg', blk % 2), C], w=[('ps', pi)])
            cp(evq(), xT[:, half * 4:half * 4 + 4, blk * 128:(blk + 1) * 128], psum[pi][:].rearrange("p (k c) -> p k c", k=4),
               r=[('ps', pi)], w=[('x', t, k) for k in range(half * 4, half * 4 + 4)])
    S.retire('S')

    def rms_stat(srcs, n, rkeys, nfeat):
        pi = psn()
        for k, (sap, rk) in enumerate(zip(srcs, rkeys)):
            q = st['sq'] % 2
            st['sq'] += 1
            act(sq2[q][:, :n], sap, AF.Square, r=(rk if isinstance(rk, list) else [rk]), w=[('sq', q)])
            mm(psum[pi][:, :n], onesb[:], sq2[q][:, :n], k == 0, k == len(srcs) - 1, r=[('sq', q), C], w=[('ps', pi)])
        act(rsA[:, :n], psum[pi][:, :n], AF.Sqrt, r=[('ps', pi)], w=['rsA'], bias=EPS, scale=1.0 / nfeat)
        recip(rsB[:, :n], rsA[:, :n], r=['rsA'], w=['rsB'])

    def rms_tile(t, gcol, dst_fn, wkeys_fn):
        c0, n = TILES[t]
        rms_stat([xT[:, k, c0:c0 + n] for k in range(8)], n, [('x', t, k) for k in range(8)], 1024.0)
        for k in range(8):
            stt(dst_fn(k), xT[:, k, c0:c0 + n], vecs[:, gcol + k:gcol + k + 1], rsB[:, :n], ALU.mult, ALU.mult,
                r=[('x', t, k), 'rsB', C], w=wkeys_fn(k))

    def norm_u(p, gcol):
        for t in PASSES[p]:
            c0, n = TILES[t]
            uc = c0 - PASS_COL0[p]
            rms_tile(t, gcol, lambda k, uc=uc, n=n: ub[:, k, uc:uc + n], lambda k, t=t: [('u', t, k)])

    def ffn(p, i, wg_d, wu_d, wd_d, gname):
        norm_u(p, VL[(gname, i)])
        tl = PASSES[p]
        h = SV(0, BF16, [128, 22, 1152])
        for un in range(11):
            c0 = un * 256
            wgb, gk = load_w(wg_d[i], 1024, c0, 256)
            wub, uk = load_w(wu_d[i], 1024, c0, 256)
            for cc in range(2):
                c = un * 2 + cc
                for t in tl:
                    col0, n = TILES[t]
                    uc = col0 - PASS_COL0[p]
                    pg = psn()
                    for k in range(8):
                        mm(psum[pg][:, :n], wgb[:, k, cc * 128:(cc + 1) * 128], ub[:, k, uc:uc + n], k == 0, k == 7,
                           r=[gk, ('u', t, k)], w=[('ps', pg)])
                    pu = psn()
                    for k in range(8):
                        mm(psum[pu][:, :n], wub[:, k, cc * 128:(cc + 1) * 128], ub[:, k, uc:uc + n], k == 0, k == 7,
                           r=[uk, ('u', t, k)], w=[('ps', pu)])
                    q = st['sq'] % 2
                    st['sq'] += 1
                    act(sq2[q][:, :n], psum[pg][:, :n], AF.Silu, r=[('ps', pg)], w=[('sq', q)])
                    tt(h[:, c, uc:uc + n], psum[pu][:, :n], sq2[q][:, :n], ALU.mult, r=[('ps', pu), ('sq', q)], w=[('S', 'h', c, t)])
        for o in range(8):
            wdb, dk = load_w(wd_d[i], 2816, o * 128, 128)
            for t in tl:
                col0, n = TILES[t]
                uc = col0 - PASS_COL0[p]
                po = psn()
                for c in range(22):
                    mm(psum[po][:, :n], wdb[:, c, :], h[:, c, uc:uc + n], c == 0, c == 21, r=[dk, ('S', 'h', c, t)], w=[('ps', po)])
                stt(xT[:, o, col0:col0 + n], psum[po][:, :n], 0.5, xT[:, o, col0:col0 + n], ALU.mult, ALU.add,
                    r=[('ps', po), ('x', t, o)], w=[('x', t, o)])

    KT = MV(0, BF16, [128, 8, 256])
    Vb = MV(4096, BF16, [128, 2, 1024])

    def memkv(i):
        mem_tm = SV(0, F32, [128, 2, 1024])
        memT = SV(8192, F32, [128, 8, 256])
        mT = SV(16384, BF16, [128, 8, 256])
        ostg = [SV(20480 + 2048 * a, F32, [128, 512]) for a in range(2)]
        dma('sp', mem_tm, mem_d.rearrange("(c p) f -> p c f", p=128), r=[], w=[('S', 'memtm')])
        for mc in range(2):
            for half in range(2):
                pi = psn()
                for kk in range(4):
                    k = half * 4 + kk
                    trp(psum[pi][:, kk * 128:(kk + 1) * 128], mem_tm[:, mc, k * 128:(k + 1) * 128], ident32[:],
                        r=[('S', 'memtm'), C], w=[('ps', pi)])
                cp(evq(), memT[:, half * 4:half * 4 + 4, mc * 128:(mc + 1) * 128], psum[pi][:].rearrange("p (k c) -> p k c", k=4),
                   r=[('ps', pi)], w=[('S', 'memT', mc, half)])
        allk = [('S', 'memT', mc, half) for mc in range(2) for half in range(2)]
        rms_stat([memT[:, k, :] for k in range(8)], 256, [[('S', 'memT', 0, k // 4), ('S', 'memT', 1, k // 4)] for k in range(8)], 1024.0)
        gcol = VL[('nmem', i)]
        for k in range(8):
            stt(mT[:, k, :], memT[:, k, :], vecs[:, gcol + k:gcol + k + 1], rsB[:, :256], ALU.mult, ALU.mult,
                r=allk + ['rsB', C], w=[('S', 'mT', k)])
        mk = [('S', 'mT', k) for k in range(8)]
        oc = 0
        for un in range(4):
            wkb, kk_ = load_w(wk_d[i], 1024, un * 256, 256)
            for cc in range(2):
                o = un * 2 + cc
                pi = psn()
                for k in range(8):
                    mm(psum[pi][:, :256], wkb[:, k, cc * 128:(cc + 1) * 128], mT[:, k, :], k == 0, k == 7, r=[kk_, mk[k]], w=[('ps', pi)])
                cp(evq(), KT[:, o, :], psum[pi][:, :256], r=[('ps', pi)], w=[('M', 'KT', o)])
            for mc in range(2):
                pi = psn()
                for k in range(8):
                    mm(psum[pi][:, :256], mT[:, k, mc * 128:(mc + 1) * 128], wkb[:, k, :], k == 0, k == 7, r=[kk_, mk[k]], w=[('ps', pi)])
                a = oc % 2
                oc += 1
                cp(evq(), ostg[a][:, :256], psum[pi][:, :256], r=[('ps', pi)], w=[('S', 'ostg', a)])
                dma('act', mkp_o[i, mc * 128:(mc + 1) * 128, un * 256:(un + 1) * 256], ostg[a][:, :256], r=[('S', 'ostg', a)], w=[])
        for un in range(4):
            wvb, vk_ = load_w(wv_d[i], 1024, un * 256, 256)
            for mc in range(2):
                pi = psn()
                for k in range(8):
                    mm(psum[pi][:, :256], mT[:, k, mc * 128:(mc + 1) * 128], wvb[:, k, :], k == 0, k == 7, r=[vk_, mk[k]], w=[('ps', pi)])
                a = oc % 2
                oc += 1
                cp('act', ostg[a][:, :256], psum[pi][:, :256], r=[('ps', pi)], w=[('S', 'ostg', a)])
                cp('dve', Vb[:, mc, un * 256:(un + 1) * 256], psum[pi][:, :256], r=[('ps', pi)], w=[('M', 'Vb', mc, un)])
                dma('act', mvp_o[i, mc * 128:(mc + 1) * 128, un * 256:(un + 1) * 256], ostg[a][:, :256], r=[('S', 'ostg', a)], w=[])
        S.retire('S')

    def attn(p, i):
        norm_u(p, VL[('ncr', i)])
        tl = PASSES[p]
        qT = SV(0, BF16, [128, 8, 1152])
        oT = SV(18432, BF16, [128, 8, 1152])
        pT = SV(36864, BF16, [128, 2, 4, 512])
        KTk = [('M', 'KT', o) for o in range(8)]
        for un in range(4):
            wqb, qk = load_w(wq_d[i], 1024, un * 256, 256)
            for cc in range(2):
                o = un * 2 + cc
                for t in tl:
                    col0, n = TILES[t]
                    uc = col0 - PASS_COL0[p]
                    pi = psn()
                    for k in range(8):
                        mm(psum[pi][:, :n], wqb[:, k, cc * 128:(cc + 1) * 128], ub[:, k, uc:uc + n], k == 0, k == 7,
                           r=[qk, ('u', t, k)], w=[('ps', pi)])
                    cp(evq(), qT[:, o, uc:uc + n], psum[pi][:, :n], r=[('ps', pi)], w=[('S', 'q', o, t)])
        for t in tl:
            col0, n = TILES[t]
            uc = col0 - PASS_COL0[p]
            if t == 4:
                attn_sample(i, qT, oT, uc)
                continue
            for hh in range(4):
                for mc in range(2):
                    pi = psn()
                    for dc in range(2):
                        mm(psum[pi][:, :n], KT[:, 2 * hh + dc, mc * 128:(mc + 1) * 128], qT[:, 2 * hh + dc, uc:uc + n], dc == 0, dc == 1,
                           r=[KTk[2 * hh + dc], ('S', 'q', 2 * hh + dc, t)], w=[('ps', pi)])
                    act(pT[:, mc, hh, :n], psum[pi][:, :n], AF.Exp, r=[('ps', pi)], w=[('S', 'p', mc, hh)], scale=1.0 / 16.0)
                pd = psn()
                for mc in range(2):
                    mm(psum[pd][:, :n], onesb[:], pT[:, mc, hh, :n], mc == 0, mc == 1, r=[C, ('S', 'p', mc, hh)], w=[('ps', pd)])
                recip(rsB[:, :n], psum[pd][:, :n], r=[('ps', pd)], w=['rsB'])
                for dc in range(2):
                    po = psn()
                    for mc in range(2):
                        mm(psum[po][:, :n], Vb[:, mc, (2 * hh + dc) * 128:(2 * hh + dc + 1) * 128], pT[:, mc, hh, :n], mc == 0, mc == 1,
                           r=[('M', 'Vb', mc, (2 * hh + dc) // 2), ('S', 'p', mc, hh)], w=[('ps', po)])
                    tt(oT[:, 2 * hh + dc, uc:uc + n], psum[po][:, :n], rsB[:, :n], ALU.mult, r=[('ps', po), 'rsB'], w=[('S', 'o', 2 * hh + dc, t)])
        for un in range(4):
            wob, ok_ = load_w(wo_d[i], 1024, un * 256, 256)
            for cc in range(2):
                o = un * 2 + cc
                for t in tl:
                    col0, n = TILES[t]
                    uc = col0 - PASS_COL0[p]
                    pi = psn()
                    for k in range(8):
                        mm(psum[pi][:, :n], wob[:, k, cc * 128:(cc + 1) * 128], oT[:, k, uc:uc + n], k == 0, k == 7,
                           r=[ok_, ('S', 'o', k, t)], w=[('ps', pi)])
                    tt(xT[:, o, col0:col0 + n], psum[pi][:, :n], xT[:, o, col0:col0 + n], ALU.add, r=[('ps', pi), ('x', t, o)], w=[('x', t, o)])

    def attn_sample(i, qT, oT, uc):
        t = 4
        for b in range(16):
            Kc = RV((b % 2) * 4096, BF16, [128, 2, 1024])
            Vc = RV(8192 + (b % 2) * 4096, BF16, [128, 2, 1024])
            KcT = RV(16384, BF16, [128, 8, 256])
            S.dma('pool', lambda e, Kc=Kc, b=b: e.dma_start(out=Kc, in_=ck_d[i, b].rearrange("(c p) f -> p c f", p=128)), w=[('R', 'Kc', b % 2)])
            S.dma('pool', lambda e, Vc=Vc, b=b: e.dma_start(out=Vc, in_=cv_d[i, b].rearrange("(c p) f -> p c f", p=128)), w=[('R', 'Vc', b % 2)])
            for mc in range(2):
                pi = psn()
                psb = psum[pi][:].bitcast(BF16)
                for fc in range(8):
                    trp(psb[:, fc * 128:(fc + 1) * 128], Kc[:, mc, fc * 128:(fc + 1) * 128], identb[:], r=[('R', 'Kc', b % 2), C], w=[('ps', pi)])
                cp(evq(), KcT[:, :, mc * 128:(mc + 1) * 128], psb.rearrange("p (k c) -> p k c", k=8), r=[('ps', pi)], w=[('R', 'KcT', mc)])
            pss = psn()
            for hh in range(4):
                for mc in range(2):
                    sl = (mc * 4 + hh) * 8
                    for dc in range(2):
                        mm(psum[pss][:, sl:sl + 8], KcT[:, 2 * hh + dc, mc * 128:(mc + 1) * 128], qT[:, 2 * hh + dc, uc + 8 * b:uc + 8 * b + 8],
                           dc == 0, dc == 1, r=[('R', 'KcT', mc), ('S', 'q', 2 * hh + dc, t)], w=[('ps', pss)])
            act(pTs[:], psum[pss][:, 0:64], AF.Exp, r=[('ps', pss)], w=['pTs'], scale=1.0 / 16.0)
            pd = psn()
            for hh in range(4):
                for mc in range(2):
                    sl = (mc * 4 + hh) * 8
                    mm(psum[pd][:, hh * 8:hh * 8 + 8], onesb[:], pTs[:, sl:sl + 8], mc == 0, mc == 1, r=[C, 'pTs'], w=[('ps', pd)])
            recip(rds[:], psum[pd][:, 0:32], r=[('ps', pd)], w=['rds'])
            po = psn()
            for hh in range(4):
                for dc in range(2):
                    f = 2 * hh + dc
                    for mc in range(2):
                        sl = (mc * 4 + hh) * 8
                        mm(psum[po][:, f * 8:f * 8 + 8], Vc[:, mc, f * 128:(f + 1) * 128], pTs[:, sl:sl + 8], mc == 0, mc == 1,
                           r=[('R', 'Vc', b % 2), 'pTs'], w=[('ps', po)])
            tt(oT[:, :, uc + 8 * b:uc + 8 * b + 8].rearrange("p (h d) t -> p h d t", d=2),
               psum[po][:, 0:64].rearrange("p (h d t) -> p h d t", h=4, d=2),
               rds[:].rearrange("p (h t) -> p h t", h=4).unsqueeze(2).to_broadcast([128, 4, 2, 8]), ALU.mult,
               r=[('ps', po), 'rds'], w=[('S', 'o', k, t) for k in range(8)])

    def poolmix(p, i):
        j = i // 2
        tl = PASSES[p]
        gcol = VL[('nmix', i)]
        xx = SV(0, F32, [128, 8, 1039])
        xss = SV(33248, F32, [128, 8, 16, 23])
        tmp = [RV(4224 * a, F32, [128, 2, 527]) for a in range(2)]
        tmps = [RV(4224 * a, F32, [128, 2, 16, 23]) for a in range(2)]
        stg = RV(8448, F32, [128, 1024])
        ostg = RV(12544, F32, [128, 1024])
        if p == 0:
            memset('dve', xx[:, :, 0:15], 0.0, w=[('S', 'xxh')])
        else:
            cp('dve', xx[:, :, 0:15], xx[:, :, 1024:1039], r=[('S', 'xx', 1, k) for k in range(8)] + [('S', 'xxh')], w=[('S', 'xxh')])
        for t in tl:
            col0, n = TILES[t]
            uc = col0 - PASS_COL0[p]
            if t < 4:
                rms_tile(t, gcol, lambda k, uc=uc, n=n: xx[:, k, 15 + uc:15 + uc + n], lambda k, t=t: [('S', 'xx', t % 2, k)])
            else:
                for half in range(2):
                    dma('sp', stg[0:120, :], spl_d[j, half * 8:(half + 1) * 8].rearrange("b t f -> (b t) f"), r=[], w=[('R', 'stg')])
                    for h2 in range(2):
                        pi = psn()
                        for kk in range(4):
                            k = h2 * 4 + kk
                            trp(psum[pi][:, kk * 120:(kk + 1) * 120], stg[0:120, k * 128:(k + 1) * 128], ident32[0:120, 0:120],
                                r=[('R', 'stg'), C], w=[('ps', pi)])
                        for kk in range(4):
                            k = h2 * 4 + kk
                            cp(evq(), xss[:, k, half * 8:(half + 1) * 8, 0:15], psum[pi][:, kk * 120:(kk + 1) * 120].rearrange("p (b t) -> p b t", b=8),
                               r=[('ps', pi)], w=[('S', 'xsh', k)])
                c0, n = TILES[4]
                rms_stat([xT[:, k, c0:c0 + n] for k in range(8)], n, [('x', 4, k) for k in range(8)], 1024.0)
                for k in range(8):
                    stt(xss[:, k, :, 15:23], xT[:, k, c0:c0 + n].rearrange("p (b t) -> p b t", b=16), vecs[:, gcol + k:gcol + k + 1],
                        rsB[:, :n].rearrange("p (b t) -> p b t", b=16), ALU.mult, ALU.mult,
                        r=[('x', 4, k), 'rsB', C, ('S', 'xsh', k)], w=[('S', 'xs', k)])
        for t in tl:
            col0, n = TILES[t]
            uc = col0 - PASS_COL0[p]
            for g in range(4):
                w_ = 2 << g
                if t < 4:
                    src = xx[:, 2 * g:2 * g + 2, uc:uc + 15 + n]
                    rk = [('S', 'xx', t % 2, 2 * g), ('S', 'xx', t % 2, 2 * g + 1), ('S', 'xxh')]
                    if uc > 0:
                        rk += [('S', 'xx', (t + 1) % 2, 2 * g), ('S', 'xx', (t + 1) % 2, 2 * g + 1)]
                    cur = src
                    step = 1
                    a = 0
                    L = 15 + n
                    while step < w_:
                        dst = tmp[a][:, :, 0:L]
                        tt(dst[:, :, step:L], cur[:, :, step:L], cur[:, :, 0:L - step], ALU.add, r=rk + [('R', 'tmp', 1 - a)], w=[('R', 'tmp', a)])
                        cur = dst
                        a = 1 - a
                        step *= 2
                    la = 1 - a
                    stt(ub[:, 2 * g:2 * g + 2, uc:uc + n], cur[:, :, 15:15 + n], 1.0 / w_, src[:, :, 15:15 + n], ALU.mult, ALU.subtract,
                        r=rk + [('R', 'tmp', la)], w=[('u', t, 2 * g), ('u', t, 2 * g + 1)])
                    if t == 0:
                        for tc_ in range(w_ - 1):
                            stt(ub[:, 2 * g:2 * g + 2, tc_:tc_ + 1], cur[:, :, 15 + tc_:16 + tc_], 1.0 / (tc_ + 1), src[:, :, 15 + tc_:16 + tc_],
                                ALU.mult, ALU.subtract, r=rk + [('R', 'tmp', la)], w=[('u', t, 2 * g), ('u', t, 2 * g + 1)])
                else:
                    src = xss[:, 2 * g:2 * g + 2, :, :]
                    rk = [('S', 'xs', 2 * g), ('S', 'xs', 2 * g + 1)]
                    cur = src
                    step = 1
                    a = 0
                    while step < w_:
                        dst = tmps[a]
                        tt(dst[:, :, :, step:23], cur[:, :, :, step:23], cur[:, :, :, 0:23 - step], ALU.add, r=rk + [('R', 'tmp', 1 - a)], w=[('R', 'tmp', a)])
                        cur = dst
                        a = 1 - a
                        step *= 2
                    la = 1 - a
                    for kk in range(2):
                        stt(ub[:, 2 * g + kk, uc:uc + n].rearrange("p (b t) -> p b t", b=16), cur[:, kk, :, 15:23], 1.0 / w_, src[:, kk, :, 15:23],
                            ALU.mult, ALU.subtract, r=rk + [('R', 'tmp', la)], w=[('u', t, 2 * g + kk)])
        for g in range(4):
            pwb, pk = load_w(pw_d[j, g], 256, 0, 256)
            for oc in range(2):
                o = 2 * g + oc
                for t in tl:
                    col0, n = TILES[t]
                    uc = col0 - PASS_COL0[p]
                    pi = psn()
                    for k in range(2):
                        mm(psum[pi][:, :n], pwb[:, k, oc * 128:(oc + 1) * 128], ub[:, 2 * g + k, uc:uc + n], k == 0, k == 1,
                           r=[pk, ('u', t, 2 * g + k)], w=[('ps', pi)])
                    sc = VL[('psc', j)] + o
                    stt(xT[:, o, col0:col0 + n], psum[pi][:, :n], vecs[:, sc:sc + 1], xT[:, o, col0:col0 + n], ALU.mult, ALU.add,
                        r=[('ps', pi), ('x', t, o), C], w=[('x', t, o)])
        if p == 1:
            for half in range(2):
                pi = psn()
                for kk in range(4):
                    k = half * 4 + kk
                    trp(psum[pi][:, kk * 128:(kk + 1) * 128], xx[:, k, 15 + 896:15 + 1024], ident32[:], r=[('S', 'xx', 1, k), C], w=[('ps', pi)])
                cp(evq(), ostg[:, half * 512:(half + 1) * 512], psum[pi][:], r=[('ps', pi)], w=[('R', 'ostg')])
            dma('act', poolp_o[j], ostg[113:128, :], r=[('R', 'ostg')], w=[])
            ov = ostg.rearrange("p (k c) -> p k c", k=8)
            for k in range(8):
                cp(evq(), ov[:, k, :].rearrange("p (b t) -> p b t", b=16), xss[:, k, :, 15:23], r=[('S', 'xs', k), ('R', 'ostg')], w=[('R', 'ostg')])
            for half in range(2):
                pi = psn()
                for kk in range(4):
                    k = half * 4 + kk
                    trp(psum[pi][:, kk * 128:(kk + 1) * 128], ov[:, k, :], ident32[:], r=[('R', 'ostg'), C], w=[('ps', pi)])
                cp(evq(), stg[:, half * 512:(half + 1) * 512], psum[pi][:], r=[('ps', pi)], w=[('R', 'stg')])
            for b in range(16):
                dma('act', pools_o[j, b, 7:15, :], stg[8 * b:8 * b + 8, :], r=[('R', 'stg')], w=[])
            dma('act', pools_o[j, :, 0:7, :], spl_d[j, :, 8:15, :], r=[], w=[])

    def ssd(p, i):
        j = i // 2
        tl = PASSES[p]
        norm_u(p, VL[('nmix', i)])
        zs = SV(0, BF16, [128, 16, 512])
        xbcT = SV(16384, BF16, [128, 24, 512])
        xdt = SV(40960, BF16, [128, 2048])
        xdte = SV(45056, BF16, [128, 2048])
        Btm = SV(49152, BF16, [128, 512])
        hT = RV(0, F32, [128, 2048])
        hTb = RV(8192, BF16, [128, 2048])
        cst = [RV(12288 + 2112 * a, F32, [128, 528]) for a in range(2)]
        convh = RV(16512, F32, [128, 24, 3])
        dec = RV(16800, BF16, [128, 8, 128])
        Esb = RV(18848, F32, [128, 8, 128])
        cbm = RV(22944, BF16, [128, 128])
        F1 = MV(0, F32, [128, 512])
        F2 = MV(2048, F32, [128, 512])
        F3 = MV(4096, F32, [128, 512])
        A3 = MV(6144, BF16, [128, 512])
        nA3 = MV(7168, BF16, [128, 512])
        Ce = MV(8192, BF16, [128, 8, 128])
        cwc = VL[('cw', j)]
        cbc = VL[('cb', j)]
        dsk = VL[('dsk', j)]
        snw = VL[('snw', j)]
        if p == 0:
            memset('dve', convh, 0.0, w=[('R', 'convh', ci) for ci in range(24)])
            memset('dve', hT, 0.0, w=[('R', 'hT', g) for g in range(4)])
            memset('dve', hTb, 0.0, w=[('R', 'hTb', g) for g in range(4)])
        def ssd_tile(t):
            col0, n = TILES[t]
            uc = col0 - PASS_COL0[p]
            samp = (t == 4)
            nb, bs = (16, 8) if samp else (1, n)
            nch = n // 128
            ukeys = [('u', t, k) for k in range(8)]
            if samp:
                hS = RV(4096, F32, [128, 24, 48])
                sc48 = RV(12288, F32, [128, 1056])
                for q4 in range(3):
                    dma('sp', sc48[0:48, 0:1024], scv_d[j, :, q4 * 1024:(q4 + 1) * 1024], r=[], w=[('R', 'cst', 0), ('R', 'cst', 1)])
                    for h2 in range(2):
                        pi = psn()
                        for kk in range(4):
                            trp(psum[pi][:, kk * 48:(kk + 1) * 48], sc48[0:48, (h2 * 4 + kk) * 128:(h2 * 4 + kk + 1) * 128], ident32[0:48, 0:48],
                                r=[('R', 'cst', 0), C], w=[('ps', pi)])
                        cp(evq(), hS[:, q4 * 8 + h2 * 4:q4 * 8 + h2 * 4 + 4, :], psum[pi][:, 0:192].rearrange("p (k c) -> p k c", k=4),
                           r=[('ps', pi)], w=[('R', 'hS', q4 * 2 + h2)])
            for un in range(20):
                wib, ik = load_w(inw_d[j], 1024, un * 256, 256)
                for cc in range(2):
                    fc = un * 2 + cc
                    pi = psn()
                    for k in range(8):
                        mm(psum[pi][:, :n], wib[:, k, cc * 128:(cc + 1) * 128], ub[:, k, uc:uc + n], k == 0, k == 7, r=[ik, ukeys[k]], w=[('ps', pi)])
                    if fc < 16:
                        act(zs[:, fc, :n], psum[pi][:, :n], AF.Silu, r=[('ps', pi)], w=[('S', 'zs', fc)])
                    else:
                        ci = fc - 16
                        a = ci % 2
                        cv = cst[a][:, 0:nb * (3 + bs)].rearrange("p (b s) -> p b s", b=nb)
                        if samp:
                            cp('dve', cv[:, :, 0:3], hS[:, ci, :].rearrange("p (b s) -> p b s", b=16), r=[('R', 'hS', ci // 4)], w=[('R', 'cst', a)])
                        else:
                            cp('dve', cv[:, :, 0:3], convh[:, ci, :].unsqueeze(1), r=[('R', 'convh', ci)], w=[('R', 'cst', a)])
                        cp('act', cv[:, :, 3:3 + bs], psum[pi][:, :n].rearrange("p (b s) -> p b s", b=nb), r=[('ps', pi)], w=[('R', 'cst', a)])
                        accb, acck = (rsA, 'rsA') if a == 0 else (rsB, 'rsB')
                        accv = accb[:, :n].rearrange("p (b s) -> p b s", b=nb)
                        ts(accv, cv[:, :, 0:bs], vecs[:, cwc + ci * 4:cwc + ci * 4 + 1], vecs[:, cbc + ci:cbc + ci + 1], ALU.mult, ALU.add,
                           r=[('R', 'cst', a), C], w=[acck])
                        for tap in range(1, 4):
                            stt(accv, cv[:, :, tap:tap + bs], vecs[:, cwc + ci * 4 + tap:cwc + ci * 4 + tap + 1], accv, ALU.mult, ALU.add,
                                r=[('R', 'cst', a), acck, C], w=[acck])
                        act(xbcT[:, ci, :n], accb[:, :n], AF.Silu, r=[acck], w=[('S', 'xbc', ci)])
                        if samp:
                            cp('dve', hS[:, ci, :].rearrange("p (b s) -> p b s", b=16), cv[:, :, bs:bs + 3], r=[('R', 'cst', a)], w=[('R', 'hS2', ci), ('R', 'hS', ci // 4)])
                        else:
                            cp('dve', convh[:, ci, :].unsqueeze(1), cv[:, :, bs:bs + 3], r=[('R', 'cst', a)], w=[('R', 'convh', ci)])
            wdt, dk_ = load_w(inw_d[j], 1024, 5120, 32)
            pdt = psn()
            for r3 in range(3):
                for k in range(8):
                    mm(psum[pdt][32 * r3:32 * r3 + 32, :n], wdt[:, k, :], ub[:, k, uc:uc + n], k == 0, k == 7, r=[dk_, ukeys[k]], w=[('ps', pdt)])
            act(F1[0:96, :n], psum[pdt][0:96, :n], AF.Exp, r=[('ps', pdt), C], w=[('M', 'F1')], bias=hv[:, 2 * j:2 * j + 1])
            act(F2[0:96, :n], F1[0:96, :n], AF.Ln, r=[('M', 'F1')], w=[('M', 'F2')], bias=1.0)
            ts(F1[0:96, :n], F2[0:96, :n], avec[:, j:j + 1], None, ALU.mult, None, r=[('M', 'F2'), ('avec', j)], w=[('M', 'F1')])
            for c in range(nch):
                cs = slice(c * 128, (c + 1) * 128)
                d0 = resetm[:, :] if samp else ones1[0:96, 0:1].to_broadcast([96, 128])
                S.op('dve', lambda e, cs=cs, d0=d0: e.tensor_tensor_scan(out=F3[0:96, cs], data0=d0, data1=F1[0:96, cs], initial=0.0,
                                                                       op0=ALU.mult, op1=ALU.add), r=[('M', 'F1'), C], w=[('M', 'F3')])
            cp('dve', A3[0:96, :n], F3[0:96, :n], r=[('M', 'F3')], w=[('M', 'A3')])
            tt(F1[0:96, :n], F3[0:96, :n], A3[0:96, :n], ALU.subtract, r=[('M', 'F3'), ('M', 'A3')], w=[('M', 'F1')])
            cp('dve', A3[32:64, :n], F1[32:64, :n], r=[('M', 'F1')], w=[('M', 'A3')])
            cp('dve', A3[64:96, :n], F1[64:96, :n], r=[('M', 'F1')], w=[('M', 'A3')])
            tt(F1[64:96, :n], F1[64:96, :n], A3[64:96, :n], ALU.subtract, r=[('M', 'F1'), ('M', 'A3')], w=[('M', 'F1')])
            cp('dve', A3[64:96, :n], F1[64:96, :n], r=[('M', 'F1')], w=[('M', 'A3')])
            ts(nA3[0:96, :n], A3[0:96, :n], -1.0, None, ALU.mult, None, r=[('M', 'A3')], w=[('M', 'nA3')])
            m01 = m01bd if samp else m01c
            mng = mngbd if samp else mngc
            for c in range(nch):
                cs = slice(c * 128, (c + 1) * 128)
                first = (t == 0 and c == 0)
                cp('act', STt[0:32, :], F2[0:32, cs], r=[('M', 'F2')], w=['STt'])
                cbs = 8 if samp else 128
                a3v = F3[32:64, cs].rearrange("p (b s) -> p b s", b=nb)
                tt(STt[32:64, :].rearrange("p (b s) -> p b s", b=nb), a3v[:, :, cbs - 1:cbs].to_broadcast([32, nb, cbs]), a3v,
                   ALU.subtract, r=[('M', 'F3'), 'STt'], w=['STt'])
                act(STt[32:64, :], STt[32:64, :], AF.Exp, r=['STt'], w=['STt'])
                tt(STt[32:64, :], STt[32:64, :], F2[32:64, cs], ALU.mult, r=['STt', ('M', 'F2')], w=['STt'])
                pi = psn()
                trp(psum[pi][:, 0:64], STt[:, :], ident32[0:64, 0:64], r=['STt', C], w=[('ps', pi)])
                cp('act', sctm[:, :], psum[pi][:, 0:64], r=[('ps', pi)], w=['sctm'])
                px = [psn(), psn(), psn()]
                for ci in range(20):
                    pb = psum[px[ci // 8]][:].bitcast(BF16)
                    trp(pb[:, (ci % 8) * 128:(ci % 8 + 1) * 128], xbcT[:, ci, cs], identb[:], r=[('S', 'xbc', ci), C], w=[('ps', px[ci // 8])])
                for hf in range(2):
                    pb = psum[px[hf]][:].bitcast(BF16).rearrange("p (h d) -> p h d", h=16)
                    tt(xdt[:, hf * 1024:(hf + 1) * 1024].rearrange("p (h d) -> p h d", h=16), pb,
                       sctm[:, hf * 16:(hf + 1) * 16].unsqueeze(2).to_broadcast([128, 16, 64]), ALU.mult,
                       r=[('ps', px[hf]), 'sctm'], w=[('S', 'xdt', hf)])
                    tt(xdte[:, hf * 1024:(hf + 1) * 1024].rearrange("p (h d) -> p h d", h=16), pb,
                       sctm[:, 32 + hf * 16:32 + (hf + 1) * 16].unsqueeze(2).to_broadcast([128, 16, 64]), ALU.mult,
                       r=[('ps', px[hf]), 'sctm'], w=[('S', 'xdte', hf)])
                cp('act', Btm[:, :], psum[px[2]][:].bitcast(BF16)[:, 0:512], r=[('ps', px[2])], w=[('S', 'Btm')])
                ypss = []
                for g in range(4):
                    pc = psn()
                    mm(psum[pc][:, 0:128], xbcT[:, 16 + g, cs], xbcT[:, 20 + g, cs], True, True, r=[('S', 'xbc', 16 + g), ('S', 'xbc', 20 + g)], w=[('ps', pc)])
                    tt(cbm[:, :], psum[pc][:, 0:128], m01[:], ALU.mult, r=[('ps', pc), C], w=[('R', 'cbm')])
                    pB = [psn(), psn()]
                    pE = [psn(), psn()]
                    for ih in range(8):
                        hh = 8 * g + ih
                        sel = i3b[:, hh:hh + 1].to_broadcast([96, 128])
                        ob = psum[pB[ih // 4]][:, (ih % 4) * 128:(ih % 4 + 1) * 128]
                        mm(ob, sel, A3[0:96, cs], True, False, r=[C, ('M', 'A3')], w=[('ps', pB[ih // 4])])
                        mm(ob, nA3[0:96, cs], sel, False, False, r=[C, ('M', 'nA3')], w=[('ps', pB[ih // 4])])
                        mm(ob, identb[:], mng[:], False, True, r=[C], w=[('ps', pB[ih // 4])])
                        oe = psum[pE[ih // 4]][:, (ih % 4) * 128:(ih % 4 + 1) * 128]
                        mm(oe, sel, A3[0:96, cs], True, True, r=[C, ('M', 'A3')], w=[('ps', pE[ih // 4])])
                    for q in range(2):
                        act(dec[:, q * 4:(q + 1) * 4, :], psum[pB[q]][:].rearrange("p (h c) -> p h c", h=4), AF.Exp, r=[('ps', pB[q])], w=[('R', 'dec', q)])
                        act(Esb[:, q * 4:(q + 1) * 4, :], psum[pE[q]][:].rearrange("p (h c) -> p h c", h=4), AF.Exp, r=[('ps', pE[q])], w=[('R', 'Esb', q)])
                    tt(dec[:, :, :], dec[:, :, :], cbm[:, :].unsqueeze(1).to_broadcast([128, 8, 128]), ALU.mult,
                       r=[('R', 'dec', 0), ('R', 'dec', 1), ('R', 'cbm')], w=[('R', 'dec', 0), ('R', 'dec', 1)])
                    if not first:
                        tt(Ce[:, :, :], Esb[:, :, :], xbcT[:, 20 + g, cs].unsqueeze(1).to_broadcast([128, 8, 128]), ALU.mult,
                           r=[('R', 'Esb', 0), ('R', 'Esb', 1), ('S', 'xbc', 20 + g)], w=[('M', 'Ce')])
                    if samp:
                        cp('act', cdE[:, 8 * g:8 * g + 8, :], Esb[:, :, :].rearrange("p h (b s) -> p h b s", b=16)[:, :, :, 7],
                           r=[('R', 'Esb', 0), ('R', 'Esb', 1)], w=[('cdE', g)])
                    py = psn()
                    ypss.append(py)
                    for hp in range(4):
                        for sd in range(2):
                            hh = 8 * g + 2 * hp + sd
                            oy = psum[py][64 * sd:64 * sd + 64, hp * 128:(hp + 1) * 128]
                            only = first
                            mm(oy, xdt[:, hh * 64:(hh + 1) * 64], dec[:, 2 * hp + sd, :], (hp == 0) if samp else True, only,
                               r=[('S', 'xdt', hh // 16), ('R', 'dec', 0), ('R', 'dec', 1)], w=[('ps', py)])
                            if not only and not samp:
                                mm(oy, hTb[:, hh * 64:(hh + 1) * 64], Ce[:, 2 * hp + sd, :], False, True, r=[('R', 'hTb', g), ('M', 'Ce')], w=[('ps', py)])
                    if not samp:
                        finish_y(g, py, cs, j, zs, xbcT, dsk)
                        pst = psn()
                        mm(psum[pst][:, :], Btm[:, g * 128:(g + 1) * 128], xdte[:, g * 512:(g + 1) * 512], True, True,
                           r=[('S', 'Btm'), ('S', 'xdte', g // 2)], w=[('ps', pst)])
                        hv_ = hT[:, g * 512:(g + 1) * 512].rearrange("p (h d) -> p h d", h=8)
                        if not first:
                            tt(hv_, hv_, Esb[:, :, 127:128].to_broadcast([128, 8, 64]), ALU.mult, r=[('R', 'hT', g), ('R', 'Esb', 0), ('R', 'Esb', 1)], w=[('R', 'hT', g)])
                        tt(hT[:, g * 512:(g + 1) * 512], hT[:, g * 512:(g + 1) * 512], psum[pst][:, :], ALU.add, r=[('R', 'hT', g), ('ps', pst)], w=[('R', 'hT', g)])
                        cp('act', hTb[:, g * 512:(g + 1) * 512], hT[:, g * 512:(g + 1) * 512], r=[('R', 'hT', g)], w=[('R', 'hTb', g)])
                    else:
                        st['resv'].add(py)
                        ssd_sample_group(j, g, py, Ce, Esb, xdte, Btm, xbcT, zs, dsk, cs)
                        st['resv'].discard(py)
            for g in range(4):
                rms_stat([zs[:, 4 * g + q, :n] for q in range(4)], n, [('S', 'zs', 4 * g + q) for q in range(4)], 512.0)
                for q in range(4):
                    fc = 4 * g + q
                    stt(zs[:, fc, :n], zs[:, fc, :n], vecs[:, snw + fc:snw + fc + 1], rsB[:, :n], ALU.mult, ALU.mult,
                        r=[('S', 'zs', fc), 'rsB', C], w=[('S', 'zs', fc)])
            for un in range(8):
                wob, ok_ = load_w(outw_d[j], 2048, un * 128, 128)
                pi = psn()
                for kc in range(16):
                    mm(psum[pi][:, :n], wob[:, kc, :], zs[:, kc, :n], kc == 0, kc == 15, r=[ok_, ('S', 'zs', kc)], w=[('ps', pi)])
                tt(xT[:, un, col0:col0 + n], psum[pi][:, :n], xT[:, un, col0:col0 + n], ALU.add, r=[('ps', pi), ('x', t, un)], w=[('x', t, un)])
            if samp:
                for q4 in range(4):
                    osg = RV(12288, F32, [128, 768])
                    for half in range(2):
                        pi = psn()
                        for kk in range(3):
                            ci = q4 * 6 + half * 3 + kk
                            trp(psum[pi][0:48, kk * 128:(kk + 1) * 128], hS[:, ci, :], ident32[:], r=[('R', 'hS2', ci), C], w=[('ps', pi)])
                        cp(evq(), osg[0:48, half * 384:(half + 1) * 384], psum[pi][0:48, 0:384], r=[('ps', pi)], w=[('R', 'cst', 0), ('R', 'cst', 1)])
                    dma('act', convs_o[j, :, q4 * 768:(q4 + 1) * 768], osg[0:48, :], r=[('R', 'cst', 0)], w=[])
        def prompt_state_out():
            ost = RV(12288, F32, [128, 1056])
            for q4 in range(4):
                pi = psn()
                for kk in range(4):
                    f = q4 * 4 + kk
                    trp(psum[pi][:, kk * 128:(kk + 1) * 128], hT[:, f * 128:(f + 1) * 128], ident32[:], r=[('R', 'hT', f // 4), C], w=[('ps', pi)])
                cp(evq(), ost[:, 0:512], psum[pi][:, :], r=[('ps', pi)], w=[('R', 'cst', 0), ('R', 'cst', 1)])
                for kk in range(4):
                    f = q4 * 4 + kk
                    dma('act', ssmp_o[j, f * 128:(f + 1) * 128, :], ost[:, kk * 128:(kk + 1) * 128], r=[('R', 'cst', 0)], w=[])
            for q4 in range(4):
                for half in range(2):
                    pi = psn()
                    for kk in range(3):
                        ci = q4 * 6 + half * 3 + kk
                        trp(psum[pi][0:3, kk * 128:(kk + 1) * 128], convh[:, ci, :], ident32[:], r=[('R', 'convh', ci), C], w=[('ps', pi)])
                    cp(evq(), ost[0:3, half * 384:(half + 1) * 384], psum[pi][0:3, 0:384], r=[('ps', pi)], w=[('R', 'cst', 0), ('R', 'cst', 1)])
                dma('act', convp_o[j, :, q4 * 768:(q4 + 1) * 768], ost[0:3, 0:768], r=[('R', 'cst', 0)], w=[])

        for t in tl:
            ssd_tile(t)
            if t == 3:
                prompt_state_out()
                S.retire('R')

    def finish_y(g, py, cs, j, zs, xbcT, dsk):
        for hp in range(4):
            fc = 4 * g + hp
            a = fc % 2
            stt(ytmp[a][:, :], xbcT[:, fc, cs], vecs[:, dsk + fc:dsk + fc + 1], psum[py][:, hp * 128:(hp + 1) * 128], ALU.mult, ALU.add,
                r=[('S', 'xbc', fc), ('ps', py), C], w=[('ytmp', a)])
            tt(zs[:, fc, cs], ytmp[a][:, :], zs[:, fc, cs], ALU.mult, r=[('ytmp', a), ('S', 'zs', fc)], w=[('S', 'zs', fc)])

    def ssd_sample_group(j, g, py, Ce, Esb, xdte, Btm, xbcT, zs, dsk, cs):
        h0 = [RV(a * 2048, F32, [128, 4, 128]) for a in range(2)]
        h0T = RV(8704, BF16, [128, 512])
        Bm = RV(9728, BF16, [128, 128])
        nst = [RV(9984, F32, [128, 4, 128]) for a in range(2)]
        for b in range(16):
            a = b % 2
            dma('sp', h0[a], sst_d[j, b, g * 512:(g + 1) * 512, :].rearrange("(f p) n -> p f n", p=128), r=[], w=[('R', 'h0', a)])
            pi = psn()
            for f in range(4):
                trp(psum[pi][:, f * 128:(f + 1) * 128], h0[a][:, f, :], ident32[:], r=[('R', 'h0', a), C], w=[('ps', pi)])
            cp('act', h0T[:, :], psum[pi][:, :], r=[('ps', pi)], w=[('R', 'h0T')])
            for hp in range(4):
                for sd in range(2):
                    oy = psum[py][64 * sd:64 * sd + 64, hp * 128 + 8 * b:hp * 128 + 8 * b + 8]
                    mm(oy, h0T[:, hp * 128 + 64 * sd:hp * 128 + 64 * sd + 64], Ce[:, 2 * hp + sd, 8 * b:8 * b + 8], False, (b == 15),
                       r=[('R', 'h0T'), ('M', 'Ce')], w=[('ps', py)])
            ts(Bm[:, :], Btm[:, g * 128:(g + 1) * 128], bmask[:, b:b + 1], None, ALU.mult, None, r=[('S', 'Btm'), C], w=[('R', 'Bm')])
            pst = psn()
            for f in range(4):
                mm(psum[pst][:, f * 128:(f + 1) * 128], xdte[:, (4 * g + f) * 128:(4 * g + f + 1) * 128], Bm[:, :], True, True,
                   r=[('S', 'xdte', g // 2), ('R', 'Bm')], w=[('ps', pst)])
            for f in range(4):
                for sd in range(2):
                    hh = 8 * g + 2 * f + sd
                    sl = slice(64 * sd, 64 * sd + 64)
                    stt(nst[a][sl, f, :], h0[a][sl, f, :], cdE[sl, hh, b:b + 1], psum[pst][sl, f * 128:(f + 1) * 128], ALU.mult, ALU.add,
                        r=[('R', 'h0', a), ('cdE', g), ('ps', pst)], w=[('R', 'nst')])
            dma('act', ssms_o[j, b, g * 512:(g + 1) * 512, :].rearrange("(f p) n -> p f n", p=128), nst[a], r=[('R', 'nst')], w=[])
        finish_y(g, py, cs, j, zs, xbcT, dsk)

    for i in range(NL):
        for sub in SUBS:
            if sub == 'ffn1':
                for p in range(2):
                    ffn(p, i, wg1_d, wu1_d, wd1_d, 'nf1')
            elif sub == 'ffn2':
                for p in range(2):
                    ffn(p, i, wg2_d, wu2_d, wd2_d, 'nf2')
            elif sub == 'mix':
                for p in range(2):
                    if i % 2 == 0:
                        ssd(p, i)
                    else:
                        poolmix(p, i)
            elif sub == 'attn':
                memkv(i)
                for p in range(2):
                    attn(p, i)
            S.retire('S')
            S.retire('R')
            S.retire('M')

    gcol = VL['fin']
    for t in range(5):
        c0, n = TILES[t]
        yn = SV(0, F32, [128, 8, 512])
        rms_tile(t, gcol, lambda k, n=n: yn[:, k, :n], lambda k: [('S', 'yn', k)])
        for blk in range(n // 128):
            ostg = SV(16384 + ((st['ev'] // 2) % 2) * 4096, F32, [128, 1024])
            a = (st['ev'] // 2) % 2
            for half in range(2):
                pi = psn()
                for kk in range(4):
                    k = half * 4 + kk
                    trp(psum[pi][:, kk * 128:(kk + 1) * 128], yn[:, k, blk * 128:(blk + 1) * 128], ident32[:], r=[('S', 'yn', k), C], w=[('ps', pi)])
                cp('act' if half else 'dve', ostg[:, half * 512:(half + 1) * 512], psum[pi][:], r=[('ps', pi)], w=[('S', 'ostg', a, half)])
            st['ev'] += 2
            dst = yp_o[c0 + blk * 128:c0 + (blk + 1) * 128, :] if t < 4 else ys_o
            dma('act', dst, ostg, r=[('S', 'ostg', a, 0), ('S', 'ostg', a, 1)], w=[])

    S.finish()
    S.emit(nc)
    es.close()
    return nc


_NC_CACHE = {}


def _fm(v):
    v = np.asarray(v, dtype=np.float32)
    return np.ascontiguousarray(v.reshape(-1, 128).T)


def pack_vecs(inp):
    vecs = np.zeros((128, NV), np.float32)
    for i in range(4):
        for nm, key in (('nf1', 'norm_ffn1'), ('nmix', 'norm_mix'), ('ncr', 'norm_cross'), ('nmem', 'norm_mem'), ('nf2', 'norm_ffn2')):
            c = VL[(nm, i)]
            vecs[:, c:c + 8] = _fm(inp[key][i])
    c = VL['fin']
    vecs[:, c:c + 8] = _fm(inp['final_norm'])
    for j in range(2):
        c = VL[('psc', j)]
        vecs[:, c:c + 8] = _fm(inp['pool_scale'][j])
        c = VL[('snw', j)]
        vecs[:, c:c + 16] = _fm(inp['ssd_norm_w'][j])
        c = VL[('cw', j)]
        cw = np.asarray(inp['ssd_conv_w'][j], np.float32).reshape(4, 24, 128)
        vecs[:, c:c + 96] = np.ascontiguousarray(cw.transpose(2, 1, 0)).reshape(128, 96)
        c = VL[('cb', j)]
        vecs[:, c:c + 24] = _fm(inp['ssd_conv_b'][j])
        c = VL[('dsk', j)]
        vecs[:, c:c + 16] = _fm(np.repeat(np.asarray(inp['ssd_d'][j], np.float32), 64))
    hv = np.zeros((96, 4), np.float32)
    for j in range(2):
        hv[:, 2 * j] = np.tile(np.asarray(inp['ssd_dt_bias'][j], np.float32), 3)
        hv[:, 2 * j + 1] = np.tile(np.asarray(inp['ssd_a_log'][j], np.float32), 3)
    return vecs, hv


def make_in_maps(inp):
    f = lambda a: np.ascontiguousarray(np.asarray(a, dtype=np.float32))
    vecs, hv = pack_vecs(inp)
    shared = {
        "vecs": vecs, "hv": hv,
        "wg1": f(inp['ffn1_w_gate']), "wu1": f(inp['ffn1_w_up']), "wd1": f(inp['ffn1_w_down']),
        "wg2": f(inp['ffn2_w_gate']), "wu2": f(inp['ffn2_w_up']), "wd2": f(inp['ffn2_w_down']),
        "inw": f(inp['ssd_in_w']), "outw": f(inp['ssd_out_w']), "pw": f(inp['pool_w']),
        "wq": f(inp['xa_wq']), "wk": f(inp['xa_wk']), "wv": f(inp['xa_wv']), "wo": f(inp['xa_wo']),
    }
    maps = []
    for c in range(8):
        sl = slice(16 * c, 16 * c + 16)
        m = dict(shared)
        m["xp"] = f(inp['x_prompt'][c])
        m["xs"] = f(np.asarray(inp['x_sample'])[sl].reshape(128, 1024))
        m["mem"] = f(inp['mem_prompt'][c])
        m["ck"] = f(np.asarray(inp['cache_mem_k'])[:, sl].reshape(4, 16, 256, 1024))
        m["cv"] = f(np.asarray(inp['cache_mem_v'])[:, sl].reshape(4, 16, 256, 1024))
        m["sst"] = f(np.asarray(inp['state_ssm'])[:, sl].reshape(2, 16, 2048, 128))
        m["scv"] = f(np.asarray(inp['state_conv'])[:, sl].reshape(2, 48, 3072))
        m["spl"] = f(np.asarray(inp['state_pool'])[:, sl])
        maps.append(m)
    return maps


def assemble(results):
    R = results
    y_p = np.stack([R[c]["y_p"] for c in range(8)], 0)
    y_s = np.concatenate([R[c]["y_s"].reshape(16, 8, 1024) for c in range(8)], 0)
    ssm_p = np.stack([R[c]["ssm_p"].reshape(2, 32, 64, 128) for c in range(8)], 1)
    conv_p = np.stack([R[c]["conv_p"] for c in range(8)], 1)
    pool_p = np.stack([R[c]["pool_p"] for c in range(8)], 1)
    mk_p = np.stack([R[c]["mk_p"].reshape(4, 256, 4, 256) for c in range(8)], 1)
    mv_p = np.stack([R[c]["mv_p"].reshape(4, 256, 4, 256) for c in range(8)], 1)
    ssm_s = np.concatenate([R[c]["ssm_s"].reshape(2, 16, 32, 64, 128) for c in range(8)], 1)
    conv_s = np.concatenate([R[c]["conv_s"].reshape(2, 16, 3, 3072) for c in range(8)], 1)
    pool_s = np.concatenate([R[c]["pool_s"] for c in range(8)], 1)
    outs = (y_p, y_s, ssm_p, conv_p, pool_p, mk_p, mv_p, ssm_s, conv_s, pool_s)
    return tuple(np.ascontiguousarray(o.astype(np.float32, copy=False)) for o in outs)


def kernel(**inputs):
    if 'nc' not in _NC_CACHE:
        _NC_CACHE['nc'] = build()
    nc = _NC_CACHE['nc']
    in_maps = make_in_maps(inputs)
    res = run_bass_kernel_spmd(nc, in_maps, core_ids=list(range(8)))
    return assemble(res.results)
```

```python
import numpy as np
from contextlib import ExitStack
import concourse.bass as bass
import concourse.mybir as mybir
from concourse.bass_utils import run_bass_kernel_spmd

F32 = mybir.dt.float32
BF16 = mybir.dt.bfloat16
AF = mybir.ActivationFunctionType
ALU = mybir.AluOpType

EPOCH = 30000
NSLOT = 8
EPS = 1e-5

D = 1024
DFF = 2816
T = 2176
TILES = [(0, 512), (512, 512), (1024, 512), (1536, 512), (2048, 128)]
PASSES = [[0, 1], [2, 3, 4]]
PASS_COL0 = [0, 1024]
NWB = 5
WBE = 2048


class Sched:
    def __init__(self):
        self.engs = ['pe', 'act', 'dve', 'pool', 'sp']
        self.stream = {e: [] for e in self.engs}
        self.cnt = {e: 0 for e in self.engs}
        self.res = {}
        self.waited = {e: {} for e in self.engs}
        self.dmacnt = {e: 0 for e in self.engs}
        self.semkeys = {}
        self.fence = {}

    def _semkey(self, k):
        self.semkeys.setdefault(k, None)
        return k

    def _need(self, eng, tok, waits, skip_self=False):
        if tok[0] == 'e':
            _, e, idx = tok
            if e == eng and (eng == 'pe' or skip_self):
                return
            key = ('e', e)
            if self.waited[eng].get(key, -1) >= idx:
                return
            self.waited[eng][key] = idx
            waits[key] = (self._semkey(('e', e, idx // EPOCH)), idx % EPOCH + 1)
        else:
            _, q, k = tok
            key = ('d', q, k % NSLOT)
            if self.waited[eng].get(key, -1) >= k:
                return
            self.waited[eng][key] = k
            waits[key] = (self._semkey(('d', q, k % NSLOT)), 16 * (k // NSLOT + 1))

    def _deps(self, eng, r, w):
        waits = {}
        for key in list(r) + list(w):
            if isinstance(key, tuple) and key[0] in self.fence and key not in self.res:
                for t in self.fence[key[0]].values():
                    self._need(eng, t, waits)
        for key in r:
            st = self.res.get(key)
            if st and st['w'] is not None:
                self._need(eng, st['w'], waits)
        for key in w:
            st = self.res.get(key)
            if st:
                if st['w'] is not None:
                    self._need(eng, st['w'], waits, True)
                for t in st['r'].values():
                    self._need(eng, t, waits, True)
        return waits

    def _mark(self, tok, r, w):
        for key in r:
            st = self.res.setdefault(key, {'w': None, 'r': {}})
            if tok[0] == 'e':
                st['r'][('e', tok[1])] = tok
            else:
                st['r'][tok] = tok
        for key in w:
            self.res[key] = {'w': tok, 'r': {}}

    def op(self, eng, fn, r=(), w=()):
        waits = self._deps(eng, r, w)
        for (sk, val) in waits.values():
            self.stream[eng].append(('wait', sk, val))
        idx = self.cnt[eng]
        self.cnt[eng] += 1
        self.stream[eng].append(('op', fn, self._semkey(('e', eng, idx // EPOCH))))
        self._mark(('e', eng, idx), r, w)

    def dma(self, q, fn, r=(), w=()):
        waits = self._deps(q, r, w)
        k = self.dmacnt[q]
        self.dmacnt[q] += 1
        if k >= NSLOT:
            self._need(q, ('d', q, k - NSLOT), waits)
        for (sk, val) in waits.values():
            self.stream[q].append(('wait', sk, val))
        self.stream[q].append(('dma', fn, self._semkey(('d', q, k % NSLOT))))
        self._mark(('d', q, k), r, w)

    def retire(self, region):
        toks = dict(self.fence.get(region, {}))
        for key in list(self.res):
            if isinstance(key, tuple) and key[0] == region:
                st = self.res.pop(key)
                for t in ([st['w']] if st['w'] is not None else []) + list(st['r'].values()):
                    if t[0] == 'e':
                        k = ('e', t[1])
                        if k not in toks or toks[k][2] < t[2]:
                            toks[k] = t
                    else:
                        toks[t] = t
        self.fence[region] = toks

    def finish(self):
        for q in self.engs:
            n = self.dmacnt[q]
            for k in range(max(0, n - NSLOT), n):
                waits = {}
                self._need(q, ('d', q, k), waits)
                for (sk, val) in waits.values():
                    self.stream[q].append(('wait', sk, val))

    def emit(self, nc):
        with ExitStack() as es:
            sems = {}
            for i, k in enumerate(self.semkeys):
                sems[k] = es.enter_context(nc.semaphore("s%d" % i))
            block = es.enter_context(nc.Block())

            def run(e, eng):
                for it in self.stream[e]:
                    if it[0] == 'wait':
                        eng.wait_ge(sems[it[1]], it[2])
                    elif it[0] == 'op':
                        it[1](eng).then_inc(sems[it[2]], 1)
                    else:
                        it[1](eng).then_inc(sems[it[2]], 16)

            @block.tensor
            def _(eng):
                run('pe', eng)

            @block.scalar
            def _(eng):
                run('act', eng)

            @block.vector
            def _(eng):
                run('dve', eng)

            @block.gpsimd
            def _(eng):
                run('pool', eng)

            @block.sync
            def _(eng):
                run('sp', eng)


def vec_layout():
    lay = {}
    c = 0
    for i in range(4):
        for nm in ('nf1', 'nmix', 'ncr', 'nmem', 'nf2'):
            lay[(nm, i)] = c
            c += 8
    lay['fin'] = c
    c += 8
    for j in range(2):
        lay[('psc', j)] = c
        c += 8
        lay[('snw', j)] = c
        c += 16
        lay[('cw', j)] = c
        c += 96
        lay[('cb', j)] = c
        c += 24
        lay[('dsk', j)] = c
        c += 16
    return lay, c


VL, NV = vec_layout()


def build(cfg=None):
    cfg = cfg or {}
    NL = cfg.get('nlayers', 4)
    SUBS = cfg.get('subs', ('ffn1', 'mix', 'attn', 'ffn2'))
    nc = bass.Bass("TRN2", target_bir_lowering=False)
    S = Sched()

    def din(name, shape):
        return nc.dram_tensor(name, shape, F32, kind="ExternalInput").ap()

    def dout(name, shape):
        return nc.dram_tensor(name, shape, F32, kind="ExternalOutput").ap()

    xp_d = din("xp", [2048, 1024])
    xs_d = din("xs", [128, 1024])
    mem_d = din("mem", [256, 1024])
    ck_d = din("ck", [4, 16, 256, 1024])
    cv_d = din("cv", [4, 16, 256, 1024])
    sst_d = din("sst", [2, 16, 2048, 128])
    scv_d = din("scv", [2, 48, 3072])
    spl_d = din("spl", [2, 16, 15, 1024])
    vecs_d = din("vecs", [128, NV])
    hv_d = din("hv", [96, 4])
    wg1_d = din("wg1", [4, 1024, 2816])
    wu1_d = din("wu1", [4, 1024, 2816])
    wd1_d = din("wd1", [4, 2816, 1024])
    wg2_d = din("wg2", [4, 1024, 2816])
    wu2_d = din("wu2", [4, 1024, 2816])
    wd2_d = din("wd2", [4, 2816, 1024])
    inw_d = din("inw", [2, 1024, 5152])
    outw_d = din("outw", [2, 2048, 1024])
    pw_d = din("pw", [2, 4, 256, 256])
    wq_d = din("wq", [4, 1024, 1024])
    wk_d = din("wk", [4, 1024, 1024])
    wv_d = din("wv", [4, 1024, 1024])
    wo_d = din("wo", [4, 1024, 1024])

    yp_o = dout("y_p", [2048, 1024])
    ys_o = dout("y_s", [128, 1024])
    ssmp_o = dout("ssm_p", [2, 2048, 128])
    convp_o = dout("conv_p", [2, 3, 3072])
    poolp_o = dout("pool_p", [2, 15, 1024])
    mkp_o = dout("mk_p", [4, 256, 1024])
    mvp_o = dout("mv_p", [4, 256, 1024])
    ssms_o = dout("ssm_s", [2, 16, 2048, 128])
    convs_o = dout("conv_s", [2, 48, 3072])
    pools_o = dout("pool_s", [2, 16, 15, 1024])

    es = ExitStack()

    def sb(name, shape, dt):
        return es.enter_context(nc.sbuf_tensor("s_" + name, shape, dt))

    xT = sb("xT", [128, 8, T], F32)
    ub = sb("ub", [128, 8, 1152], BF16)
    Sreg = sb("Sreg", [128, 25344], BF16)
    Rreg = sb("Rreg", [128, 5888], F32)
    Mreg = sb("Mreg", [128, 2560], F32)
    wbuf = [sb("wb%d" % i, [128, WBE], BF16) for i in range(NWB)]
    vecs = sb("vecs", [128, NV], F32)
    hv = sb("hv", [96, 4], F32)
    avec = sb("avec", [96, 2], F32)
    ident32 = sb("ident32", [128, 128], F32)
    identb = sb("identb", [128, 128], BF16)
    onesb = sb("onesb", [128, 128], BF16)
    ones1 = sb("ones1", [128, 1], F32)
    m01c = sb("m01c", [128, 128], BF16)
    m01bd = sb("m01bd", [128, 128], BF16)
    mngc = sb("mngc", [128, 128], BF16)
    mngbd = sb("mngbd", [128, 128], BF16)
    i3b = sb("i3b", [96, 32], BF16)
    resetm = sb("resetm", [96, 128], F32)
    bmask = sb("bmask", [128, 16], F32)
    sq2 = [sb("sq%d" % i, [128, 512], BF16) for i in range(2)]
    rsA = sb("rsA", [128, 512], F32)
    rsB = sb("rsB", [128, 512], F32)
    sctm = sb("sctm", [128, 64], F32)
    STt = sb("STt", [64, 128], F32)
    cdv = [sb("cdv%d" % i, [128, 8], F32) for i in range(2)]
    Dd = sb("Dd", [128, 16, 128], BF16)
    cdE = sb("cdE", [128, 32, 16], F32)
    pTs = sb("pTs", [128, 64], BF16)
    rds = sb("rds", [128, 32], F32)
    psum = [es.enter_context(nc.psum_tensor("ps%d" % i, [128, 512], F32)) for i in range(8)]

    st = {'ps': 0, 'wb': 0, 'sq': 0, 'ev': 0, 'resv': set()}

    def psn():
        while True:
            i = st['ps'] % 8
            st['ps'] += 1
            if i not in st['resv']:
                return i

    def carve(reg, esz, off, dt, shape):
        n = 1
        for s_ in shape[1:]:
            n *= s_
        dsz = 4 if dt == F32 else 2
        nbytes = n * dsz
        a = reg[:, off // esz:(off + nbytes) // esz]
        if dsz != esz:
            a = a.bitcast(dt)
        if len(shape) == 3:
            a = a.rearrange("p (a b) -> p a b", a=shape[1])
        elif len(shape) == 4:
            a = a.rearrange("p (a b c) -> p a b c", a=shape[1], b=shape[2])
        return a

    def SV(off, dt, shape):
        return carve(Sreg, 2, off, dt, shape)

    def RV(off, dt, shape):
        return carve(Rreg, 4, off, dt, shape)

    def MV(off, dt, shape):
        return carve(Mreg, 4, off, dt, shape)

    def mm(out, lhsT, rhs, start, stop, r, w):
        S.op('pe', lambda e: e.matmul(out, lhsT=lhsT, rhs=rhs, start=start, stop=stop), r=r, w=w)

    def trp(out, in_, ident, r, w):
        S.op('pe', lambda e: e.transpose(out, in_, ident), r=r, w=w)

    def act(out, in_, func, r, w, bias=None, scale=None):
        kw = {}
        if bias is not None:
            kw['bias'] = bias
        if scale is not None:
            kw['scale'] = scale
        S.op('act', lambda e: e.activation(out=out, in_=in_, func=func, **kw), r=r, w=w)

    def cp(eng, out, in_, r, w):
        if eng == 'act':
            S.op('act', lambda e: e.activation(out=out, in_=in_, func=AF.Copy), r=r, w=w)
        else:
            S.op(eng, lambda e: e.tensor_copy(out=out, in_=in_), r=r, w=w)

    def evq():
        st['ev'] += 1
        return 'act' if st['ev'] % 2 else 'dve'

    def tt(out, in0, in1, op, r, w, eng='dve'):
        S.op(eng, lambda e: e.tensor_tensor(out=out, in0=in0, in1=in1, op=op), r=r, w=w)

    def ts(out, in0, s1, s2, op0, op1, r, w):
        if op1 is None:
            S.op('dve', lambda e: e.tensor_scalar(out=out, in0=in0, scalar1=s1, scalar2=None, op0=op0), r=r, w=w)
        else:
            S.op('dve', lambda e: e.tensor_scalar(out=out, in0=in0, scalar1=s1, scalar2=s2, op0=op0, op1=op1), r=r, w=w)

    def stt(out, in0, scalar, in1, op0, op1, r, w):
        S.op('dve', lambda e: e.scalar_tensor_tensor(out=out, in0=in0, scalar=scalar, in1=in1, op0=op0, op1=op1), r=r, w=w)

    def recip(out, in_, r, w):
        S.op('dve', lambda e: e.reciprocal(out=out, in_=in_), r=r, w=w)

    def memset(eng, ap, val, w, r=()):
        S.op(eng, lambda e: e.memset(ap, val), r=r, w=w)

    def dma(q, out, in_, r, w, nonc=False):
        if nonc:
            S.dma(q, lambda e: e.dma_start(out=out, in_=in_, allow_slow_non_contiguous=True), r=r, w=w)
        else:
            S.dma(q, lambda e: e.dma_start(out=out, in_=in_), r=r, w=w)

    def load_w(W2, K, c0, ncol):
        KC = K // 128
        assert KC * ncol <= WBE
        bi = st['wb'] % NWB
        st['wb'] += 1
        view = wbuf[bi][:, 0:KC * ncol].rearrange("p (k n) -> p k n", k=KC)
        src = W2.rearrange("(k p) n -> p k n", p=128)[:, :, c0:c0 + ncol]
        S.dma('pool', lambda e: e.dma_start(out=view, in_=src), w=[('wb', bi)])
        return view, ('wb', bi)

    C = 'consts'

    dma('sp', vecs[:], vecs_d, r=[], w=[C])
    dma('sp', hv[:], hv_d, r=[], w=[C])
    memset('pool', ones1[:], 1.0, w=[C])
    memset('pool', rsA[:], 1.0, w=['rsA'])
    memset('pool', rsB[:], 0.0, w=['rsB'])
    memset('pool', onesb[:], 1.0, w=[C])
    S.op('pool', lambda e: e.affine_select(out=ident32[:], in_=rsA[:, 0:128], pattern=[[-1, 128]], compare_op=ALU.is_equal,
                                           fill=0.0, base=0, channel_multiplier=1), r=['rsA'], w=[C])
    cp('pool', identb[:], ident32[:], r=[C], w=[C])
    S.op('pool', lambda e: e.affine_select(out=m01c[:], in_=rsA[:, 0:128], pattern=[[1, 128]], compare_op=ALU.is_ge,
                                           fill=0.0, base=0, channel_multiplier=-1), r=['rsA'], w=[C])
    S.op('pool', lambda e: e.affine_select(out=mngc[:], in_=rsB[:, 0:128], pattern=[[1, 128]], compare_op=ALU.is_ge,
                                           fill=-30000.0, base=0, channel_multiplier=-1), r=['rsB'], w=[C])
    S.op('pool', lambda e: e.affine_select(out=m01bd[:].rearrange("p (a b) -> p a b", a=16), in_=m01c[:].rearrange("p (a b) -> p a b", a=16),
                                           pattern=[[-8, 16], [0, 8]], compare_op=ALU.is_ge,
                                           fill=0.0, base=0, channel_multiplier=1), r=[C], w=[C])
    S.op('pool', lambda e: e.affine_select(out=mngbd[:].rearrange("p (a b) -> p a b", a=16), in_=mngc[:].rearrange("p (a b) -> p a b", a=16),
                                           pattern=[[-8, 16], [0, 8]], compare_op=ALU.is_ge,
                                           fill=-30000.0, base=0, channel_multiplier=1), r=[C], w=[C])
    memset('pool', sctm[:], 0.0, w=['sctm'])
    for j3 in range(3):
        S.op('pool', lambda e, j3=j3: e.affine_select(out=sctm[32 * j3:32 * j3 + 32, 0:32], in_=rsA[32 * j3:32 * j3 + 32, 0:32],
                                                      pattern=[[-1, 32]], compare_op=ALU.is_equal, fill=0.0, base=0,
                                                      channel_multiplier=1), r=['rsA', 'sctm'], w=['sctm'])
    cp('pool', i3b[:], sctm[0:96, 0:32], r=['sctm'], w=[C])
    memset('pool', resetm[:], 1.0, w=[C])
    memset('pool', resetm[:].rearrange("p (a b) -> p a b", a=16)[:, :, 0:1], 0.0, w=[C], r=[C])
    S.op('pool', lambda e: e.affine_select(out=bmask[:], in_=rsA[:, 0:16], pattern=[[-8, 16]], compare_op=ALU.is_ge,
                                           fill=0.0, base=0, channel_multiplier=1), r=['rsA'], w=['bm0'])
    S.op('pool', lambda e: e.affine_select(out=bmask[:], in_=bmask[:], pattern=[[8, 16]], compare_op=ALU.is_ge,
                                           fill=0.0, base=7, channel_multiplier=-1), r=['bm0'], w=[C])
    for j in range(2):
        act(avec[:, j:j + 1], hv[:, 2 * j + 1:2 * j + 2], AF.Exp, r=[C], w=[('avec', j)])
        ts(avec[:, j:j + 1], avec[:, j:j + 1], -1.0, None, ALU.mult, None, r=[('avec', j)], w=[('avec', j)])

    for blk in range(17):
        stg = SV((blk % 2) * 4096, F32, [128, 1024])
        src = xp_d[blk * 128:(blk + 1) * 128, :] if blk < 16 else xs_d
        dma('sp', stg, src, r=[], w=[('S', 'stg', blk % 2)])
        t = min(blk // 4, 4)
        for half in range(2):
            pi = psn()
            for kk in range(4):
                trp(psum[pi][:, kk * 128:(kk + 1) * 128], stg[:, (half * 4 + kk) * 128:(half * 4 + kk + 1) * 128], ident32[:],
                    r=[('S', 'stg', blk % 2), C], w=[('ps', pi)])
            cp(evq(), xT[:, half * 4:half * 4 + 4, blk * 128:(blk + 1) * 128], psum[pi][:].rearrange("p (k c) -> p k c", k=4),
               r=[('ps', pi)], w=[('x', t, k) for k in range(half * 4, half * 4 + 4)])
    S.retire('S')

    def rms_stat(srcs, n, rkeys, nfeat):
        pi = psn()
        for k, (sap, rk) in enumerate(zip(srcs, rkeys)):
            q = st['sq'] % 2
            st['sq'] += 1
            act(sq2[q][:, :n], sap, AF.Square, r=(rk if isinstance(rk, list) else [rk]), w=[('sq', q)])
            mm(psum[pi][:, :n], onesb[:], sq2[q][:, :n], k == 0, k == len(srcs) - 1, r=[('sq', q), C], w=[('ps', pi)])
        act(rsA[:, :n], psum[pi][:, :n], AF.Sqrt, r=[('ps', pi)], w=['rsA'], bias=EPS, scale=1.0 / nfeat)
        recip(rsB[:, :n], rsA[:, :n], r=['rsA'], w=['rsB'])

    def rms_tile(t, gcol, dst_fn, wkeys_fn):
        c0, n = TILES[t]
        rms_stat([xT[:, k, c0:c0 + n] for k in range(8)], n, [('x', t, k) for k in range(8)], 1024.0)
        for k in range(8):
            stt(dst_fn(k), xT[:, k, c0:c0 + n], vecs[:, gcol + k:gcol + k + 1], rsB[:, :n], ALU.mult, ALU.mult,
                r=[('x', t, k), 'rsB', C], w=wkeys_fn(k))

    def norm_u(p, gcol):
        for t in PASSES[p]:
            c0, n = TILES[t]
            uc = c0 - PASS_COL0[p]
            rms_tile(t, gcol, lambda k, uc=uc, n=n: ub[:, k, uc:uc + n], lambda k, t=t: [('u', t, k)])

    def ffn(p, i, wg_d, wu_d, wd_d, gname):
        norm_u(p, VL[(gname, i)])
        tl = PASSES[p]
        h = SV(0, BF16, [128, 22, 1152])
        for un in range(11):
            c0 = un * 256
            wgb, gk = load_w(wg_d[i], 1024, c0, 256)
            wub, uk = load_w(wu_d[i], 1024, c0, 256)
            for cc in range(2):
                c = un * 2 + cc
                for t in tl:
                    col0, n = TILES[t]
                    uc = col0 - PASS_COL0[p]
                    pg = psn()
                    for k in range(8):
                        mm(psum[pg][:, :n], wgb[:, k, cc * 128:(cc + 1) * 128], ub[:, k, uc:uc + n], k == 0, k == 7,
                           r=[gk, ('u', t, k)], w=[('ps', pg)])
                    pu = psn()
                    for k in range(8):
                        mm(psum[pu][:, :n], wub[:, k, cc * 128:(cc + 1) * 128], ub[:, k, uc:uc + n], k == 0, k == 7,
                           r=[uk, ('u', t, k)], w=[('ps', pu)])
                    q = st['sq'] % 2
                    st['sq'] += 1
                    act(sq2[q][:, :n], psum[pg][:, :n], AF.Silu, r=[('ps', pg)], w=[('sq', q)])
                    tt(h[:, c, uc:uc + n], psum[pu][:, :n], sq2[q][:, :n], ALU.mult, r=[('ps', pu), ('sq', q)], w=[('S', 'h', c, t)])
        for o in range(8):
            wdh = [load_w(wd_d[i][0:1408, :], 1408, o * 128, 128), load_w(wd_d[i][1408:2816, :], 1408, o * 128, 128)]
            for t in tl:
                col0, n = TILES[t]
                uc = col0 - PASS_COL0[p]
                po = psn()
                for c in range(22):
                    wdb, dk = wdh[c // 11]
                    mm(psum[po][:, :n], wdb[:, c % 11, :], h[:, c, uc:uc + n], c == 0, c == 21, r=[dk, ('S', 'h', c, t)], w=[('ps', po)])
                stt(xT[:, o, col0:col0 + n], psum[po][:, :n], 0.5, xT[:, o, col0:col0 + n], ALU.mult, ALU.add,
                    r=[('ps', po), ('x', t, o)], w=[('x', t, o)])

    KT = MV(0, BF16, [128, 8, 256])
    Vb = MV(4096, BF16, [128, 2, 1024])

    def memkv(i):
        mem_tm = SV(0, F32, [128, 2, 1024])
        memT = SV(8192, F32, [128, 8, 256])
        mT = SV(16384, BF16, [128, 8, 256])
        ostg = [SV(20480 + 2048 * a, F32, [128, 512]) for a in range(2)]
        dma('sp', mem_tm, mem_d.rearrange("(c p) f -> p c f", p=128), r=[], w=[('S', 'memtm')])
        for mc in range(2):
            for half in range(2):
                pi = psn()
                for kk in range(4):
                    k = half * 4 + kk
                    trp(psum[pi][:, kk * 128:(kk + 1) * 128], mem_tm[:, mc, k * 128:(k + 1) * 128], ident32[:],
                        r=[('S', 'memtm'), C], w=[('ps', pi)])
                cp(evq(), memT[:, half * 4:half * 4 + 4, mc * 128:(mc + 1) * 128], psum[pi][:].rearrange("p (k c) -> p k c", k=4),
                   r=[('ps', pi)], w=[('S', 'memT', mc, half)])
        allk = [('S', 'memT', mc, half) for mc in range(2) for half in range(2)]
        rms_stat([memT[:, k, :] for k in range(8)], 256, [[('S', 'memT', 0, k // 4), ('S', 'memT', 1, k // 4)] for k in range(8)], 1024.0)
        gcol = VL[('nmem', i)]
        for k in range(8):
            stt(mT[:, k, :], memT[:, k, :], vecs[:, gcol + k:gcol + k + 1], rsB[:, :256], ALU.mult, ALU.mult,
                r=allk + ['rsB', C], w=[('S', 'mT', k)])
        mk = [('S', 'mT', k) for k in range(8)]
        oc = 0
        for un in range(4):
            wkb, kk_ = load_w(wk_d[i], 1024, un * 256, 256)
            for cc in range(2):
                o = un * 2 + cc
                pi = psn()
                for k in range(8):
                    mm(psum[pi][:, :256], wkb[:, k, cc * 128:(cc + 1) * 128], mT[:, k, :], k == 0, k == 7, r=[kk_, mk[k]], w=[('ps', pi)])
                cp(evq(), KT[:, o, :], psum[pi][:, :256], r=[('ps', pi)], w=[('M', 'KT', o)])
            for mc in range(2):
                pi = psn()
                for k in range(8):
                    mm(psum[pi][:, :256], mT[:, k, mc * 128:(mc + 1) * 128], wkb[:, k, :], k == 0, k == 7, r=[kk_, mk[k]], w=[('ps', pi)])
                a = oc % 2
                oc += 1
                cp(evq(), ostg[a][:, :256], psum[pi][:, :256], r=[('ps', pi)], w=[('S', 'ostg', a)])
                dma('act', mkp_o[i, mc * 128:(mc + 1) * 128, un * 256:(un + 1) * 256], ostg[a][:, :256], r=[('S', 'ostg', a)], w=[])
        for un in range(4):
            wvb, vk_ = load_w(wv_d[i], 1024, un * 256, 256)
            for mc in range(2):
                pi = psn()
                for k in range(8):
                    mm(psum[pi][:, :256], mT[:, k, mc * 128:(mc + 1) * 128], wvb[:, k, :], k == 0, k == 7, r=[vk_, mk[k]], w=[('ps', pi)])
                a = oc % 2
                oc += 1
                cp('act', ostg[a][:, :256], psum[pi][:, :256], r=[('ps', pi)], w=[('S', 'ostg', a)])
                cp('dve', Vb[:, mc, un * 256:(un + 1) * 256], psum[pi][:, :256], r=[('ps', pi)], w=[('M', 'Vb', mc, un)])
                dma('act', mvp_o[i, mc * 128:(mc + 1) * 128, un * 256:(un + 1) * 256], ostg[a][:, :256], r=[('S', 'ostg', a)], w=[])
        S.retire('S')

    def attn(p, i):
        norm_u(p, VL[('ncr', i)])
        tl = PASSES[p]
        qT = SV(0, BF16, [128, 8, 1152])
        oT = SV(18432, BF16, [128, 8, 1152])
        pT = SV(36864, BF16, [128, 2, 4, 512])
        KTk = [('M', 'KT', o) for o in range(8)]
        for un in range(4):
            wqb, qk = load_w(wq_d[i], 1024, un * 256, 256)
            for cc in range(2):
                o = un * 2 + cc
                for t in tl:
                    col0, n = TILES[t]
                    uc = col0 - PASS_COL0[p]
                    pi = psn()
                    for k in range(8):
                        mm(psum[pi][:, :n], wqb[:, k, cc * 128:(cc + 1) * 128], ub[:, k, uc:uc + n], k == 0, k == 7,
                           r=[qk, ('u', t, k)], w=[('ps', pi)])
                    cp(evq(), qT[:, o, uc:uc + n], psum[pi][:, :n], r=[('ps', pi)], w=[('S', 'q', o, t)])
        for t in tl:
            col0, n = TILES[t]
            uc = col0 - PASS_COL0[p]
            if t == 4:
                attn_sample(i, qT, oT, uc)
                continue
            for hh in range(4):
                for mc in range(2):
                    pi = psn()
                    for dc in range(2):
                        mm(psum[pi][:, :n], KT[:, 2 * hh + dc, mc * 128:(mc + 1) * 128], qT[:, 2 * hh + dc, uc:uc + n], dc == 0, dc == 1,
                           r=[KTk[2 * hh + dc], ('S', 'q', 2 * hh + dc, t)], w=[('ps', pi)])
                    act(pT[:, mc, hh, :n], psum[pi][:, :n], AF.Exp, r=[('ps', pi)], w=[('S', 'p', mc, hh)], scale=1.0 / 16.0)
                pd = psn()
                for mc in range(2):
                    mm(psum[pd][:, :n], onesb[:], pT[:, mc, hh, :n], mc == 0, mc == 1, r=[C, ('S', 'p', mc, hh)], w=[('ps', pd)])
                recip(rsB[:, :n], psum[pd][:, :n], r=[('ps', pd)], w=['rsB'])
                for dc in range(2):
                    po = psn()
                    for mc in range(2):
                        mm(psum[po][:, :n], Vb[:, mc, (2 * hh + dc) * 128:(2 * hh + dc + 1) * 128], pT[:, mc, hh, :n], mc == 0, mc == 1,
                           r=[('M', 'Vb', mc, (2 * hh + dc) // 2), ('S', 'p', mc, hh)], w=[('ps', po)])
                    tt(oT[:, 2 * hh + dc, uc:uc + n], psum[po][:, :n], rsB[:, :n], ALU.mult, r=[('ps', po), 'rsB'], w=[('S', 'o', 2 * hh + dc, t)])
        for un in range(4):
            wob, ok_ = load_w(wo_d[i], 1024, un * 256, 256)
            for cc in range(2):
                o = un * 2 + cc
                for t in tl:
                    col0, n = TILES[t]
                    uc = col0 - PASS_COL0[p]
                    pi = psn()
                    for k in range(8):
                        mm(psum[pi][:, :n], wob[:, k, cc * 128:(cc + 1) * 128], oT[:, k, uc:uc + n], k == 0, k == 7,
                           r=[ok_, ('S', 'o', k, t)], w=[('ps', pi)])
                    tt(xT[:, o, col0:col0 + n], psum[pi][:, :n], xT[:, o, col0:col0 + n], ALU.add, r=[('ps', pi), ('x', t, o)], w=[('x', t, o)])

    def attn_sample(i, qT, oT, uc):
        t = 4
        for b in range(16):
            Kc = RV((b % 2) * 4096, BF16, [128, 2, 1024])
            Vc = RV(8192 + (b % 2) * 4096, BF16, [128, 2, 1024])
            KcT = RV(16384, BF16, [128, 8, 256])
            S.dma('pool', lambda e, Kc=Kc, b=b: e.dma_start(out=Kc, in_=ck_d[i, b].rearrange("(c p) f -> p c f", p=128)), w=[('R', 'Kc', b % 2)])
            S.dma('pool', lambda e, Vc=Vc, b=b: e.dma_start(out=Vc, in_=cv_d[i, b].rearrange("(c p) f -> p c f", p=128)), w=[('R', 'Vc', b % 2)])
            for mc in range(2):
                pi = psn()
                psb = psum[pi][:].bitcast(BF16)
                for fc in range(8):
                    trp(psb[:, fc * 128:(fc + 1) * 128], Kc[:, mc, fc * 128:(fc + 1) * 128], identb[:], r=[('R', 'Kc', b % 2), C], w=[('ps', pi)])
                cp(evq(), KcT[:, :, mc * 128:(mc + 1) * 128], psb.rearrange("p (k c) -> p k c", k=8), r=[('ps', pi)], w=[('R', 'KcT', mc)])
            pss = psn()
            for hh in range(4):
                for mc in range(2):
                    sl = (mc * 4 + hh) * 8
                    for dc in range(2):
                        mm(psum[pss][:, sl:sl + 8], KcT[:, 2 * hh + dc, mc * 128:(mc + 1) * 128], qT[:, 2 * hh + dc, uc + 8 * b:uc + 8 * b + 8],
                           dc == 0, dc == 1, r=[('R', 'KcT', mc), ('S', 'q', 2 * hh + dc, t)], w=[('ps', pss)])
            act(pTs[:], psum[pss][:, 0:64], AF.Exp, r=[('ps', pss)], w=['pTs'], scale=1.0 / 16.0)
            pd = psn()
            for hh in range(4):
                for mc in range(2):
                    sl = (mc * 4 + hh) * 8
                    mm(psum[pd][:, hh * 8:hh * 8 + 8], onesb[:], pTs[:, sl:sl + 8], mc == 0, mc == 1, r=[C, 'pTs'], w=[('ps', pd)])
            recip(rds[:], psum[pd][:, 0:32], r=[('ps', pd)], w=['rds'])
            po = psn()
            for hh in range(4):
                for dc in range(2):
                    f = 2 * hh + dc
                    for mc in range(2):
                        sl = (mc * 4 + hh) * 8
                        mm(psum[po][:, f * 8:f * 8 + 8], Vc[:, mc, f * 128:(f + 1) * 128], pTs[:, sl:sl + 8], mc == 0, mc == 1,
                           r=[('R', 'Vc', b % 2), 'pTs'], w=[('ps', po)])
            tt(oT[:, :, uc + 8 * b:uc + 8 * b + 8].rearrange("p (h d) t -> p h d t", d=2),
               psum[po][:, 0:64].rearrange("p (h d t) -> p h d t", h=4, d=2),
               rds[:].rearrange("p (h t) -> p h t", h=4).unsqueeze(2).to_broadcast([128, 4, 2, 8]), ALU.mult,
               r=[('ps', po), 'rds'], w=[('S', 'o', k, t) for k in range(8)])

    def poolmix(p, i):
        j = i // 2
        tl = PASSES[p]
        gcol = VL[('nmix', i)]
        xx = SV(0, F32, [128, 8, 1039])
        xss = SV(33248, F32, [128, 8, 16, 23])
        tmp = [RV(4224 * a, F32, [128, 2, 527]) for a in range(2)]
        tmps = [RV(4224 * a, F32, [128, 2, 16, 23]) for a in range(2)]
        stg = RV(8448, F32, [128, 1024])
        ostg = RV(12544, F32, [128, 1024])
        if p == 0:
            memset('dve', xx[:, :, 0:15], 0.0, w=[('S', 'xxh')])
        else:
            cp('dve', xx[:, :, 0:15], xx[:, :, 1024:1039], r=[('S', 'xx', 1, k) for k in range(8)] + [('S', 'xxh')], w=[('S', 'xxh')])
        for t in tl:
            col0, n = TILES[t]
            uc = col0 - PASS_COL0[p]
            if t < 4:
                rms_tile(t, gcol, lambda k, uc=uc, n=n: xx[:, k, 15 + uc:15 + uc + n], lambda k, t=t: [('S', 'xx', t % 2, k)])
            else:
                for half in range(2):
                    dma('sp', stg[0:120, :], spl_d[j, half * 8:(half + 1) * 8].rearrange("b t f -> (b t) f"), r=[], w=[('R', 'stg')])
                    for h2 in range(2):
                        pi = psn()
                        for kk in range(4):
                            k = h2 * 4 + kk
                            trp(psum[pi][:, kk * 120:(kk + 1) * 120], stg[0:120, k * 128:(k + 1) * 128], ident32[0:120, 0:120],
                                r=[('R', 'stg'), C], w=[('ps', pi)])
                        for kk in range(4):
                            k = h2 * 4 + kk
                            cp(evq(), xss[:, k, half * 8:(half + 1) * 8, 0:15], psum[pi][:, kk * 120:(kk + 1) * 120].rearrange("p (b t) -> p b t", b=8),
                               r=[('ps', pi)], w=[('S', 'xsh', k)])
                c0, n = TILES[4]
                rms_stat([xT[:, k, c0:c0 + n] for k in range(8)], n, [('x', 4, k) for k in range(8)], 1024.0)
                for k in range(8):
                    stt(xss[:, k, :, 15:23], xT[:, k, c0:c0 + n].rearrange("p (b t) -> p b t", b=16), vecs[:, gcol + k:gcol + k + 1],
                        rsB[:, :n].rearrange("p (b t) -> p b t", b=16), ALU.mult, ALU.mult,
                        r=[('x', 4, k), 'rsB', C, ('S', 'xsh', k)], w=[('S', 'xs', k)])
        for t in tl:
            col0, n = TILES[t]
            uc = col0 - PASS_COL0[p]
            for g in range(4):
                w_ = 2 << g
                if t < 4:
                    src = xx[:, 2 * g:2 * g + 2, uc:uc + 15 + n]
                    rk = [('S', 'xx', t % 2, 2 * g), ('S', 'xx', t % 2, 2 * g + 1), ('S', 'xxh')]
                    if uc > 0:
                        rk += [('S', 'xx', (t + 1) % 2, 2 * g), ('S', 'xx', (t + 1) % 2, 2 * g + 1)]
                    cur = src
                    step = 1
                    a = 0
                    L = 15 + n
                    while step < w_:
                        dst = tmp[a][:, :, 0:L]
                        tt(dst[:, :, step:L], cur[:, :, step:L], cur[:, :, 0:L - step], ALU.add, r=rk + [('R', 'tmp', 1 - a)], w=[('R', 'tmp', a)])
                        cur = dst
                        a = 1 - a
                        step *= 2
                    la = 1 - a
                    stt(ub[:, 2 * g:2 * g + 2, uc:uc + n], cur[:, :, 15:15 + n], 1.0 / w_, src[:, :, 15:15 + n], ALU.mult, ALU.subtract,
                        r=rk + [('R', 'tmp', la)], w=[('u', t, 2 * g), ('u', t, 2 * g + 1)])
                    if t == 0:
                        for tc_ in range(w_ - 1):
                            stt(ub[:, 2 * g:2 * g + 2, tc_:tc_ + 1], cur[:, :, 15 + tc_:16 + tc_], 1.0 / (tc_ + 1), src[:, :, 15 + tc_:16 + tc_],
                                ALU.mult, ALU.subtract, r=rk + [('R', 'tmp', la)], w=[('u', t, 2 * g), ('u', t, 2 * g + 1)])
                else:
                    src = xss[:, 2 * g:2 * g + 2, :, :]
                    rk = [('S', 'xs', 2 * g), ('S', 'xs', 2 * g + 1)]
                    cur = src
                    step = 1
                    a = 0
                    while step < w_:
                        dst = tmps[a]
                        tt(dst[:, :, :, step:23], cur[:, :, :, step:23], cur[:, :, :, 0:23 - step], ALU.add, r=rk + [('R', 'tmp', 1 - a)], w=[('R', 'tmp', a)])
                        cur = dst
                        a = 1 - a
                        step *= 2
                    la = 1 - a
                    for kk in range(2):
                        stt(ub[:, 2 * g + kk, uc:uc + n].rearrange("p (b t) -> p b t", b=16), cur[:, kk, :, 15:23], 1.0 / w_, src[:, kk, :, 15:23],
                            ALU.mult, ALU.subtract, r=rk + [('R', 'tmp', la)], w=[('u', t, 2 * g + kk)])
        for g in range(4):
            pwb, pk = load_w(pw_d[j, g], 256, 0, 256)
            for oc in range(2):
                o = 2 * g + oc
                for t in tl:
                    col0, n = TILES[t]
                    uc = col0 - PASS_COL0[p]
                    pi = psn()
                    for k in range(2):
                        mm(psum[pi][:, :n], pwb[:, k, oc * 128:(oc + 1) * 128], ub[:, 2 * g + k, uc:uc + n], k == 0, k == 1,
                           r=[pk, ('u', t, 2 * g + k)], w=[('ps', pi)])
                    sc = VL[('psc', j)] + o
                    stt(xT[:, o, col0:col0 + n], psum[pi][:, :n], vecs[:, sc:sc + 1], xT[:, o, col0:col0 + n], ALU.mult, ALU.add,
                        r=[('ps', pi), ('x', t, o), C], w=[('x', t, o)])
        if p == 1:
            for half in range(2):
                pi = psn()
                for kk in range(4):
                    k = half * 4 + kk
                    trp(psum[pi][:, kk * 128:(kk + 1) * 128], xx[:, k, 15 + 896:15 + 1024], ident32[:], r=[('S', 'xx', 1, k), C], w=[('ps', pi)])
                cp(evq(), ostg[:, half * 512:(half + 1) * 512], psum[pi][:], r=[('ps', pi)], w=[('R', 'ostg')])
            dma('act', poolp_o[j], ostg[113:128, :], r=[('R', 'ostg')], w=[])
            ov = ostg.rearrange("p (k c) -> p k c", k=8)
            for k in range(8):
                cp(evq(), ov[:, k, :].rearrange("p (b t) -> p b t", b=16), xss[:, k, :, 15:23], r=[('S', 'xs', k), ('R', 'ostg')], w=[('R', 'ostg')])
            for half in range(2):
                pi = psn()
                for kk in range(4):
                    k = half * 4 + kk
                    trp(psum[pi][:, kk * 128:(kk + 1) * 128], ov[:, k, :], ident32[:], r=[('R', 'ostg'), C], w=[('ps', pi)])
                cp(evq(), stg[:, half * 512:(half + 1) * 512], psum[pi][:], r=[('ps', pi)], w=[('R', 'stg')])
            for b in range(16):
                dma('act', pools_o[j, b, 7:15, :], stg[8 * b:8 * b + 8, :], r=[('R', 'stg')], w=[])
            dma('act', pools_o[j, :, 0:7, :], spl_d[j, :, 8:15, :], r=[], w=[])

    def ssd(p, i):
        j = i // 2
        tl = PASSES[p]
        norm_u(p, VL[('nmix', i)])
        hT = RV(0, F32, [128, 2048])
        hTb = RV(8192, BF16, [128, 2048])
        cst = [RV(12288 + 2112 * a, F32, [128, 528]) for a in range(2)]
        convh = RV(16512, F32, [128, 24, 3])
        dec = [RV(16800 + 2048 * a, BF16, [128, 8, 128]) for a in range(2)]
        Eb = [RV(20896, BF16, [128, 8, 128]), MV(8192, BF16, [128, 8, 128])]
        cbm = [RV(22944 + 256 * a, BF16, [128, 128]) for a in range(2)]
        F1 = MV(0, F32, [128, 512])
        F2 = MV(2048, F32, [128, 512])
        F3 = MV(4096, F32, [128, 512])
        A3 = MV(6144, BF16, [128, 512])
        nA3 = MV(7168, BF16, [128, 512])
        cwc = VL[('cw', j)]
        cbc = VL[('cb', j)]
        dsk = VL[('dsk', j)]
        snw = VL[('snw', j)]
        if p == 0:
            memset('dve', convh, 0.0, w=[('R', 'convh', ci) for ci in range(24)])
            memset('dve', hT, 0.0, w=[('R', 'hT', g) for g in range(4)])
            memset('dve', hTb, 0.0, w=[('R', 'hTb', g) for g in range(4)])
            for fc in range(16):
                ts(Dd[:, fc, :], ident32[:], vecs[:, dsk + fc:dsk + fc + 1], None, ALU.mult, None, r=[C], w=[('Dd', fc)])

        def ssd_tile(t):
            col0, n = TILES[t]
            uc = col0 - PASS_COL0[p]
            samp = (t == 4)
            nb, bs = (16, 8) if samp else (1, n)
            nch = n // 128
            ukeys = [('u', t, k) for k in range(8)]
            if samp:
                zs = SV(0, BF16, [128, 16, 128])
                xbcT = SV(4096, BF16, [128, 24, 128])
                xdt = SV(10240, BF16, [128, 2048])
                xdte = SV(14336, BF16, [128, 2048])
                Btm = SV(18432, BF16, [128, 512])
            else:
                zs = SV(0, BF16, [128, 16, 512])
                xbcT = SV(16384, BF16, [128, 24, 512])
                xdt = SV(40960, BF16, [128, 2048])
                xdte = SV(45056, BF16, [128, 2048])
                Btm = SV(49152, BF16, [128, 512])
            if samp:
                hS = RV(4096, F32, [128, 24, 48])
                sc48 = RV(12288, F32, [128, 1056])
                for q4 in range(3):
                    dma('sp', sc48[0:48, 0:1024], scv_d[j, :, q4 * 1024:(q4 + 1) * 1024], r=[], w=[('R', 'cst', 0), ('R', 'cst', 1)])
                    for h2 in range(2):
                        pi = psn()
                        for kk in range(4):
                            trp(psum[pi][:, kk * 48:(kk + 1) * 48], sc48[0:48, (h2 * 4 + kk) * 128:(h2 * 4 + kk + 1) * 128], ident32[0:48, 0:48],
                                r=[('R', 'cst', 0), C], w=[('ps', pi)])
                        cp(evq(), hS[:, q4 * 8 + h2 * 4:q4 * 8 + h2 * 4 + 4, :], psum[pi][:, 0:192].rearrange("p (k c) -> p k c", k=4),
                           r=[('ps', pi)], w=[('R', 'hS', q4 * 2 + h2)])
            wdt, dk_ = load_w(inw_d[j], 1024, 5120, 32)
            pdt = psn()
            for r3 in range(3):
                for k in range(8):
                    mm(psum[pdt][32 * r3:32 * r3 + 32, :n], wdt[:, k, :], ub[:, k, uc:uc + n], k == 0, k == 7, r=[dk_, ukeys[k]], w=[('ps', pdt)])
            act(F1[0:96, :n], psum[pdt][0:96, :n], AF.Exp, r=[('ps', pdt), C], w=[('M', 'F1')], bias=hv[:, 2 * j:2 * j + 1])
            act(F2[0:96, :n], F1[0:96, :n], AF.Ln, r=[('M', 'F1')], w=[('M', 'F2')], bias=1.0)
            ts(F1[0:96, :n], F2[0:96, :n], avec[:, j:j + 1], None, ALU.mult, None, r=[('M', 'F2'), ('avec', j)], w=[('M', 'F1')])
            for c in range(nch):
                cs = slice(c * 128, (c + 1) * 128)
                d0 = resetm[:, :] if samp else ones1[0:96, 0:1].to_broadcast([96, 128])
                S.op('dve', lambda e, cs=cs, d0=d0: e.tensor_tensor_scan(out=F3[0:96, cs], data0=d0, data1=F1[0:96, cs], initial=0.0,
                                                                       op0=ALU.mult, op1=ALU.add), r=[('M', 'F1'), C], w=[('M', 'F3')])
            cp('dve', A3[0:96, :n], F3[0:96, :n], r=[('M', 'F3')], w=[('M', 'A3')])
            tt(F1[0:96, :n], F3[0:96, :n], A3[0:96, :n], ALU.subtract, r=[('M', 'F3'), ('M', 'A3')], w=[('M', 'F1')])
            cp('dve', A3[32:64, :n], F1[32:64, :n], r=[('M', 'F1')], w=[('M', 'A3')])
            cp('dve', A3[64:96, :n], F1[64:96, :n], r=[('M', 'F1')], w=[('M', 'A3')])
            tt(F1[64:96, :n], F1[64:96, :n], A3[64:96, :n], ALU.subtract, r=[('M', 'F1'), ('M', 'A3')], w=[('M', 'F1')])
            cp('dve', A3[64:96, :n], F1[64:96, :n], r=[('M', 'F1')], w=[('M', 'A3')])
            ts(nA3[0:96, :n], A3[0:96, :n], -1.0, None, ALU.mult, None, r=[('M', 'A3')], w=[('M', 'nA3')])
            for un in range(20):
                wib, ik = load_w(inw_d[j], 1024, un * 256, 256)
                for cc in range(2):
                    fc = un * 2 + cc
                    pi = psn()
                    for k in range(8):
                        mm(psum[pi][:, :n], wib[:, k, cc * 128:(cc + 1) * 128], ub[:, k, uc:uc + n], k == 0, k == 7, r=[ik, ukeys[k]], w=[('ps', pi)])
                    if fc < 16:
                        act(zs[:, fc, :n], psum[pi][:, :n], AF.Silu, r=[('ps', pi)], w=[('S', 'zs', fc)])
                    else:
                        ci = fc - 16
                        a = ci % 2
                        cv = cst[a][:, 0:nb * (3 + bs)].rearrange("p (b s) -> p b s", b=nb)
                        if samp:
                            cp('dve', cv[:, :, 0:3], hS[:, ci, :].rearrange("p (b s) -> p b s", b=16), r=[('R', 'hS', ci // 4)], w=[('R', 'cst', a, 'h')])
                        else:
                            cp('dve', cv[:, :, 0:3], convh[:, ci, :].unsqueeze(1), r=[('R', 'convh', ci)], w=[('R', 'cst', a, 'h')])
                        psv = psum[pi][:, :n].rearrange("p (b s) -> p b s", b=nb)
                        cp('act', cv[:, :, 3:3 + bs], psv, r=[('ps', pi)], w=[('R', 'cst', a)])
                        accb, acck = (rsA, 'rsA') if a == 0 else (rsB, 'rsB')
                        accv = accb[:, :n].rearrange("p (b s) -> p b s", b=nb)
                        act(accv, psv, AF.Identity, r=[('ps', pi), C], w=[acck], bias=vecs[:, cbc + ci:cbc + ci + 1],
                            scale=vecs[:, cwc + ci * 4 + 3:cwc + ci * 4 + 4])
                        for tap in range(3):
                            stt(accv, cv[:, :, tap:tap + bs], vecs[:, cwc + ci * 4 + tap:cwc + ci * 4 + tap + 1], accv, ALU.mult, ALU.add,
                                r=[('R', 'cst', a), ('R', 'cst', a, 'h'), acck, C], w=[acck])
                        act(xbcT[:, ci, :n], accb[:, :n], AF.Silu, r=[acck], w=[('S', 'xbc', ci)])
                        if samp:
                            cp('dve', hS[:, ci, :].rearrange("p (b s) -> p b s", b=16), cv[:, :, bs:bs + 3], r=[('R', 'cst', a)],
                               w=[('R', 'hS2', ci), ('R', 'hS', ci // 4)])
                        else:
                            cp('dve', convh[:, ci, :].unsqueeze(1), cv[:, :, bs:bs + 3], r=[('R', 'cst', a)], w=[('R', 'convh', ci)])
            m01 = m01bd if samp else m01c
            mng = mngbd if samp else mngc
            for c in range(nch):
                cs = slice(c * 128, (c + 1) * 128)
                first = (t == 0 and c == 0)
                cbs = 8 if samp else 128
                cp('act', STt[0:32, :], F2[0:32, cs], r=[('M', 'F2')], w=['STt0'])
                a3v = F3[32:64, cs].rearrange("p (b s) -> p b s", b=nb)
                tt(STt[32:64, :].rearrange("p (b s) -> p b s", b=nb), a3v[:, :, cbs - 1:cbs].to_broadcast([32, nb, cbs]), a3v,
                   ALU.subtract, r=[('M', 'F3'), 'STt1'], w=['STt1'])
                act(STt[32:64, :], STt[32:64, :], AF.Exp, r=['STt1'], w=['STt1'])
                tt(STt[32:64, :], STt[32:64, :], F2[32:64, cs], ALU.mult, r=['STt1', ('M', 'F2')], w=['STt1'])
                pi = psn()
                trp(psum[pi][:, 0:64], STt[:, :], ident32[0:64, 0:64], r=['STt0', 'STt1', C], w=[('ps', pi)])
                cp('act', sctm[:, :], psum[pi][:, 0:64], r=[('ps', pi)], w=['sctm'])
                px = [psn(), psn(), psn()]
                for ci in range(20):
                    pb = psum[px[ci // 8]][:].bitcast(BF16)
                    trp(pb[:, (ci % 8) * 128:(ci % 8 + 1) * 128], xbcT[:, ci, cs], identb[:], r=[('S', 'xbc', ci), C], w=[('ps', px[ci // 8])])
                for hf in range(2):
                    pb = psum[px[hf]][:].bitcast(BF16).rearrange("p (h d) -> p h d", h=16)
                    tt(xdt[:, hf * 1024:(hf + 1) * 1024].rearrange("p (h d) -> p h d", h=16), pb,
                       sctm[:, hf * 16:(hf + 1) * 16].unsqueeze(2).to_broadcast([128, 16, 64]), ALU.mult,
                       r=[('ps', px[hf]), 'sctm'], w=[('S', 'xdt', hf)])
                    tt(xdte[:, hf * 1024:(hf + 1) * 1024].rearrange("p (h d) -> p h d", h=16), pb,
                       sctm[:, 32 + hf * 16:32 + (hf + 1) * 16].unsqueeze(2).to_broadcast([128, 16, 64]), ALU.mult,
                       r=[('ps', px[hf]), 'sctm'], w=[('S', 'xdte', hf)])
                cp('act', Btm[:, :], psum[px[2]][:].bitcast(BF16)[:, 0:512], r=[('ps', px[2])], w=[('S', 'Btm')])

                def stageA(g):
                    a = g % 2
                    pc = psn()
                    mm(psum[pc][:, 0:128], xbcT[:, 16 + g, cs], xbcT[:, 20 + g, cs], True, True, r=[('S', 'xbc', 16 + g), ('S', 'xbc', 20 + g)], w=[('ps', pc)])
                    tt(cbm[a][:, :], psum[pc][:, 0:128], m01[:], ALU.mult, r=[('ps', pc), C], w=[('R', 'cbm', a)])
                    pB = [psn(), psn()]
                    pE = [psn(), psn()]
                    for ih in range(8):
                        hh = 8 * g + ih
                        sel = i3b[:, hh:hh + 1].to_broadcast([96, 128])
                        ob = psum[pB[ih // 4]][:, (ih % 4) * 128:(ih % 4 + 1) * 128]
                        mm(ob, sel, A3[0:96, cs], True, False, r=[C, ('M', 'A3')], w=[('ps', pB[ih // 4])])
                        mm(ob, nA3[0:96, cs], sel, False, False, r=[C, ('M', 'nA3')], w=[('ps', pB[ih // 4])])
                        mm(ob, identb[:], mng[:], False, True, r=[C], w=[('ps', pB[ih // 4])])
                        oe = psum[pE[ih // 4]][:, (ih % 4) * 128:(ih % 4 + 1) * 128]
                        mm(oe, sel, A3[0:96, cs], True, True, r=[C, ('M', 'A3')], w=[('ps', pE[ih // 4])])
                    for q in range(2):
                        act(dec[a][:, q * 4:(q + 1) * 4, :], psum[pB[q]][:].rearrange("p (h c) -> p h c", h=4), AF.Exp, r=[('ps', pB[q])], w=[('R', 'dec', a, q)])
                        act(Eb[a][:, q * 4:(q + 1) * 4, :], psum[pE[q]][:].rearrange("p (h c) -> p h c", h=4), AF.Exp, r=[('ps', pE[q])], w=[('R', 'Eb', a, q)])
                        if samp:
                            act(cdE[:, 8 * g + 4 * q:8 * g + 4 * q + 4, :], psum[pE[q]][:].rearrange("p (h b s) -> p h b s", h=4, b=16)[:, :, :, 7], AF.Exp,
                                r=[('ps', pE[q])], w=[('cdE', g, q)])
                        else:
                            act(cdv[a][:, q * 4:(q + 1) * 4], psum[pE[q]][:].rearrange("p (h c) -> p h c", h=4)[:, :, 127], AF.Exp,
                                r=[('ps', pE[q])], w=[('cdv', a, q)])
                    tt(dec[a][:, :, :], dec[a][:, :, :], cbm[a][:, :].unsqueeze(1).to_broadcast([128, 8, 128]), ALU.mult,
                       r=[('R', 'dec', a, 0), ('R', 'dec', a, 1), ('R', 'cbm', a)], w=[('R', 'dec', a, 0), ('R', 'dec', a, 1)])
                    if not first:
                        tt(Eb[a][:, :, :], Eb[a][:, :, :], xbcT[:, 20 + g, cs].unsqueeze(1).to_broadcast([128, 8, 128]), ALU.mult,
                           r=[('R', 'Eb', a, 0), ('R', 'Eb', a, 1), ('S', 'xbc', 20 + g)], w=[('R', 'Eb', a, 0), ('R', 'Eb', a, 1)])

                def stageB(g):
                    a = g % 2
                    dk2 = [('R', 'dec', a, 0), ('R', 'dec', a, 1)]
                    ek2 = [('R', 'Eb', a, 0), ('R', 'Eb', a, 1)]
                    py = psn()
                    for hp in range(4):
                        fc = 4 * g + hp
                        mm(psum[py][:, hp * 128:(hp + 1) * 128], Dd[:, fc, :], xbcT[:, fc, cs], hp == 0, False, r=[('Dd', fc), ('S', 'xbc', fc)], w=[('ps', py)])
                        for sd in range(2):
                            hh = 8 * g + 2 * hp + sd
                            oy = psum[py][64 * sd:64 * sd + 64, hp * 128:(hp + 1) * 128]
                            lastd = (first or samp)
                            mm(oy, xdt[:, hh * 64:(hh + 1) * 64], dec[a][:, 2 * hp + sd, :], False, lastd and not samp and sd == 1,
                               r=[('S', 'xdt', hh // 16)] + dk2, w=[('ps', py)])
                            if not lastd:
                                mm(oy, hTb[:, hh * 64:(hh + 1) * 64], Eb[a][:, 2 * hp + sd, :], False, sd == 1, r=[('R', 'hTb', g)] + ek2, w=[('ps', py)])
                    if samp:
                        st['resv'].add(py)
                        ssd_sample_group(j, g, py, Eb[a], ek2, xdte, Btm)
                        st['resv'].discard(py)
                    tt(zs[:, 4 * g:4 * g + 4, cs], psum[py][:, :].rearrange("p (f c) -> p f c", f=4), zs[:, 4 * g:4 * g + 4, cs], ALU.mult,
                       r=[('ps', py)] + [('S', 'zs', 4 * g + q) for q in range(4)], w=[('S', 'zs', 4 * g + q) for q in range(4)])
                    if not samp:
                        pst = psn()
                        mm(psum[pst][:, :], Btm[:, g * 128:(g + 1) * 128], xdte[:, g * 512:(g + 1) * 512], True, True,
                           r=[('S', 'Btm'), ('S', 'xdte', g // 2)], w=[('ps', pst)])
                        hv_ = hT[:, g * 512:(g + 1) * 512].rearrange("p (h d) -> p h d", h=8)
                        if not first:
                            tt(hv_, hv_, cdv[a][:, :].unsqueeze(2).to_broadcast([128, 8, 64]), ALU.mult,
                               r=[('R', 'hT', g), ('cdv', a, 0), ('cdv', a, 1)], w=[('R', 'hT', g)])
                        tt(hT[:, g * 512:(g + 1) * 512], hT[:, g * 512:(g + 1) * 512], psum[pst][:, :], ALU.add, r=[('R', 'hT', g), ('ps', pst)], w=[('R', 'hT', g)])
                        cp('act', hTb[:, g * 512:(g + 1) * 512], hT[:, g * 512:(g + 1) * 512], r=[('R', 'hT', g)], w=[('R', 'hTb', g)])

                stageA(0)
                for g in range(4):
                    if g + 1 < 4:
                        stageA(g + 1)
                    stageB(g)
            for g in range(4):
                rms_stat([zs[:, 4 * g + q, :n] for q in range(4)], n, [('S', 'zs', 4 * g + q) for q in range(4)], 512.0)
                for q in range(4):
                    fc = 4 * g + q
                    stt(zs[:, fc, :n], zs[:, fc, :n], vecs[:, snw + fc:snw + fc + 1], rsB[:, :n], ALU.mult, ALU.mult,
                        r=[('S', 'zs', fc), 'rsB', C], w=[('S', 'zs', fc)])
            for un in range(8):
                wob, ok_ = load_w(outw_d[j], 2048, un * 128, 128)
                pi = psn()
                for kc in range(16):
                    mm(psum[pi][:, :n], wob[:, kc, :], zs[:, kc, :n], kc == 0, kc == 15, r=[ok_, ('S', 'zs', kc)], w=[('ps', pi)])
                tt(xT[:, un, col0:col0 + n], psum[pi][:, :n], xT[:, un, col0:col0 + n], ALU.add, r=[('ps', pi), ('x', t, un)], w=[('x', t, un)])
            if samp:
                for q4 in range(4):
                    osg = RV(12288, F32, [128, 768])
                    for half in range(2):
                        pi = psn()
                        for kk in range(3):
                            ci = q4 * 6 + half * 3 + kk
                            trp(psum[pi][0:48, kk * 128:(kk + 1) * 128], hS[:, ci, :], ident32[:], r=[('R', 'hS2', ci), C], w=[('ps', pi)])
                        cp(evq(), osg[0:48, half * 384:(half + 1) * 384], psum[pi][0:48, 0:384], r=[('ps', pi)], w=[('R', 'cst', 0), ('R', 'cst', 1)])
                    dma('act', convs_o[j, :, q4 * 768:(q4 + 1) * 768], osg[0:48, :], r=[('R', 'cst', 0)], w=[])

        def prompt_state_out():
            ost = RV(12288, F32, [128, 1056])
            for q4 in range(4):
                pi = psn()
                for kk in range(4):
                    f = q4 * 4 + kk
                    trp(psum[pi][:, kk * 128:(kk + 1) * 128], hT[:, f * 128:(f + 1) * 128], ident32[:], r=[('R', 'hT', f // 4), C], w=[('ps', pi)])
                cp(evq(), ost[:, 0:512], psum[pi][:, :], r=[('ps', pi)], w=[('R', 'cst', 0), ('R', 'cst', 1)])
                for kk in range(4):
                    f = q4 * 4 + kk
                    dma('act', ssmp_o[j, f * 128:(f + 1) * 128, :], ost[:, kk * 128:(kk + 1) * 128], r=[('R', 'cst', 0)], w=[])
            for q4 in range(4):
                for half in range(2):
                    pi = psn()
                    for kk in range(3):
                        ci = q4 * 6 + half * 3 + kk
                        trp(psum[pi][0:3, kk * 128:(kk + 1) * 128], convh[:, ci, :], ident32[:], r=[('R', 'convh', ci), C], w=[('ps', pi)])
                    cp(evq(), ost[0:3, half * 384:(half + 1) * 384], psum[pi][0:3, 0:384], r=[('ps', pi)], w=[('R', 'cst', 0), ('R', 'cst', 1)])
                dma('act', convp_o[j, :, q4 * 768:(q4 + 1) * 768], ost[0:3, 0:768], r=[('R', 'cst', 0)], w=[])

        for t in tl:
            if t == 4:
                S.retire('S')
            ssd_tile(t)
            if t == 3:
                prompt_state_out()
                S.retire('R')

    def ssd_sample_group(j, g, py, Ceb, ek2, xdte, Btm):
        h0 = [SV(19456 + 2048 * a, F32, [128, 4, 128]) for a in range(3)]
        h0T = [SV(25600 + 1024 * a, BF16, [128, 512]) for a in range(2)]
        Bm = [SV(27648 + 256 * a, BF16, [128, 128]) for a in range(2)]
        nst = [SV(28160 + 2048 * a, F32, [128, 4, 128]) for a in range(3)]
        cdP = SV(34304, F32, [128, 4, 16])
        for sd in range(2):
            sl = slice(64 * sd, 64 * sd + 64)
            cp('dve', cdP[sl, :, :], cdE[sl, 8 * g:8 * g + 8, :].rearrange("p (f s) b -> p f s b", s=2)[:, :, sd, :],
               r=[('cdE', g, 0), ('cdE', g, 1)], w=[('S', 'cdP')])

        def load(b):
            dma('sp', h0[b % 3], sst_d[j, b, g * 512:(g + 1) * 512, :].rearrange("(f p) n -> p f n", p=128), r=[], w=[('S', 'h0', b % 3)])

        def stT(b):
            a = b % 2
            pi = psn()
            for f in range(4):
                trp(psum[pi][:, f * 128:(f + 1) * 128], h0[b % 3][:, f, :], ident32[:], r=[('S', 'h0', b % 3), C], w=[('ps', pi)])
            cp('act', h0T[a][:, :], psum[pi][:, :], r=[('ps', pi)], w=[('S', 'h0T', a)])
            ts(Bm[a][:, :], Btm[:, g * 128:(g + 1) * 128], bmask[:, b:b + 1], None, ALU.mult, None, r=[('S', 'Btm'), C], w=[('S', 'Bm', a)])

        def stC(b):
            a = b % 2
            for hp in range(4):
                for sd in range(2):
                    oy = psum[py][64 * sd:64 * sd + 64, hp * 128 + 8 * b:hp * 128 + 8 * b + 8]
                    mm(oy, h0T[a][:, hp * 128 + 64 * sd:hp * 128 + 64 * sd + 64], Ceb[:, 2 * hp + sd, 8 * b:8 * b + 8], False, (b == 15),
                       r=[('S', 'h0T', a)] + ek2, w=[('ps', py)])
            pst = psn()
            for f in range(4):
                mm(psum[pst][:, f * 128:(f + 1) * 128], xdte[:, (4 * g + f) * 128:(4 * g + f + 1) * 128], Bm[a][:, :], f == 0, f == 3,
                   r=[('S', 'xdte', g // 2), ('S', 'Bm', a)], w=[('ps', pst)])
            for f in range(4):
                stt(nst[b % 3][:, f, :], h0[b % 3][:, f, :], cdP[:, f, b:b + 1], psum[pst][:, f * 128:(f + 1) * 128], ALU.mult, ALU.add,
                    r=[('S', 'h0', b % 3), ('S', 'cdP'), ('ps', pst)], w=[('S', 'nst', b % 3, f)])
            dma('act', ssms_o[j, b, g * 512:(g + 1) * 512, :].rearrange("(f p) n -> p f n", p=128), nst[b % 3],
                r=[('S', 'nst', b % 3, f) for f in range(4)], w=[])

        load(0)
        load(1)
        for b in range(16):
            stT(b)
            if b >= 1:
                stC(b - 1)
            if b + 2 < 16:
                load(b + 2)
        stC(15)

    for i in range(NL):
        for sub in SUBS:
            if sub == 'ffn1':
                for p in range(2):
                    ffn(p, i, wg1_d, wu1_d, wd1_d, 'nf1')
            elif sub == 'ffn2':
                for p in range(2):
                    ffn(p, i, wg2_d, wu2_d, wd2_d, 'nf2')
            elif sub == 'mix':
                for p in range(2):
                    if i % 2 == 0:
                        ssd(p, i)
                    else:
                        poolmix(p, i)
            elif sub == 'attn':
                memkv(i)
                for p in range(2):
                    attn(p, i)
            S.retire('S')
            S.retire('R')
            S.retire('M')

    gcol = VL['fin']
    for t in range(5):
        c0, n = TILES[t]
        yn = SV(0, F32, [128, 8, 512])
        rms_tile(t, gcol, lambda k, n=n: yn[:, k, :n], lambda k: [('S', 'yn', k)])
        for blk in range(n // 128):
            ostg = SV(16384 + ((st['ev'] // 2) % 2) * 4096, F32, [128, 1024])
            a = (st['ev'] // 2) % 2
            for half in range(2):
                pi = psn()
                for kk in range(4):
                    k = half * 4 + kk
                    trp(psum[pi][:, kk * 128:(kk + 1) * 128], yn[:, k, blk * 128:(blk + 1) * 128], ident32[:], r=[('S', 'yn', k), C], w=[('ps', pi)])
                cp('act' if half else 'dve', ostg[:, half * 512:(half + 1) * 512], psum[pi][:], r=[('ps', pi)], w=[('S', 'ostg', a, half)])
            st['ev'] += 2
            dst = yp_o[c0 + blk * 128:c0 + (blk + 1) * 128, :] if t < 4 else ys_o
            dma('act', dst, ostg, r=[('S', 'ostg', a, 0), ('S', 'ostg', a, 1)], w=[])

    S.finish()
    S.emit(nc)
    es.close()
    return nc


_NC_CACHE = {}


def _fm(v):
    v = np.asarray(v, dtype=np.float32)
    return np.ascontiguousarray(v.reshape(-1, 128).T)


def pack_vecs(inp):
    vecs = np.zeros((128, NV), np.float32)
    for i in range(4):
        for nm, key in (('nf1', 'norm_ffn1'), ('nmix', 'norm_mix'), ('ncr', 'norm_cross'), ('nmem', 'norm_mem'), ('nf2', 'norm_ffn2')):
            c = VL[(nm, i)]
            vecs[:, c:c + 8] = _fm(inp[key][i])
    c = VL['fin']
    vecs[:, c:c + 8] = _fm(inp['final_norm'])
    for j in range(2):
        c = VL[('psc', j)]
        vecs[:, c:c + 8] = _fm(inp['pool_scale'][j])
        c = VL[('snw', j)]
        vecs[:, c:c + 16] = _fm(inp['ssd_norm_w'][j])
        c = VL[('cw', j)]
        cw = np.asarray(inp['ssd_conv_w'][j], np.float32).reshape(4, 24, 128)
        vecs[:, c:c + 96] = np.ascontiguousarray(cw.transpose(2, 1, 0)).reshape(128, 96)
        c = VL[('cb', j)]
        vecs[:, c:c + 24] = _fm(inp['ssd_conv_b'][j])
        c = VL[('dsk', j)]
        vecs[:, c:c + 16] = _fm(np.repeat(np.asarray(inp['ssd_d'][j], np.float32), 64))
    hv = np.zeros((96, 4), np.float32)
    for j in range(2):
        hv[:, 2 * j] = np.tile(np.asarray(inp['ssd_dt_bias'][j], np.float32), 3)
        hv[:, 2 * j + 1] = np.tile(np.asarray(inp['ssd_a_log'][j], np.float32), 3)
    return vecs, hv


def make_in_maps(inp):
    f = lambda a: np.ascontiguousarray(np.asarray(a, dtype=np.float32))
    vecs, hv = pack_vecs(inp)
    shared = {
        "vecs": vecs, "hv": hv,
        "wg1": f(inp['ffn1_w_gate']), "wu1": f(inp['ffn1_w_up']), "wd1": f(inp['ffn1_w_down']),
        "wg2": f(inp['ffn2_w_gate']), "wu2": f(inp['ffn2_w_up']), "wd2": f(inp['ffn2_w_down']),
        "inw": f(inp['ssd_in_w']), "outw": f(inp['ssd_out_w']), "pw": f(inp['pool_w']),
        "wq": f(inp['xa_wq']), "wk": f(inp['xa_wk']), "wv": f(inp['xa_wv']), "wo": f(inp['xa_wo']),
    }
    maps = []
    for c in range(8):
        sl = slice(16 * c, 16 * c + 16)
        m = dict(shared)
        m["xp"] = f(inp['x_prompt'][c])
        m["xs"] = f(np.asarray(inp['x_sample'])[sl].reshape(128, 1024))
        m["mem"] = f(inp['mem_prompt'][c])
        m["ck"] = f(np.asarray(inp['cache_mem_k'])[:, sl].reshape(4, 16, 256, 1024))
        m["cv"] = f(np.asarray(inp['cache_mem_v'])[:, sl].reshape(4, 16, 256, 1024))
        m["sst"] = f(np.asarray(inp['state_ssm'])[:, sl].reshape(2, 16, 2048, 128))
        m["scv"] = f(np.asarray(inp['state_conv'])[:, sl].reshape(2, 48, 3072))
        m["spl"] = f(np.asarray(inp['state_pool'])[:, sl])
        maps.append(m)
    return maps


def assemble(results):
    R = results
    y_p = np.stack([R[c]["y_p"] for c in range(8)], 0)
    y_s = np.concatenate([R[c]["y_s"].reshape(16, 8, 1024) for c in range(8)], 0)
    ssm_p = np.stack([R[c]["ssm_p"].reshape(2, 32, 64, 128) for c in range(8)], 1)
    conv_p = np.stack([R[c]["conv_p"] for c in range(8)], 1)
    pool_p = np.stack([R[c]["pool_p"] for c in range(8)], 1)
    mk_p = np.stack([R[c]["mk_p"].reshape(4, 256, 4, 256) for c in range(8)], 1)
    mv_p = np.stack([R[c]["mv_p"].reshape(4, 256, 4, 256) for c in range(8)], 1)
    ssm_s = np.concatenate([R[c]["ssm_s"].reshape(2, 16, 32, 64, 128) for c in range(8)], 1)
    conv_s = np.concatenate([R[c]["conv_s"].reshape(2, 16, 3, 3072) for c in range(8)], 1)
    pool_s = np.concatenate([R[c]["pool_s"] for c in range(8)], 1)
    outs = (y_p, y_s, ssm_p, conv_p, pool_p, mk_p, mv_p, ssm_s, conv_s, pool_s)
    return tuple(np.ascontiguousarray(o.astype(np.float32, copy=False)) for o in outs)


def kernel(**inputs):
    if 'nc' not in _NC_CACHE:
        _NC_CACHE['nc'] = build()
    nc = _NC_CACHE['nc']
    in_maps = make_in_maps(inputs)
    res = run_bass_kernel_spmd(nc, in_maps, core_ids=list(range(8)))
    return assemble(res.results)
```

```python
import numpy as np
from contextlib import ExitStack
import concourse.bass as bass
import concourse.mybir as mybir
from concourse.bass_utils import run_bass_kernel_spmd

F32 = mybir.dt.float32
BF16 = mybir.dt.bfloat16
AF = mybir.ActivationFunctionType
ALU = mybir.AluOpType

EPOCH = 30000
NSLOT = 8
EPS = 1e-5

D = 1024
DFF = 2816
T = 2176
TILES = [(0, 512), (512, 512), (1024, 512), (1536, 512), (2048, 128)]
PASSES = [[0, 1], [2, 3, 4]]
PASS_COL0 = [0, 1024]
NWB = 5
WBE = 2048


class Sched:
    def __init__(self):
        self.engs = ['pe', 'act', 'dve', 'pool', 'sp']
        self.stream = {e: [] for e in self.engs}
        self.cnt = {e: 0 for e in self.engs}
        self.res = {}
        self.waited = {e: {} for e in self.engs}
        self.dmacnt = {e: 0 for e in self.engs}
        self.semkeys = {}
        self.fence = {}

    def _semkey(self, k):
        self.semkeys.setdefault(k, None)
        return k

    def _need(self, eng, tok, waits, skip_self=False):
        if tok[0] == 'e':
            _, e, idx = tok
            if e == eng and (eng == 'pe' or skip_self):
                return
            key = ('e', e)
            if self.waited[eng].get(key, -1) >= idx:
                return
            self.waited[eng][key] = idx
            waits[key] = (self._semkey(('e', e, idx // EPOCH)), idx % EPOCH + 1)
        else:
            _, q, k = tok
            key = ('d', q, k % NSLOT)
            if self.waited[eng].get(key, -1) >= k:
                return
            self.waited[eng][key] = k
            waits[key] = (self._semkey(('d', q, k % NSLOT)), 16 * (k // NSLOT + 1))

    def _deps(self, eng, r, w):
        waits = {}
        for key in list(r) + list(w):
            if isinstance(key, tuple) and key[0] in self.fence and key not in self.res:
                for t in self.fence[key[0]].values():
                    self._need(eng, t, waits)
        for key in r:
            st = self.res.get(key)
            if st and st['w'] is not None:
                self._need(eng, st['w'], waits)
        for key in w:
            st = self.res.get(key)
            if st:
                if st['w'] is not None:
                    self._need(eng, st['w'], waits, True)
                for t in st['r'].values():
                    self._need(eng, t, waits, True)
        return waits

    def _mark(self, tok, r, w):
        for key in r:
            st = self.res.setdefault(key, {'w': None, 'r': {}})
            if tok[0] == 'e':
                st['r'][('e', tok[1])] = tok
            else:
                st['r'][tok] = tok
        for key in w:
            self.res[key] = {'w': tok, 'r': {}}

    def op(self, eng, fn, r=(), w=()):
        waits = self._deps(eng, r, w)
        for (sk, val) in waits.values():
            self.stream[eng].append(('wait', sk, val))
        idx = self.cnt[eng]
        self.cnt[eng] += 1
        self.stream[eng].append(('op', fn, self._semkey(('e', eng, idx // EPOCH))))
        self._mark(('e', eng, idx), r, w)

    def dma(self, q, fn, r=(), w=()):
        waits = self._deps(q, r, w)
        k = self.dmacnt[q]
        self.dmacnt[q] += 1
        if k >= NSLOT:
            self._need(q, ('d', q, k - NSLOT), waits)
        for (sk, val) in waits.values():
            self.stream[q].append(('wait', sk, val))
        self.stream[q].append(('dma', fn, self._semkey(('d', q, k % NSLOT))))
        self._mark(('d', q, k), r, w)

    def retire(self, region):
        toks = dict(self.fence.get(region, {}))
        for key in list(self.res):
            if isinstance(key, tuple) and key[0] == region:
                st = self.res.pop(key)
                for t in ([st['w']] if st['w'] is not None else []) + list(st['r'].values()):
                    if t[0] == 'e':
                        k = ('e', t[1])
                        if k not in toks or toks[k][2] < t[2]:
                            toks[k] = t
                    else:
                        toks[t] = t
        self.fence[region] = toks

    def finish(self):
        for q in self.engs:
            n = self.dmacnt[q]
            for k in range(max(0, n - NSLOT), n):
                waits = {}
                self._need(q, ('d', q, k), waits)
                for (sk, val) in waits.values():
                    self.stream[q].append(('wait', sk, val))

    def emit(self, nc):
        with ExitStack() as es:
            sems = {}
            for i, k in enumerate(self.semkeys):
                sems[k] = es.enter_context(nc.semaphore("s%d" % i))
            block = es.enter_context(nc.Block())

            def run(e, eng):
                for it in self.stream[e]:
                    if it[0] == 'wait':
                        eng.wait_ge(sems[it[1]], it[2])
                    elif it[0] == 'op':
                        it[1](eng).then_inc(sems[it[2]], 1)
                    else:
                        it[1](eng).then_inc(sems[it[2]], 16)

            @block.tensor
            def _(eng):
                run('pe', eng)

            @block.scalar
            def _(eng):
                run('act', eng)

            @block.vector
            def _(eng):
                run('dve', eng)

            @block.gpsimd
            def _(eng):
                run('pool', eng)

            @block.sync
            def _(eng):
                run('sp', eng)


def vec_layout():
    lay = {}
    c = 0
    for i in range(4):
        for nm in ('nf1', 'nmix', 'ncr', 'nmem', 'nf2'):
            lay[(nm, i)] = c
            c += 8
    lay['fin'] = c
    c += 8
    for j in range(2):
        lay[('psc', j)] = c
        c += 8
        lay[('snw', j)] = c
        c += 16
        lay[('cw', j)] = c
        c += 96
        lay[('cb', j)] = c
        c += 24
        lay[('dsk', j)] = c
        c += 16
    return lay, c


VL, NV = vec_layout()


def build(cfg=None):
    cfg = cfg or {}
    NL = cfg.get('nlayers', 4)
    SUBS = cfg.get('subs', ('ffn1', 'mix', 'attn', 'ffn2'))
    nc = bass.Bass("TRN2", target_bir_lowering=False)
    S = Sched()

    def din(name, shape):
        return nc.dram_tensor(name, shape, F32, kind="ExternalInput").ap()

    def dout(name, shape):
        return nc.dram_tensor(name, shape, F32, kind="ExternalOutput").ap()

    xp_d = din("xp", [2048, 1024])
    xs_d = din("xs", [128, 1024])
    mem_d = din("mem", [256, 1024])
    ck_d = din("ck", [4, 16, 256, 1024])
    cv_d = din("cv", [4, 16, 256, 1024])
    sst_d = din("sst", [2, 16, 2048, 128])
    scv_d = din("scv", [2, 48, 3072])
    spl_d = din("spl", [2, 16, 15, 1024])
    vecs_d = din("vecs", [128, NV])
    hv_d = din("hv", [96, 4])
    wg1_d = din("wg1", [4, 1024, 2816])
    wu1_d = din("wu1", [4, 1024, 2816])
    wd1_d = din("wd1", [4, 2816, 1024])
    wg2_d = din("wg2", [4, 1024, 2816])
    wu2_d = din("wu2", [4, 1024, 2816])
    wd2_d = din("wd2", [4, 2816, 1024])
    inw_d = din("inw", [2, 1024, 5152])
    outw_d = din("outw", [2, 2048, 1024])
    pw_d = din("pw", [2, 4, 256, 256])
    wq_d = din("wq", [4, 1024, 1024])
    wk_d = din("wk", [4, 1024, 1024])
    wv_d = din("wv", [4, 1024, 1024])
    wo_d = din("wo", [4, 1024, 1024])

    yp_o = dout("y_p", [2048, 1024])
    ys_o = dout("y_s", [128, 1024])
    ssmp_o = dout("ssm_p", [2, 2048, 128])
    convp_o = dout("conv_p", [2, 3, 3072])
    poolp_o = dout("pool_p", [2, 15, 1024])
    mkp_o = dout("mk_p", [4, 256, 1024])
    mvp_o = dout("mv_p", [4, 256, 1024])
    ssms_o = dout("ssm_s", [2, 16, 2048, 128])
    convs_o = dout("conv_s", [2, 48, 3072])
    pools_o = dout("pool_s", [2, 16, 15, 1024])

    es = ExitStack()

    def sb(name, shape, dt):
        return es.enter_context(nc.sbuf_tensor("s_" + name, shape, dt))

    xT = sb("xT", [128, 8, T], F32)
    ub = sb("ub", [128, 8, 1152], BF16)
    Sreg = sb("Sreg", [128, 25344], BF16)
    Rreg = sb("Rreg", [128, 5888], F32)
    Mreg = sb("Mreg", [128, 2560], F32)
    wbuf = [sb("wb%d" % i, [128, WBE], BF16) for i in range(NWB)]
    vecs = sb("vecs", [128, NV], F32)
    hv = sb("hv", [96, 4], F32)
    avec = sb("avec", [96, 2], F32)
    ident32 = sb("ident32", [128, 128], F32)
    identb = sb("identb", [128, 128], BF16)
    onesb = sb("onesb", [128, 128], BF16)
    ones1 = sb("ones1", [128, 1], F32)
    m01c = sb("m01c", [128, 128], BF16)
    m01bd = sb("m01bd", [128, 128], BF16)
    mngc = sb("mngc", [128, 128], BF16)
    mngbd = sb("mngbd", [128, 128], BF16)
    i3b = sb("i3b", [96, 32], BF16)
    resetm = sb("resetm", [96, 128], F32)
    bmask = sb("bmask", [128, 16], F32)
    sq2 = [sb("sq%d" % i, [128, 512], BF16) for i in range(2)]
    rsA = sb("rsA", [128, 512], F32)
    rsB = sb("rsB", [128, 512], F32)
    sctm = sb("sctm", [128, 64], F32)
    STt = sb("STt", [64, 128], F32)
    cdv = [sb("cdv%d" % i, [128, 8], F32) for i in range(2)]
    Dd = sb("Dd", [128, 16, 128], BF16)
    cdE = sb("cdE", [128, 32, 16], F32)
    pTs2 = [sb("pTs%d" % i, [128, 64], BF16) for i in range(2)]
    rds2 = [sb("rds%d" % i, [128, 32], F32) for i in range(2)]
    psum = [es.enter_context(nc.psum_tensor("ps%d" % i, [128, 512], F32)) for i in range(8)]

    st = {'ps': 0, 'wb': 0, 'sq': 0, 'ev': 0, 'resv': set()}

    def psn():
        while True:
            i = st['ps'] % 8
            st['ps'] += 1
            if i not in st['resv']:
                return i

    def carve(reg, esz, off, dt, shape):
        n = 1
        for s_ in shape[1:]:
            n *= s_
        dsz = 4 if dt == F32 else 2
        nbytes = n * dsz
        a = reg[:, off // esz:(off + nbytes) // esz]
        if dsz != esz:
            a = a.bitcast(dt)
        if len(shape) == 3:
            a = a.rearrange("p (a b) -> p a b", a=shape[1])
        elif len(shape) == 4:
            a = a.rearrange("p (a b c) -> p a b c", a=shape[1], b=shape[2])
        return a

    def SV(off, dt, shape):
        return carve(Sreg, 2, off, dt, shape)

    def RV(off, dt, shape):
        return carve(Rreg, 4, off, dt, shape)

    def MV(off, dt, shape):
        return carve(Mreg, 4, off, dt, shape)

    def mm(out, lhsT, rhs, start, stop, r, w):
        S.op('pe', lambda e: e.matmul(out, lhsT=lhsT, rhs=rhs, start=start, stop=stop), r=r, w=w)

    def trp(out, in_, ident, r, w):
        S.op('pe', lambda e: e.transpose(out, in_, ident), r=r, w=w)

    def act(out, in_, func, r, w, bias=None, scale=None):
        kw = {}
        if bias is not None:
            kw['bias'] = bias
        if scale is not None:
            kw['scale'] = scale
        S.op('act', lambda e: e.activation(out=out, in_=in_, func=func, **kw), r=r, w=w)

    def cp(eng, out, in_, r, w):
        if eng == 'act':
            S.op('act', lambda e: e.activation(out=out, in_=in_, func=AF.Copy), r=r, w=w)
        else:
            S.op(eng, lambda e: e.tensor_copy(out=out, in_=in_), r=r, w=w)

    def evq():
        st['ev'] += 1
        return 'act' if st['ev'] % 2 else 'dve'

    def tt(out, in0, in1, op, r, w, eng='dve'):
        S.op(eng, lambda e: e.tensor_tensor(out=out, in0=in0, in1=in1, op=op), r=r, w=w)

    def ts(out, in0, s1, s2, op0, op1, r, w):
        if op1 is None:
            S.op('dve', lambda e: e.tensor_scalar(out=out, in0=in0, scalar1=s1, scalar2=None, op0=op0), r=r, w=w)
        else:
            S.op('dve', lambda e: e.tensor_scalar(out=out, in0=in0, scalar1=s1, scalar2=s2, op0=op0, op1=op1), r=r, w=w)

    def stt(out, in0, scalar, in1, op0, op1, r, w):
        S.op('dve', lambda e: e.scalar_tensor_tensor(out=out, in0=in0, scalar=scalar, in1=in1, op0=op0, op1=op1), r=r, w=w)

    def recip(out, in_, r, w):
        S.op('dve', lambda e: e.reciprocal(out=out, in_=in_), r=r, w=w)

    def memset(eng, ap, val, w, r=()):
        S.op(eng, lambda e: e.memset(ap, val), r=r, w=w)

    def dma(q, out, in_, r, w, nonc=False):
        if nonc:
            S.dma(q, lambda e: e.dma_start(out=out, in_=in_, allow_slow_non_contiguous=True), r=r, w=w)
        else:
            S.dma(q, lambda e: e.dma_start(out=out, in_=in_), r=r, w=w)

    def load_w(W2, K, c0, ncol):
        KC = K // 128
        assert KC * ncol <= WBE
        bi = st['wb'] % NWB
        st['wb'] += 1
        view = wbuf[bi][:, 0:KC * ncol].rearrange("p (k n) -> p k n", k=KC)
        src = W2.rearrange("(k p) n -> p k n", p=128)[:, :, c0:c0 + ncol]
        S.dma('pool', lambda e: e.dma_start(out=view, in_=src), w=[('wb', bi)])
        return view, ('wb', bi)

    C = 'consts'

    dma('sp', vecs[:], vecs_d, r=[], w=[C])
    dma('sp', hv[:], hv_d, r=[], w=[C])
    memset('pool', ones1[:], 1.0, w=[C])
    memset('pool', rsA[:], 1.0, w=['rsA'])
    memset('pool', rsB[:], 0.0, w=['rsB'])
    memset('pool', onesb[:], 1.0, w=[C])
    S.op('pool', lambda e: e.affine_select(out=ident32[:], in_=rsA[:, 0:128], pattern=[[-1, 128]], compare_op=ALU.is_equal,
                                           fill=0.0, base=0, channel_multiplier=1), r=['rsA'], w=[C])
    cp('pool', identb[:], ident32[:], r=[C], w=[C])
    S.op('pool', lambda e: e.affine_select(out=m01c[:], in_=rsA[:, 0:128], pattern=[[1, 128]], compare_op=ALU.is_ge,
                                           fill=0.0, base=0, channel_multiplier=-1), r=['rsA'], w=[C])
    S.op('pool', lambda e: e.affine_select(out=mngc[:], in_=rsB[:, 0:128], pattern=[[1, 128]], compare_op=ALU.is_ge,
                                           fill=-30000.0, base=0, channel_multiplier=-1), r=['rsB'], w=[C])
    S.op('pool', lambda e: e.affine_select(out=m01bd[:].rearrange("p (a b) -> p a b", a=16), in_=m01c[:].rearrange("p (a b) -> p a b", a=16),
                                           pattern=[[-8, 16], [0, 8]], compare_op=ALU.is_ge,
                                           fill=0.0, base=0, channel_multiplier=1), r=[C], w=[C])
    S.op('pool', lambda e: e.affine_select(out=mngbd[:].rearrange("p (a b) -> p a b", a=16), in_=mngc[:].rearrange("p (a b) -> p a b", a=16),
                                           pattern=[[-8, 16], [0, 8]], compare_op=ALU.is_ge,
                                           fill=-30000.0, base=0, channel_multiplier=1), r=[C], w=[C])
    memset('pool', sctm[:], 0.0, w=['sctm'])
    for j3 in range(3):
        S.op('pool', lambda e, j3=j3: e.affine_select(out=sctm[32 * j3:32 * j3 + 32, 0:32], in_=rsA[32 * j3:32 * j3 + 32, 0:32],
                                                      pattern=[[-1, 32]], compare_op=ALU.is_equal, fill=0.0, base=0,
                                                      channel_multiplier=1), r=['rsA', 'sctm'], w=['sctm'])
    cp('pool', i3b[:], sctm[0:96, 0:32], r=['sctm'], w=[C])
    memset('pool', resetm[:], 1.0, w=[C])
    memset('pool', resetm[:].rearrange("p (a b) -> p a b", a=16)[:, :, 0:1], 0.0, w=[C], r=[C])
    S.op('pool', lambda e: e.affine_select(out=bmask[:], in_=rsA[:, 0:16], pattern=[[-8, 16]], compare_op=ALU.is_ge,
                                           fill=0.0, base=0, channel_multiplier=1), r=['rsA'], w=['bm0'])
    S.op('pool', lambda e: e.affine_select(out=bmask[:], in_=bmask[:], pattern=[[8, 16]], compare_op=ALU.is_ge,
                                           fill=0.0, base=7, channel_multiplier=-1), r=['bm0'], w=[C])
    for j in range(2):
        act(avec[:, j:j + 1], hv[:, 2 * j + 1:2 * j + 2], AF.Exp, r=[C], w=[('avec', j)])
        ts(avec[:, j:j + 1], avec[:, j:j + 1], -1.0, None, ALU.mult, None, r=[('avec', j)], w=[('avec', j)])

    for blk in range(17):
        stg = SV((blk % 2) * 4096, F32, [128, 1024])
        src = xp_d[blk * 128:(blk + 1) * 128, :] if blk < 16 else xs_d
        dma('sp', stg, src, r=[], w=[('S', 'stg', blk % 2)])
        t = min(blk // 4, 4)
        for half in range(2):
            pi = psn()
            for kk in range(4):
                trp(psum[pi][:, kk * 128:(kk + 1) * 128], stg[:, (half * 4 + kk) * 128:(half * 4 + kk + 1) * 128], ident32[:],
                    r=[('S', 'stg', blk % 2), C], w=[('ps', pi)])
            cp(evq(), xT[:, half * 4:half * 4 + 4, blk * 128:(blk + 1) * 128], psum[pi][:].rearrange("p (k c) -> p k c", k=4),
               r=[('ps', pi)], w=[('x', t, k) for k in range(half * 4, half * 4 + 4)])
    S.retire('S')

    def rms_stat(srcs, n, rkeys, nfeat):
        pi = psn()
        for k, (sap, rk) in enumerate(zip(srcs, rkeys)):
            q = st['sq'] % 2
            st['sq'] += 1
            act(sq2[q][:, :n], sap, AF.Square, r=(rk if isinstance(rk, list) else [rk]), w=[('sq', q)])
            mm(psum[pi][:, :n], onesb[:], sq2[q][:, :n], k == 0, k == len(srcs) - 1, r=[('sq', q), C], w=[('ps', pi)])
        act(rsA[:, :n], psum[pi][:, :n], AF.Sqrt, r=[('ps', pi)], w=['rsA'], bias=EPS, scale=1.0 / nfeat)
        recip(rsB[:, :n], rsA[:, :n], r=['rsA'], w=['rsB'])

    def rms_tile(t, gcol, dst_fn, wkeys_fn):
        c0, n = TILES[t]
        rms_stat([xT[:, k, c0:c0 + n] for k in range(8)], n, [('x', t, k) for k in range(8)], 1024.0)
        for k in range(8):
            stt(dst_fn(k), xT[:, k, c0:c0 + n], vecs[:, gcol + k:gcol + k + 1], rsB[:, :n], ALU.mult, ALU.mult,
                r=[('x', t, k), 'rsB', C], w=wkeys_fn(k))

    def norm_u(p, gcol):
        for t in PASSES[p]:
            c0, n = TILES[t]
            uc = c0 - PASS_COL0[p]
            rms_tile(t, gcol, lambda k, uc=uc, n=n: ub[:, k, uc:uc + n], lambda k, t=t: [('u', t, k)])

    def ffn(p, i, wg_d, wu_d, wd_d, gname):
        norm_u(p, VL[(gname, i)])
        tl = PASSES[p]
        h = SV(0, BF16, [128, 22, 1152])
        for un in range(11):
            c0 = un * 256
            wgb, gk = load_w(wg_d[i], 1024, c0, 256)
            wub, uk = load_w(wu_d[i], 1024, c0, 256)
            for cc in range(2):
                c = un * 2 + cc
                for t in tl:
                    col0, n = TILES[t]
                    uc = col0 - PASS_COL0[p]
                    pg = psn()
                    for k in range(8):
                        mm(psum[pg][:, :n], wgb[:, k, cc * 128:(cc + 1) * 128], ub[:, k, uc:uc + n], k == 0, k == 7,
                           r=[gk, ('u', t, k)], w=[('ps', pg)])
                    pu = psn()
                    for k in range(8):
                        mm(psum[pu][:, :n], wub[:, k, cc * 128:(cc + 1) * 128], ub[:, k, uc:uc + n], k == 0, k == 7,
                           r=[uk, ('u', t, k)], w=[('ps', pu)])
                    q = st['sq'] % 2
                    st['sq'] += 1
                    act(sq2[q][:, :n], psum[pg][:, :n], AF.Silu, r=[('ps', pg)], w=[('sq', q)])
                    tt(h[:, c, uc:uc + n], psum[pu][:, :n], sq2[q][:, :n], ALU.mult, r=[('ps', pu), ('sq', q)], w=[('S', 'h', c, t)])
        for o in range(8):
            wdh = [load_w(wd_d[i][0:1408, :], 1408, o * 128, 128), load_w(wd_d[i][1408:2816, :], 1408, o * 128, 128)]
            for t in tl:
                col0, n = TILES[t]
                uc = col0 - PASS_COL0[p]
                po = psn()
                for c in range(22):
                    wdb, dk = wdh[c // 11]
                    mm(psum[po][:, :n], wdb[:, c % 11, :], h[:, c, uc:uc + n], c == 0, c == 21, r=[dk, ('S', 'h', c, t)], w=[('ps', po)])
                stt(xT[:, o, col0:col0 + n], psum[po][:, :n], 0.5, xT[:, o, col0:col0 + n], ALU.mult, ALU.add,
                    r=[('ps', po), ('x', t, o)], w=[('x', t, o)])

    KT = MV(0, BF16, [128, 8, 256])
    Vb = MV(4096, BF16, [128, 2, 1024])

    def memkv(i):
        mem_tm = SV(0, F32, [128, 2, 1024])
        memT = SV(8192, F32, [128, 8, 256])
        mT = SV(16384, BF16, [128, 8, 256])
        ostg = [SV(20480 + 2048 * a, F32, [128, 512]) for a in range(2)]
        dma('sp', mem_tm, mem_d.rearrange("(c p) f -> p c f", p=128), r=[], w=[('S', 'memtm')])
        for mc in range(2):
            for half in range(2):
                pi = psn()
                for kk in range(4):
                    k = half * 4 + kk
                    trp(psum[pi][:, kk * 128:(kk + 1) * 128], mem_tm[:, mc, k * 128:(k + 1) * 128], ident32[:],
                        r=[('S', 'memtm'), C], w=[('ps', pi)])
                cp(evq(), memT[:, half * 4:half * 4 + 4, mc * 128:(mc + 1) * 128], psum[pi][:].rearrange("p (k c) -> p k c", k=4),
                   r=[('ps', pi)], w=[('S', 'memT', mc, half)])
        allk = [('S', 'memT', mc, half) for mc in range(2) for half in range(2)]
        rms_stat([memT[:, k, :] for k in range(8)], 256, [[('S', 'memT', 0, k // 4), ('S', 'memT', 1, k // 4)] for k in range(8)], 1024.0)
        gcol = VL[('nmem', i)]
        for k in range(8):
            stt(mT[:, k, :], memT[:, k, :], vecs[:, gcol + k:gcol + k + 1], rsB[:, :256], ALU.mult, ALU.mult,
                r=allk + ['rsB', C], w=[('S', 'mT', k)])
        mk = [('S', 'mT', k) for k in range(8)]
        oc = 0
        for un in range(4):
            wkb, kk_ = load_w(wk_d[i], 1024, un * 256, 256)
            for cc in range(2):
                o = un * 2 + cc
                pi = psn()
                for k in range(8):
                    mm(psum[pi][:, :256], wkb[:, k, cc * 128:(cc + 1) * 128], mT[:, k, :], k == 0, k == 7, r=[kk_, mk[k]], w=[('ps', pi)])
                cp(evq(), KT[:, o, :], psum[pi][:, :256], r=[('ps', pi)], w=[('M', 'KT', o)])
            for mc in range(2):
                pi = psn()
                for k in range(8):
                    mm(psum[pi][:, :256], mT[:, k, mc * 128:(mc + 1) * 128], wkb[:, k, :], k == 0, k == 7, r=[kk_, mk[k]], w=[('ps', pi)])
                a = oc % 2
                oc += 1
                cp(evq(), ostg[a][:, :256], psum[pi][:, :256], r=[('ps', pi)], w=[('S', 'ostg', a)])
                dma('act', mkp_o[i, mc * 128:(mc + 1) * 128, un * 256:(un + 1) * 256], ostg[a][:, :256], r=[('S', 'ostg', a)], w=[])
        for un in range(4):
            wvb, vk_ = load_w(wv_d[i], 1024, un * 256, 256)
            for mc in range(2):
                pi = psn()
                for k in range(8):
                    mm(psum[pi][:, :256], mT[:, k, mc * 128:(mc + 1) * 128], wvb[:, k, :], k == 0, k == 7, r=[vk_, mk[k]], w=[('ps', pi)])
                a = oc % 2
                oc += 1
                cp('act', ostg[a][:, :256], psum[pi][:, :256], r=[('ps', pi)], w=[('S', 'ostg', a)])
                cp('dve', Vb[:, mc, un * 256:(un + 1) * 256], psum[pi][:, :256], r=[('ps', pi)], w=[('M', 'Vb', mc, un)])
                dma('act', mvp_o[i, mc * 128:(mc + 1) * 128, un * 256:(un + 1) * 256], ostg[a][:, :256], r=[('S', 'ostg', a)], w=[])
        S.retire('S')

    def attn(p, i):
        norm_u(p, VL[('ncr', i)])
        tl = PASSES[p]
        qT = SV(0, BF16, [128, 8, 1152])
        oT = SV(18432, BF16, [128, 8, 1152])
        pT = SV(36864, BF16, [128, 2, 4, 512])
        KTk = [('M', 'KT', o) for o in range(8)]
        for un in range(4):
            wqb, qk = load_w(wq_d[i], 1024, un * 256, 256)
            for cc in range(2):
                o = un * 2 + cc
                for t in tl:
                    col0, n = TILES[t]
                    uc = col0 - PASS_COL0[p]
                    pi = psn()
                    for k in range(8):
                        mm(psum[pi][:, :n], wqb[:, k, cc * 128:(cc + 1) * 128], ub[:, k, uc:uc + n], k == 0, k == 7,
                           r=[qk, ('u', t, k)], w=[('ps', pi)])
                    cp(evq(), qT[:, o, uc:uc + n], psum[pi][:, :n], r=[('ps', pi)], w=[('S', 'q', o, t)])
        for t in tl:
            col0, n = TILES[t]
            uc = col0 - PASS_COL0[p]
            if t == 4:
                attn_sample(i, qT, oT, uc)
                continue
            def att_S(hh, t=t, n=n, uc=uc):
                for mc in range(2):
                    pi = psn()
                    for dc in range(2):
                        mm(psum[pi][:, :n], KT[:, 2 * hh + dc, mc * 128:(mc + 1) * 128], qT[:, 2 * hh + dc, uc:uc + n], dc == 0, dc == 1,
                           r=[KTk[2 * hh + dc], ('S', 'q', 2 * hh + dc, t)], w=[('ps', pi)])
                    act(pT[:, mc, hh, :n], psum[pi][:, :n], AF.Exp, r=[('ps', pi)], w=[('S', 'p', mc, hh)], scale=1.0 / 16.0)

            def att_F(hh, t=t, n=n, uc=uc):
                rb, rk = (rsA, 'rsA') if hh % 2 == 0 else (rsB, 'rsB')
                pd = psn()
                for mc in range(2):
                    mm(psum[pd][:, :n], onesb[:], pT[:, mc, hh, :n], mc == 0, mc == 1, r=[C, ('S', 'p', mc, hh)], w=[('ps', pd)])
                recip(rb[:, :n], psum[pd][:, :n], r=[('ps', pd)], w=[rk])
                for dc in range(2):
                    po = psn()
                    for mc in range(2):
                        mm(psum[po][:, :n], Vb[:, mc, (2 * hh + dc) * 128:(2 * hh + dc + 1) * 128], pT[:, mc, hh, :n], mc == 0, mc == 1,
                           r=[('M', 'Vb', mc, (2 * hh + dc) // 2), ('S', 'p', mc, hh)], w=[('ps', po)])
                    tt(oT[:, 2 * hh + dc, uc:uc + n], psum[po][:, :n], rb[:, :n], ALU.mult, r=[('ps', po), rk], w=[('S', 'o', 2 * hh + dc, t)])

            att_S(0)
            for hh in range(4):
                if hh + 1 < 4:
                    att_S(hh + 1)
                att_F(hh)
        for un in range(4):
            wob, ok_ = load_w(wo_d[i], 1024, un * 256, 256)
            for cc in range(2):
                o = un * 2 + cc
                for t in tl:
                    col0, n = TILES[t]
                    uc = col0 - PASS_COL0[p]
                    pi = psn()
                    for k in range(8):
                        mm(psum[pi][:, :n], wob[:, k, cc * 128:(cc + 1) * 128], oT[:, k, uc:uc + n], k == 0, k == 7,
                           r=[ok_, ('S', 'o', k, t)], w=[('ps', pi)])
                    tt(xT[:, o, col0:col0 + n], psum[pi][:, :n], xT[:, o, col0:col0 + n], ALU.add, r=[('ps', pi), ('x', t, o)], w=[('x', t, o)])

    def attn_sample(i, qT, oT, uc):
        t = 4
        Kcb = [RV(a * 4096, BF16, [128, 2, 1024]) for a in range(2)]
        Vcb = [RV(8192 + a * 4096, BF16, [128, 2, 1024]) for a in range(2)]
        KcTb = [RV(16384, BF16, [128, 8, 256]), SV(45056, BF16, [128, 8, 256])]

        def load(b):
            a = b % 2
            S.dma('pool', lambda e: e.dma_start(out=Kcb[a], in_=ck_d[i, b].rearrange("(c p) f -> p c f", p=128)), w=[('R', 'Kc', a)])
            S.dma('pool', lambda e: e.dma_start(out=Vcb[a], in_=cv_d[i, b].rearrange("(c p) f -> p c f", p=128)), w=[('R', 'Vc', a)])

        def stT(b):
            a = b % 2
            for mc in range(2):
                pi = psn()
                psb = psum[pi][:].bitcast(BF16)
                for fc in range(8):
                    trp(psb[:, fc * 128:(fc + 1) * 128], Kcb[a][:, mc, fc * 128:(fc + 1) * 128], identb[:], r=[('R', 'Kc', a), C], w=[('ps', pi)])
                cp(evq(), KcTb[a][:, :, mc * 128:(mc + 1) * 128], psb.rearrange("p (k c) -> p k c", k=8), r=[('ps', pi)], w=[('S', 'KcT', a, mc)])

        def stC(b):
            a = b % 2
            KcT = KcTb[a]
            Vc = Vcb[a]
            pss = psn()
            for hh in range(4):
                for mc in range(2):
                    sl = (mc * 4 + hh) * 8
                    for dc in range(2):
                        mm(psum[pss][:, sl:sl + 8], KcT[:, 2 * hh + dc, mc * 128:(mc + 1) * 128], qT[:, 2 * hh + dc, uc + 8 * b:uc + 8 * b + 8],
                           dc == 0, dc == 1, r=[('S', 'KcT', a, mc), ('S', 'q', 2 * hh + dc, t)], w=[('ps', pss)])
            act(pTs2[a][:], psum[pss][:, 0:64], AF.Exp, r=[('ps', pss)], w=[('pTs', a)], scale=1.0 / 16.0)
            pd = psn()
            for hh in range(4):
                for mc in range(2):
                    sl = (mc * 4 + hh) * 8
                    mm(psum[pd][:, hh * 8:hh * 8 + 8], onesb[:], pTs2[a][:, sl:sl + 8], mc == 0, mc == 1, r=[C, ('pTs', a)], w=[('ps', pd)])
            recip(rds2[a][:], psum[pd][:, 0:32], r=[('ps', pd)], w=[('rds', a)])
            po = psn()
            for hh in range(4):
                for dc in range(2):
                    f = 2 * hh + dc
                    for mc in range(2):
                        sl = (mc * 4 + hh) * 8
                        mm(psum[po][:, f * 8:f * 8 + 8], Vc[:, mc, f * 128:(f + 1) * 128], pTs2[a][:, sl:sl + 8], mc == 0, mc == 1,
                           r=[('R', 'Vc', a), ('pTs', a)], w=[('ps', po)])
            tt(oT[:, :, uc + 8 * b:uc + 8 * b + 8].rearrange("p (h d) t -> p h d t", d=2),
               psum[po][:, 0:64].rearrange("p (h d t) -> p h d t", h=4, d=2),
               rds2[a][:].rearrange("p (h t) -> p h t", h=4).unsqueeze(2).to_broadcast([128, 4, 2, 8]), ALU.mult,
               r=[('ps', po), ('rds', a)], w=[('S', 'o', k, t) for k in range(8)])

        load(0)
        load(1)
        stT(0)
        for b in range(16):
            if b + 1 < 16:
                stT(b + 1)
            stC(b)
            if b + 2 < 16:
                load(b + 2)

    def poolmix(p, i):
        j = i // 2
        tl = PASSES[p]
        gcol = VL[('nmix', i)]
        xx = SV(0, F32, [128, 8, 1039])
        xss = SV(33248, F32, [128, 8, 16, 23])
        tmp = [RV(4224 * a, F32, [128, 2, 527]) for a in range(2)]
        tmps = [RV(4224 * a, F32, [128, 2, 16, 23]) for a in range(2)]
        stg = RV(8448, F32, [128, 1024])
        ostg = RV(12544, F32, [128, 1024])
        if p == 0:
            memset('dve', xx[:, :, 0:15], 0.0, w=[('S', 'xxh')])
        else:
            cp('dve', xx[:, :, 0:15], xx[:, :, 1024:1039], r=[('S', 'xx', 1, k) for k in range(8)] + [('S', 'xxh')], w=[('S', 'xxh')])
        for t in tl:
            col0, n = TILES[t]
            uc = col0 - PASS_COL0[p]
            if t < 4:
                rms_tile(t, gcol, lambda k, uc=uc, n=n: xx[:, k, 15 + uc:15 + uc + n], lambda k, t=t: [('S', 'xx', t % 2, k)])
            else:
                for half in range(2):
                    dma('sp', stg[0:120, :], spl_d[j, half * 8:(half + 1) * 8].rearrange("b t f -> (b t) f"), r=[], w=[('R', 'stg')])
                    for h2 in range(2):
                        pi = psn()
                        for kk in range(4):
                            k = h2 * 4 + kk
                            trp(psum[pi][:, kk * 120:(kk + 1) * 120], stg[0:120, k * 128:(k + 1) * 128], ident32[0:120, 0:120],
                                r=[('R', 'stg'), C], w=[('ps', pi)])
                        for kk in range(4):
                            k = h2 * 4 + kk
                            cp(evq(), xss[:, k, half * 8:(half + 1) * 8, 0:15], psum[pi][:, kk * 120:(kk + 1) * 120].rearrange("p (b t) -> p b t", b=8),
                               r=[('ps', pi)], w=[('S', 'xsh', k)])
                c0, n = TILES[4]
                rms_stat([xT[:, k, c0:c0 + n] for k in range(8)], n, [('x', 4, k) for k in range(8)], 1024.0)
                for k in range(8):
                    stt(xss[:, k, :, 15:23], xT[:, k, c0:c0 + n].rearrange("p (b t) -> p b t", b=16), vecs[:, gcol + k:gcol + k + 1],
                        rsB[:, :n].rearrange("p (b t) -> p b t", b=16), ALU.mult, ALU.mult,
                        r=[('x', 4, k), 'rsB', C, ('S', 'xsh', k)], w=[('S', 'xs', k)])
        for t in tl:
            col0, n = TILES[t]
            uc = col0 - PASS_COL0[p]
            for g in range(4):
                w_ = 2 << g
                if t < 4:
                    src = xx[:, 2 * g:2 * g + 2, uc:uc + 15 + n]
                    rk = [('S', 'xx', t % 2, 2 * g), ('S', 'xx', t % 2, 2 * g + 1), ('S', 'xxh')]
                    if uc > 0:
                        rk += [('S', 'xx', (t + 1) % 2, 2 * g), ('S', 'xx', (t + 1) % 2, 2 * g + 1)]
                    cur = src
                    step = 1
                    a = 0
                    L = 15 + n
                    while step < w_:
                        dst = tmp[a][:, :, 0:L]
                        tt(dst[:, :, step:L], cur[:, :, step:L], cur[:, :, 0:L - step], ALU.add, r=rk + [('R', 'tmp', 1 - a)], w=[('R', 'tmp', a)])
                        cur = dst
                        a = 1 - a
                        step *= 2
                    la = 1 - a
                    stt(ub[:, 2 * g:2 * g + 2, uc:uc + n], cur[:, :, 15:15 + n], 1.0 / w_, src[:, :, 15:15 + n], ALU.mult, ALU.subtract,
                        r=rk + [('R', 'tmp', la)], w=[('u', t, 2 * g), ('u', t, 2 * g + 1)])
                    if t == 0:
                        for tc_ in range(w_ - 1):
                            stt(ub[:, 2 * g:2 * g + 2, tc_:tc_ + 1], cur[:, :, 15 + tc_:16 + tc_], 1.0 / (tc_ + 1), src[:, :, 15 + tc_:16 + tc_],
                                ALU.mult, ALU.subtract, r=rk + [('R', 'tmp', la)], w=[('u', t, 2 * g), ('u', t, 2 * g + 1)])
                else:
                    src = xss[:, 2 * g:2 * g + 2, :, :]
                    rk = [('S', 'xs', 2 * g), ('S', 'xs', 2 * g + 1)]
                    cur = src
                    step = 1
                    a = 0
                    while step < w_:
                        dst = tmps[a]
                        tt(dst[:, :, :, step:23], cur[:, :, :, step:23], cur[:, :, :, 0:23 - step], ALU.add, r=rk + [('R', 'tmp', 1 - a)], w=[('R', 'tmp', a)])
                        cur = dst
                        a = 1 - a
                        step *= 2
                    la = 1 - a
                    for kk in range(2):
                        stt(ub[:, 2 * g + kk, uc:uc + n].rearrange("p (b t) -> p b t", b=16), cur[:, kk, :, 15:23], 1.0 / w_, src[:, kk, :, 15:23],
                            ALU.mult, ALU.subtract, r=rk + [('R', 'tmp', la)], w=[('u', t, 2 * g + kk)])
        for g in range(4):
            pwb, pk = load_w(pw_d[j, g], 256, 0, 256)
            for oc in range(2):
                o = 2 * g + oc
                for t in tl:
                    col0, n = TILES[t]
                    uc = col0 - PASS_COL0[p]
                    pi = psn()
                    for k in range(2):
                        mm(psum[pi][:, :n], pwb[:, k, oc * 128:(oc + 1) * 128], ub[:, 2 * g + k, uc:uc + n], k == 0, k == 1,
                           r=[pk, ('u', t, 2 * g + k)], w=[('ps', pi)])
                    sc = VL[('psc', j)] + o
                    stt(xT[:, o, col0:col0 + n], psum[pi][:, :n], vecs[:, sc:sc + 1], xT[:, o, col0:col0 + n], ALU.mult, ALU.add,
                        r=[('ps', pi), ('x', t, o), C], w=[('x', t, o)])
        if p == 1:
            for half in range(2):
                pi = psn()
                for kk in range(4):
                    k = half * 4 + kk
                    trp(psum[pi][:, kk * 128:(kk + 1) * 128], xx[:, k, 15 + 896:15 + 1024], ident32[:], r=[('S', 'xx', 1, k), C], w=[('ps', pi)])
                cp(evq(), ostg[:, half * 512:(half + 1) * 512], psum[pi][:], r=[('ps', pi)], w=[('R', 'ostg')])
            dma('act', poolp_o[j], ostg[113:128, :], r=[('R', 'ostg')], w=[])
            ov = ostg.rearrange("p (k c) -> p k c", k=8)
            for k in range(8):
                cp(evq(), ov[:, k, :].rearrange("p (b t) -> p b t", b=16), xss[:, k, :, 15:23], r=[('S', 'xs', k), ('R', 'ostg')], w=[('R', 'ostg')])
            for half in range(2):
                pi = psn()
                for kk in range(4):
                    k = half * 4 + kk
                    trp(psum[pi][:, kk * 128:(kk + 1) * 128], ov[:, k, :], ident32[:], r=[('R', 'ostg'), C], w=[('ps', pi)])
                cp(evq(), stg[:, half * 512:(half + 1) * 512], psum[pi][:], r=[('ps', pi)], w=[('R', 'stg')])
            for b in range(16):
                dma('act', pools_o[j, b, 7:15, :], stg[8 * b:8 * b + 8, :], r=[('R', 'stg')], w=[])
            dma('act', pools_o[j, :, 0:7, :], spl_d[j, :, 8:15, :], r=[], w=[])

    def ssd(p, i):
        j = i // 2
        tl = PASSES[p]
        norm_u(p, VL[('nmix', i)])
        hT = RV(0, F32, [128, 2048])
        hTb = RV(8192, BF16, [128, 2048])
        cst = [RV(12288 + 2112 * a, F32, [128, 528]) for a in range(2)]
        convh = RV(16512, F32, [128, 24, 3])
        dec = [RV(16800 + 2048 * a, BF16, [128, 8, 128]) for a in range(2)]
        Eb = [RV(20896, BF16, [128, 8, 128]), MV(8192, BF16, [128, 8, 128])]
        cbm = [RV(22944 + 256 * a, BF16, [128, 128]) for a in range(2)]
        F1 = MV(0, F32, [128, 512])
        F2 = MV(2048, F32, [128, 512])
        F3 = MV(4096, F32, [128, 512])
        A3 = MV(6144, BF16, [128, 512])
        nA3 = MV(7168, BF16, [128, 512])
        cwc = VL[('cw', j)]
        cbc = VL[('cb', j)]
        dsk = VL[('dsk', j)]
        snw = VL[('snw', j)]
        if p == 0:
            memset('dve', convh, 0.0, w=[('R', 'convh', ci) for ci in range(24)])
            memset('dve', hT, 0.0, w=[('R', 'hT', g) for g in range(4)])
            memset('dve', hTb, 0.0, w=[('R', 'hTb', g) for g in range(4)])
            for fc in range(16):
                ts(Dd[:, fc, :], ident32[:], vecs[:, dsk + fc:dsk + fc + 1], None, ALU.mult, None, r=[C], w=[('Dd', fc)])

        def ssd_tile(t):
            col0, n = TILES[t]
            uc = col0 - PASS_COL0[p]
            samp = (t == 4)
            nb, bs = (16, 8) if samp else (1, n)
            nch = n // 128
            ukeys = [('u', t, k) for k in range(8)]
            if samp:
                zs = SV(0, BF16, [128, 16, 128])
                xbcT = SV(4096, BF16, [128, 24, 128])
                xdt = SV(10240, BF16, [128, 2048])
                xdte = SV(14336, BF16, [128, 2048])
                Btm = SV(18432, BF16, [128, 512])
            else:
                zs = SV(0, BF16, [128, 16, 512])
                xbcT = SV(16384, BF16, [128, 24, 512])
                xdt = SV(40960, BF16, [128, 2048])
                xdte = SV(45056, BF16, [128, 2048])
                Btm = SV(49152, BF16, [128, 512])
            if samp:
                hS = RV(4096, F32, [128, 24, 48])
                sc48 = RV(12288, F32, [128, 1056])
                for q4 in range(3):
                    dma('sp', sc48[0:48, 0:1024], scv_d[j, :, q4 * 1024:(q4 + 1) * 1024], r=[], w=[('R', 'cst', 0), ('R', 'cst', 1)])
                    for h2 in range(2):
                        pi = psn()
                        for kk in range(4):
                            trp(psum[pi][:, kk * 48:(kk + 1) * 48], sc48[0:48, (h2 * 4 + kk) * 128:(h2 * 4 + kk + 1) * 128], ident32[0:48, 0:48],
                                r=[('R', 'cst', 0), C], w=[('ps', pi)])
                        cp(evq(), hS[:, q4 * 8 + h2 * 4:q4 * 8 + h2 * 4 + 4, :], psum[pi][:, 0:192].rearrange("p (k c) -> p k c", k=4),
                           r=[('ps', pi)], w=[('R', 'hS', q4 * 2 + h2)])
            wdt, dk_ = load_w(inw_d[j], 1024, 5120, 32)
            pdt = psn()
            for r3 in range(3):
                for k in range(8):
                    mm(psum[pdt][32 * r3:32 * r3 + 32, :n], wdt[:, k, :], ub[:, k, uc:uc + n], k == 0, k == 7, r=[dk_, ukeys[k]], w=[('ps', pdt)])
            act(F2[0:96, :n], psum[pdt][0:96, :n], AF.Softplus, r=[('ps', pdt), C], w=[('M', 'F2')], bias=hv[:, 2 * j:2 * j + 1])
            ts(F1[0:96, :n], F2[0:96, :n], avec[:, j:j + 1], None, ALU.mult, None, r=[('M', 'F2'), ('avec', j)], w=[('M', 'F1')])
            for c in range(nch):
                cs = slice(c * 128, (c + 1) * 128)
                d0 = resetm[:, :] if samp else ones1[0:96, 0:1].to_broadcast([96, 128])
                S.op('dve', lambda e, cs=cs, d0=d0: e.tensor_tensor_scan(out=F3[0:96, cs], data0=d0, data1=F1[0:96, cs], initial=0.0,
                                                                       op0=ALU.mult, op1=ALU.add), r=[('M', 'F1'), C], w=[('M', 'F3')])
            cp('dve', A3[0:96, :n], F3[0:96, :n], r=[('M', 'F3')], w=[('M', 'A3')])
            tt(F1[0:96, :n], F3[0:96, :n], A3[0:96, :n], ALU.subtract, r=[('M', 'F3'), ('M', 'A3')], w=[('M', 'F1')])
            cp('dve', A3[32:64, :n], F1[32:64, :n], r=[('M', 'F1')], w=[('M', 'A3')])
            cp('dve', A3[64:96, :n], F1[64:96, :n], r=[('M', 'F1')], w=[('M', 'A3')])
            tt(F1[64:96, :n], F1[64:96, :n], A3[64:96, :n], ALU.subtract, r=[('M', 'F1'), ('M', 'A3')], w=[('M', 'F1')])
            cp('dve', A3[64:96, :n], F1[64:96, :n], r=[('M', 'F1')], w=[('M', 'A3')])
            ts(nA3[0:96, :n], A3[0:96, :n], -1.0, None, ALU.mult, None, r=[('M', 'A3')], w=[('M', 'nA3')])
            def xbc1(ci, pi):
                a = ci % 2
                cv = cst[a][:, 0:nb * (3 + bs)].rearrange("p (b s) -> p b s", b=nb)
                if samp:
                    cp('dve', cv[:, :, 0:3], hS[:, ci, :].rearrange("p (b s) -> p b s", b=16), r=[('R', 'hS', ci // 4)], w=[('R', 'cst', a, 'h')])
                else:
                    cp('dve', cv[:, :, 0:3], convh[:, ci, :].unsqueeze(1), r=[('R', 'convh', ci)], w=[('R', 'cst', a, 'h')])
                psv = psum[pi][:, :n].rearrange("p (b s) -> p b s", b=nb)
                cp('act', cv[:, :, 3:3 + bs], psv, r=[('ps', pi)], w=[('R', 'cst', a)])
                accb, acck = (rsA, 'rsA') if a == 0 else (rsB, 'rsB')
                accv = accb[:, :n].rearrange("p (b s) -> p b s", b=nb)
                act(accv, psv, AF.Identity, r=[('ps', pi), C], w=[acck], bias=vecs[:, cbc + ci:cbc + ci + 1],
                    scale=vecs[:, cwc + ci * 4 + 3:cwc + ci * 4 + 4])

            def xbc2(ci):
                a = ci % 2
                cv = cst[a][:, 0:nb * (3 + bs)].rearrange("p (b s) -> p b s", b=nb)
                accb, acck = (rsA, 'rsA') if a == 0 else (rsB, 'rsB')
                accv = accb[:, :n].rearrange("p (b s) -> p b s", b=nb)
                for tap in range(3):
                    stt(accv, cv[:, :, tap:tap + bs], vecs[:, cwc + ci * 4 + tap:cwc + ci * 4 + tap + 1], accv, ALU.mult, ALU.add,
                        r=[('R', 'cst', a), ('R', 'cst', a, 'h'), acck, C], w=[acck])
                act(xbcT[:, ci, :n], accb[:, :n], AF.Silu, r=[acck], w=[('S', 'xbc', ci)])
                if samp:
                    cp('dve', hS[:, ci, :].rearrange("p (b s) -> p b s", b=16), cv[:, :, bs:bs + 3], r=[('R', 'cst', a)],
                       w=[('R', 'hS2', ci), ('R', 'hS', ci // 4)])
                else:
                    cp('dve', convh[:, ci, :].unsqueeze(1), cv[:, :, bs:bs + 3], r=[('R', 'cst', a)], w=[('R', 'convh', ci)])

            pend = None
            for un in range(20):
                wib, ik = load_w(inw_d[j], 1024, un * 256, 256)
                for cc in range(2):
                    fc = un * 2 + cc
                    pi = psn()
                    for k in range(8):
                        mm(psum[pi][:, :n], wib[:, k, cc * 128:(cc + 1) * 128], ub[:, k, uc:uc + n], k == 0, k == 7, r=[ik, ukeys[k]], w=[('ps', pi)])
                    if fc < 16:
                        act(zs[:, fc, :n], psum[pi][:, :n], AF.Silu, r=[('ps', pi)], w=[('S', 'zs', fc)])
                    else:
                        xbc1(fc - 16, pi)
                        if pend is not None:
                            xbc2(pend)
                        pend = fc - 16
            xbc2(pend)
            m01 = m01bd if samp else m01c
            mng = mngbd if samp else mngc
            for c in range(nch):
                cs = slice(c * 128, (c + 1) * 128)
                first = (t == 0 and c == 0)
                cbs = 8 if samp else 128
                cp('act', STt[0:32, :], F2[0:32, cs], r=[('M', 'F2')], w=['STt0'])
                a3v = F3[32:64, cs].rearrange("p (b s) -> p b s", b=nb)
                tt(STt[32:64, :].rearrange("p (b s) -> p b s", b=nb), a3v[:, :, cbs - 1:cbs].to_broadcast([32, nb, cbs]), a3v,
                   ALU.subtract, r=[('M', 'F3'), 'STt1'], w=['STt1'])
                act(STt[32:64, :], STt[32:64, :], AF.Exp, r=['STt1'], w=['STt1'])
                tt(STt[32:64, :], STt[32:64, :], F2[32:64, cs], ALU.mult, r=['STt1', ('M', 'F2')], w=['STt1'])
                pi = psn()
                trp(psum[pi][:, 0:64], STt[:, :], ident32[0:64, 0:64], r=['STt0', 'STt1', C], w=[('ps', pi)])
                cp('act', sctm[:, :], psum[pi][:, 0:64], r=[('ps', pi)], w=['sctm'])
                px = [psn(), psn(), psn()]
                for ci in range(20):
                    pb = psum[px[ci // 8]][:].bitcast(BF16)
                    trp(pb[:, (ci % 8) * 128:(ci % 8 + 1) * 128], xbcT[:, ci, cs], identb[:], r=[('S', 'xbc', ci), C], w=[('ps', px[ci // 8])])
                for hf in range(2):
                    pb = psum[px[hf]][:].bitcast(BF16).rearrange("p (h d) -> p h d", h=16)
                    tt(xdt[:, hf * 1024:(hf + 1) * 1024].rearrange("p (h d) -> p h d", h=16), pb,
                       sctm[:, hf * 16:(hf + 1) * 16].unsqueeze(2).to_broadcast([128, 16, 64]), ALU.mult,
                       r=[('ps', px[hf]), 'sctm'], w=[('S', 'xdt', hf)])
                    tt(xdte[:, hf * 1024:(hf + 1) * 1024].rearrange("p (h d) -> p h d", h=16), pb,
                       sctm[:, 32 + hf * 16:32 + (hf + 1) * 16].unsqueeze(2).to_broadcast([128, 16, 64]), ALU.mult,
                       r=[('ps', px[hf]), 'sctm'], w=[('S', 'xdte', hf)])
                cp('act', Btm[:, :], psum[px[2]][:].bitcast(BF16)[:, 0:512], r=[('ps', px[2])], w=[('S', 'Btm')])

                def stageA(g):
                    a = g % 2
                    pc = psn()
                    mm(psum[pc][:, 0:128], xbcT[:, 16 + g, cs], xbcT[:, 20 + g, cs], True, True, r=[('S', 'xbc', 16 + g), ('S', 'xbc', 20 + g)], w=[('ps', pc)])
                    tt(cbm[a][:, :], psum[pc][:, 0:128], m01[:], ALU.mult, r=[('ps', pc), C], w=[('R', 'cbm', a)])
                    pB = [psn(), psn()]
                    pE = [psn(), psn()]
                    for ih in range(8):
                        hh = 8 * g + ih
                        sel = i3b[:, hh:hh + 1].to_broadcast([96, 128])
                        ob = psum[pB[ih // 4]][:, (ih % 4) * 128:(ih % 4 + 1) * 128]
                        mm(ob, sel, A3[0:96, cs], True, False, r=[C, ('M', 'A3')], w=[('ps', pB[ih // 4])])
                        mm(ob, nA3[0:96, cs], sel, False, False, r=[C, ('M', 'nA3')], w=[('ps', pB[ih // 4])])
                        mm(ob, identb[:], mng[:], False, True, r=[C], w=[('ps', pB[ih // 4])])
                        oe = psum[pE[ih // 4]][:, (ih % 4) * 128:(ih % 4 + 1) * 128]
                        mm(oe, sel, A3[0:96, cs], True, True, r=[C, ('M', 'A3')], w=[('ps', pE[ih // 4])])
                    for q in range(2):
                        act(dec[a][:, q * 4:(q + 1) * 4, :], psum[pB[q]][:].rearrange("p (h c) -> p h c", h=4), AF.Exp, r=[('ps', pB[q])], w=[('R', 'dec', a, q)])
                        act(Eb[a][:, q * 4:(q + 1) * 4, :], psum[pE[q]][:].rearrange("p (h c) -> p h c", h=4), AF.Exp, r=[('ps', pE[q])], w=[('R', 'Eb', a, q)])
                        if samp:
                            act(cdE[:, 8 * g + 4 * q:8 * g + 4 * q + 4, :], psum[pE[q]][:].rearrange("p (h b s) -> p h b s", h=4, b=16)[:, :, :, 7], AF.Exp,
                                r=[('ps', pE[q])], w=[('cdE', g, q)])
                        else:
                            act(cdv[a][:, q * 4:(q + 1) * 4], psum[pE[q]][:].rearrange("p (h c) -> p h c", h=4)[:, :, 127], AF.Exp,
                                r=[('ps', pE[q])], w=[('cdv', a, q)])
                    tt(dec[a][:, :, :], dec[a][:, :, :], cbm[a][:, :].unsqueeze(1).to_broadcast([128, 8, 128]), ALU.mult,
                       r=[('R', 'dec', a, 0), ('R', 'dec', a, 1), ('R', 'cbm', a)], w=[('R', 'dec', a, 0), ('R', 'dec', a, 1)])
                    if not first:
                        tt(Eb[a][:, :, :], Eb[a][:, :, :], xbcT[:, 20 + g, cs].unsqueeze(1).to_broadcast([128, 8, 128]), ALU.mult,
                           r=[('R', 'Eb', a, 0), ('R', 'Eb', a, 1), ('S', 'xbc', 20 + g)], w=[('R', 'Eb', a, 0), ('R', 'Eb', a, 1)])

                def stageB(g):
                    a = g % 2
                    dk2 = [('R', 'dec', a, 0), ('R', 'dec', a, 1)]
                    ek2 = [('R', 'Eb', a, 0), ('R', 'Eb', a, 1)]
                    py = psn()
                    for hp in range(4):
                        fc = 4 * g + hp
                        mm(psum[py][:, hp * 128:(hp + 1) * 128], Dd[:, fc, :], xbcT[:, fc, cs], hp == 0, False, r=[('Dd', fc), ('S', 'xbc', fc)], w=[('ps', py)])
                        for sd in range(2):
                            hh = 8 * g + 2 * hp + sd
                            oy = psum[py][64 * sd:64 * sd + 64, hp * 128:(hp + 1) * 128]
                            lastd = (first or samp)
                            mm(oy, xdt[:, hh * 64:(hh + 1) * 64], dec[a][:, 2 * hp + sd, :], False, lastd and not samp and sd == 1,
                               r=[('S', 'xdt', hh // 16)] + dk2, w=[('ps', py)])
                            if not lastd:
                                mm(oy, hTb[:, hh * 64:(hh + 1) * 64], Eb[a][:, 2 * hp + sd, :], False, sd == 1, r=[('R', 'hTb', g)] + ek2, w=[('ps', py)])
                    if samp:
                        st['resv'].add(py)
                        ssd_sample_group(j, g, py, Eb[a], ek2, xdte, Btm)
                        st['resv'].discard(py)
                    tt(zs[:, 4 * g:4 * g + 4, cs], psum[py][:, :].rearrange("p (f c) -> p f c", f=4), zs[:, 4 * g:4 * g + 4, cs], ALU.mult,
                       r=[('ps', py)] + [('S', 'zs', 4 * g + q) for q in range(4)], w=[('S', 'zs', 4 * g + q) for q in range(4)])
                    if not samp:
                        pst = psn()
                        mm(psum[pst][:, :], Btm[:, g * 128:(g + 1) * 128], xdte[:, g * 512:(g + 1) * 512], True, True,
                           r=[('S', 'Btm'), ('S', 'xdte', g // 2)], w=[('ps', pst)])
                        hv_ = hT[:, g * 512:(g + 1) * 512].rearrange("p (h d) -> p h d", h=8)
                        if not first:
                            tt(hv_, hv_, cdv[a][:, :].unsqueeze(2).to_broadcast([128, 8, 64]), ALU.mult,
                               r=[('R', 'hT', g), ('cdv', a, 0), ('cdv', a, 1)], w=[('R', 'hT', g)])
                        tt(hT[:, g * 512:(g + 1) * 512], hT[:, g * 512:(g + 1) * 512], psum[pst][:, :], ALU.add, r=[('R', 'hT', g), ('ps', pst)], w=[('R', 'hT', g)])
                        cp('act', hTb[:, g * 512:(g + 1) * 512], hT[:, g * 512:(g + 1) * 512], r=[('R', 'hT', g)], w=[('R', 'hTb', g)])

                stageA(0)
                for g in range(4):
                    if g + 1 < 4:
                        stageA(g + 1)
                    stageB(g)
            for g in range(4):
                rms_stat([zs[:, 4 * g + q, :n] for q in range(4)], n, [('S', 'zs', 4 * g + q) for q in range(4)], 512.0)
                for q in range(4):
                    fc = 4 * g + q
                    stt(zs[:, fc, :n], zs[:, fc, :n], vecs[:, snw + fc:snw + fc + 1], rsB[:, :n], ALU.mult, ALU.mult,
                        r=[('S', 'zs', fc), 'rsB', C], w=[('S', 'zs', fc)])
            for un in range(8):
                wob, ok_ = load_w(outw_d[j], 2048, un * 128, 128)
                pi = psn()
                for kc in range(16):
                    mm(psum[pi][:, :n], wob[:, kc, :], zs[:, kc, :n], kc == 0, kc == 15, r=[ok_, ('S', 'zs', kc)], w=[('ps', pi)])
                tt(xT[:, un, col0:col0 + n], psum[pi][:, :n], xT[:, un, col0:col0 + n], ALU.add, r=[('ps', pi), ('x', t, un)], w=[('x', t, un)])
            if samp:
                for q4 in range(4):
                    osg = RV(12288, F32, [128, 768])
                    for half in range(2):
                        pi = psn()
                        for kk in range(3):
                            ci = q4 * 6 + half * 3 + kk
                            trp(psum[pi][0:48, kk * 128:(kk + 1) * 128], hS[:, ci, :], ident32[:], r=[('R', 'hS2', ci), C], w=[('ps', pi)])
                        cp(evq(), osg[0:48, half * 384:(half + 1) * 384], psum[pi][0:48, 0:384], r=[('ps', pi)], w=[('R', 'cst', 0), ('R', 'cst', 1)])
                    dma('act', convs_o[j, :, q4 * 768:(q4 + 1) * 768], osg[0:48, :], r=[('R', 'cst', 0)], w=[])

        def prompt_state_out():
            ost = RV(12288, F32, [128, 1056])
            for q4 in range(4):
                pi = psn()
                for kk in range(4):
                    f = q4 * 4 + kk
                    trp(psum[pi][:, kk * 128:(kk + 1) * 128], hT[:, f * 128:(f + 1) * 128], ident32[:], r=[('R', 'hT', f // 4), C], w=[('ps', pi)])
                cp(evq(), ost[:, 0:512], psum[pi][:, :], r=[('ps', pi)], w=[('R', 'cst', 0), ('R', 'cst', 1)])
                for kk in range(4):
                    f = q4 * 4 + kk
                    dma('act', ssmp_o[j, f * 128:(f + 1) * 128, :], ost[:, kk * 128:(kk + 1) * 128], r=[('R', 'cst', 0)], w=[])
            for q4 in range(4):
                for half in range(2):
                    pi = psn()
                    for kk in range(3):
                        ci = q4 * 6 + half * 3 + kk
                        trp(psum[pi][0:3, kk * 128:(kk + 1) * 128], convh[:, ci, :], ident32[:], r=[('R', 'convh', ci), C], w=[('ps', pi)])
                    cp(evq(), ost[0:3, half * 384:(half + 1) * 384], psum[pi][0:3, 0:384], r=[('ps', pi)], w=[('R', 'cst', 0), ('R', 'cst', 1)])
                dma('act', convp_o[j, :, q4 * 768:(q4 + 1) * 768], ost[0:3, 0:768], r=[('R', 'cst', 0)], w=[])

        for t in tl:
            if t == 4:
                S.retire('S')
            ssd_tile(t)
            if t == 3:
                prompt_state_out()
                S.retire('R')

    def ssd_sample_group(j, g, py, Ceb, ek2, xdte, Btm):
        h0 = [SV(19456 + 2048 * a, F32, [128, 4, 128]) for a in range(4)]
        h0T = [SV(27648 + 1024 * a, BF16, [128, 512]) for a in range(2)]
        Bm = [SV(29696 + 256 * a, BF16, [128, 128]) for a in range(2)]
        nst = [SV(30208 + 2048 * a, F32, [128, 4, 128]) for a in range(3)]
        cdP = SV(36352, F32, [128, 4, 16])
        for sd in range(2):
            sl = slice(64 * sd, 64 * sd + 64)
            cp('dve', cdP[sl, :, :], cdE[sl, 8 * g:8 * g + 8, :].rearrange("p (f s) b -> p f s b", s=2)[:, :, sd, :],
               r=[('cdE', g, 0), ('cdE', g, 1)], w=[('S', 'cdP')])

        def load(b):
            dma('sp', h0[b % 4], sst_d[j, b, g * 512:(g + 1) * 512, :].rearrange("(f p) n -> p f n", p=128), r=[], w=[('S', 'h0', b % 4)])

        def stT(b):
            a = b % 2
            pi = psn()
            for f in range(4):
                trp(psum[pi][:, f * 128:(f + 1) * 128], h0[b % 4][:, f, :], ident32[:], r=[('S', 'h0', b % 4), C], w=[('ps', pi)])
            cp('act', h0T[a][:, :], psum[pi][:, :], r=[('ps', pi)], w=[('S', 'h0T', a)])
            ts(Bm[a][:, :], Btm[:, g * 128:(g + 1) * 128], bmask[:, b:b + 1], None, ALU.mult, None, r=[('S', 'Btm'), C], w=[('S', 'Bm', a)])

        def stC(b):
            a = b % 2
            for hp in range(4):
                for sd in range(2):
                    oy = psum[py][64 * sd:64 * sd + 64, hp * 128 + 8 * b:hp * 128 + 8 * b + 8]
                    mm(oy, h0T[a][:, hp * 128 + 64 * sd:hp * 128 + 64 * sd + 64], Ceb[:, 2 * hp + sd, 8 * b:8 * b + 8], False, (b == 15),
                       r=[('S', 'h0T', a)] + ek2, w=[('ps', py)])
            pst = psn()
            for f in range(4):
                mm(psum[pst][:, f * 128:(f + 1) * 128], xdte[:, (4 * g + f) * 128:(4 * g + f + 1) * 128], Bm[a][:, :], f == 0, f == 3,
                   r=[('S', 'xdte', g // 2), ('S', 'Bm', a)], w=[('ps', pst)])
            for f in range(4):
                stt(nst[b % 3][:, f, :], h0[b % 4][:, f, :], cdP[:, f, b:b + 1], psum[pst][:, f * 128:(f + 1) * 128], ALU.mult, ALU.add,
                    r=[('S', 'h0', b % 4), ('S', 'cdP'), ('ps', pst)], w=[('S', 'nst', b % 3, f)])
            dma('act', ssms_o[j, b, g * 512:(g + 1) * 512, :].rearrange("(f p) n -> p f n", p=128), nst[b % 3],
                r=[('S', 'nst', b % 3, f) for f in range(4)], w=[])

        load(0)
        load(1)
        load(2)
        for b in range(16):
            stT(b)
            if b >= 1:
                stC(b - 1)
            if b + 3 < 16:
                load(b + 3)
        stC(15)

    for i in range(NL):
        for sub in SUBS:
            if sub == 'ffn1':
                for p in range(2):
                    ffn(p, i, wg1_d, wu1_d, wd1_d, 'nf1')
            elif sub == 'ffn2':
                for p in range(2):
                    ffn(p, i, wg2_d, wu2_d, wd2_d, 'nf2')
            elif sub == 'mix':
                for p in range(2):
                    if i % 2 == 0:
                        ssd(p, i)
                    else:
                        poolmix(p, i)
            elif sub == 'attn':
                memkv(i)
                for p in range(2):
                    attn(p, i)
            S.retire('S')
            S.retire('R')
            S.retire('M')

    gcol = VL['fin']
    for t in range(5):
        c0, n = TILES[t]
        yn = SV(0, F32, [128, 8, 512])
        rms_tile(t, gcol, lambda k, n=n: yn[:, k, :n], lambda k: [('S', 'yn', k)])
        for blk in range(n // 128):
            ostg = SV(16384 + ((st['ev'] // 2) % 2) * 4096, F32, [128, 1024])
            a = (st['ev'] // 2) % 2
            for half in range(2):
                pi = psn()
                for kk in range(4):
                    k = half * 4 + kk
                    trp(psum[pi][:, kk * 128:(kk + 1) * 128], yn[:, k, blk * 128:(blk + 1) * 128], ident32[:], r=[('S', 'yn', k), C], w=[('ps', pi)])
                cp('act' if half else 'dve', ostg[:, half * 512:(half + 1) * 512], psum[pi][:], r=[('ps', pi)], w=[('S', 'ostg', a, half)])
            st['ev'] += 2
            dst = yp_o[c0 + blk * 128:c0 + (blk + 1) * 128, :] if t < 4 else ys_o
            dma('act', dst, ostg, r=[('S', 'ostg', a, 0), ('S', 'ostg', a, 1)], w=[])

    S.finish()
    S.emit(nc)
    es.close()
    return nc


_NC_CACHE = {}


def _fm(v):
    v = np.asarray(v, dtype=np.float32)
    return np.ascontiguousarray(v.reshape(-1, 128).T)


def pack_vecs(inp):
    vecs = np.zeros((128, NV), np.float32)
    for i in range(4):
        for nm, key in (('nf1', 'norm_ffn1'), ('nmix', 'norm_mix'), ('ncr', 'norm_cross'), ('nmem', 'norm_mem'), ('nf2', 'norm_ffn2')):
            c = VL[(nm, i)]
            vecs[:, c:c + 8] = _fm(inp[key][i])
    c = VL['fin']
    vecs[:, c:c + 8] = _fm(inp['final_norm'])
    for j in range(2):
        c = VL[('psc', j)]
        vecs[:, c:c + 8] = _fm(inp['pool_scale'][j])
        c = VL[('snw', j)]
        vecs[:, c:c + 16] = _fm(inp['ssd_norm_w'][j])
        c = VL[('cw', j)]
        cw = np.asarray(inp['ssd_conv_w'][j], np.float32).reshape(4, 24, 128)
        vecs[:, c:c + 96] = np.ascontiguousarray(cw.transpose(2, 1, 0)).reshape(128, 96)
        c = VL[('cb', j)]
        vecs[:, c:c + 24] = _fm(inp['ssd_conv_b'][j])
        c = VL[('dsk', j)]
        vecs[:, c:c + 16] = _fm(np.repeat(np.asarray(inp['ssd_d'][j], np.float32), 64))
    hv = np.zeros((96, 4), np.float32)
    for j in range(2):
        hv[:, 2 * j] = np.tile(np.asarray(inp['ssd_dt_bias'][j], np.float32), 3)
        hv[:, 2 * j + 1] = np.tile(np.asarray(inp['ssd_a_log'][j], np.float32), 3)
    return vecs, hv


def make_in_maps(inp):
    f = lambda a: np.ascontiguousarray(np.asarray(a, dtype=np.float32))
    vecs, hv = pack_vecs(inp)
    shared = {
        "vecs": vecs, "hv": hv,
        "wg1": f(inp['ffn1_w_gate']), "wu1": f(inp['ffn1_w_up']), "wd1": f(inp['ffn1_w_down']),
        "wg2": f(inp['ffn2_w_gate']), "wu2": f(inp['ffn2_w_up']), "wd2": f(inp['ffn2_w_down']),
        "inw": f(inp['ssd_in_w']), "outw": f(inp['ssd_out_w']), "pw": f(inp['pool_w']),
        "wq": f(inp['xa_wq']), "wk": f(inp['xa_wk']), "wv": f(inp['xa_wv']), "wo": f(inp['xa_wo']),
    }
    maps = []
    for c in range(8):
        sl = slice(16 * c, 16 * c + 16)
        m = dict(shared)
        m["xp"] = f(inp['x_prompt'][c])
        m["xs"] = f(np.asarray(inp['x_sample'])[sl].reshape(128, 1024))
        m["mem"] = f(inp['mem_prompt'][c])
        m["ck"] = f(np.asarray(inp['cache_mem_k'])[:, sl].reshape(4, 16, 256, 1024))
        m["cv"] = f(np.asarray(inp['cache_mem_v'])[:, sl].reshape(4, 16, 256, 1024))
        m["sst"] = f(np.asarray(inp['state_ssm'])[:, sl].reshape(2, 16, 2048, 128))
        m["scv"] = f(np.asarray(inp['state_conv'])[:, sl].reshape(2, 48, 3072))
        m["spl"] = f(np.asarray(inp['state_pool'])[:, sl])
        maps.append(m)
    return maps


def assemble(results):
    R = results
    y_p = np.stack([R[c]["y_p"] for c in range(8)], 0)
    y_s = np.concatenate([R[c]["y_s"].reshape(16, 8, 1024) for c in range(8)], 0)
    ssm_p = np.stack([R[c]["ssm_p"].reshape(2, 32, 64, 128) for c in range(8)], 1)
    conv_p = np.stack([R[c]["conv_p"] for c in range(8)], 1)
    pool_p = np.stack([R[c]["pool_p"] for c in range(8)], 1)
    mk_p = np.stack([R[c]["mk_p"].reshape(4, 256, 4, 256) for c in range(8)], 1)
    mv_p = np.stack([R[c]["mv_p"].reshape(4, 256, 4, 256) for c in range(8)], 1)
    ssm_s = np.concatenate([R[c]["ssm_s"].reshape(2, 16, 32, 64, 128) for c in range(8)], 1)
    conv_s = np.concatenate([R[c]["conv_s"].reshape(2, 16, 3, 3072) for c in range(8)], 1)
    pool_s = np.concatenate([R[c]["pool_s"] for c in range(8)], 1)
    outs = (y_p, y_s, ssm_p, conv_p, pool_p, mk_p, mv_p, ssm_s, conv_s, pool_s)
    return tuple(np.ascontiguousarray(o.astype(np.float32, copy=False)) for o in outs)


def kernel(**inputs):
    if 'nc' not in _NC_CACHE:
        _NC_CACHE['nc'] = build()
    nc = _NC_CACHE['nc']
    in_maps = make_in_maps(inputs)
    res = run_bass_kernel_spmd(nc, in_maps, core_ids=list(range(8)))
    return assemble(res.results)
```
